# Optimizing a Trainium2 kernel written in Bass

```python
import math
import jax, jax.numpy as jnp
from jax import lax
import numpy as np

D_MODEL = 1024
BATCH = 8
SEQ = 4096
DEPTH = 4

CTX_LEN = 256
GRID_W = 64
N_MIXERS = 4
Q_BLOCK = 128
ROPE_THETA = 10000.0
NORM_EPS = 1e-6
N_MOD = 9
D_FF = 2816
NEG_INF = -1e30

DA_QK_DIM = 64
DA_HEADS = D_MODEL // (2 * DA_QK_DIM)
DA_V_DIM = 2 * DA_QK_DIM
DA_QKV = 2 * DA_HEADS * 2 * DA_QK_DIM + DA_HEADS * DA_V_DIM
GA_HEAD_DIM = 128
GA_HEADS = D_MODEL // GA_HEAD_DIM
GA_KV_HEADS = 2
MLA_HEADS = 16
MLA_Q_RANK = 256
MLA_KV_RANK = 128
MLA_NOPE_DIM = 64
MLA_ROPE_DIM = 32
MLA_V_DIM = 64
SWA_HEAD_DIM = 64
SWA_HEADS = D_MODEL // SWA_HEAD_DIM
SWA_KV_HEADS = 2
WINDOW = 128
BAND = Q_BLOCK + 2 * WINDOW

kernel_name = "hybrid_interleaved_diffusion_trunk"


def rmsnorm(x, g):
    xf = x.astype(jnp.float32)
    y = xf * lax.rsqrt(jnp.mean(xf * xf, axis=-1, keepdims=True) + NORM_EPS)
    return (y * g.astype(jnp.float32)).astype(x.dtype)


def axial_rope_tables(seq_len, rot_dim):
    rows = seq_len // GRID_W
    row = jnp.repeat(jnp.arange(rows, dtype=jnp.int32), GRID_W).astype(jnp.float32)
    col = jnp.tile(jnp.arange(GRID_W, dtype=jnp.int32), rows).astype(jnp.float32)
    n_axis = rot_dim // 4
    freqs = ROPE_THETA ** (-jnp.arange(n_axis, dtype=jnp.float32) / n_axis)
    ang = jnp.concatenate([row[:, None] * freqs, col[:, None] * freqs], axis=-1)
    return jnp.cos(ang), jnp.sin(ang)


def apply_rope(x, cos, sin):
    seq, half = cos.shape
    bshape = (seq,) + (1,) * (x.ndim - 3) + (half,)
    cs = cos.reshape(bshape).astype(x.dtype)
    sn = sin.reshape(bshape).astype(x.dtype)
    xp = x.reshape(x.shape[:-1] + (half, 2))
    x0, x1 = xp[..., 0], xp[..., 1]
    return jnp.stack([x0 * cs - x1 * sn, x0 * sn + x1 * cs], axis=-1).reshape(x.shape)


def sweep_query_blocks(fn, *qs):
    b, s = qs[0].shape[:2]
    nblk = s // Q_BLOCK
    blocks = tuple(jnp.moveaxis(q.reshape((b, nblk, Q_BLOCK) + q.shape[2:]), 1, 0) for q in qs)
    out = lax.map(lambda args: fn(*args), (jnp.arange(nblk),) + blocks)
    return jnp.moveaxis(out, 0, 1).reshape((b, s) + out.shape[3:])


def grouped_attend(q, k, v, scale):
    s = jnp.einsum('blhgd,bkhd->bhglk', q, k, preferred_element_type=jnp.float32) * scale
    p = jax.nn.softmax(s, axis=-1)
    o = jnp.einsum('bhglk,bkhd->blhgd', p.astype(v.dtype), v)
    return o.reshape(o.shape[:2] + (-1,))


def swiglu(h, w_in, w_out):
    gate, up = jnp.split(h @ w_in, 2, axis=-1)
    return (jax.nn.silu(gate) * up) @ w_out


def adaln(cond, w, b):
    m = jax.nn.silu(cond) @ w + b
    return m.reshape(m.shape[:-1] + (N_MOD, D_MODEL))


def modulated_norm(h, g, m, k):
    return rmsnorm(h, g) * (1.0 + m[..., k + 1, :]) + m[..., k, :]


def gated_residual(h, y, g, m, k, weight):
    return h + weight * m[..., k + 2, :] * rmsnorm(y, g)


def half_ffn(h, m, k, g_pre, g_post, w_in, w_out):
    y = swiglu(modulated_norm(h, g_pre, m, k), w_in, w_out)
    return gated_residual(h, y, g_post, m, k, 0.5)


def diff_attention_mixer(h, hc, w_in, lam, subln, w_out, lambda_init, ctx_out):
    b, s, _ = h.shape
    cos, sin = axial_rope_tables(s, DA_QK_DIM)

    def project(t):
        n = t.shape[1]
        q, k, v = jnp.split(t @ w_in, [DA_HEADS * 2 * DA_QK_DIM, 2 * DA_HEADS * 2 * DA_QK_DIM], axis=-1)
        return (q.reshape(b, n, DA_HEADS, 2, DA_QK_DIM), k.reshape(b, n, DA_HEADS, 2, DA_QK_DIM),
                v.reshape(b, n, DA_HEADS, DA_V_DIM))

    q, k, v = project(h)
    qc, kc, vc = project(hc)
    q, k = apply_rope(q, cos, sin), apply_rope(k, cos, sin)
    lf = lam.astype(jnp.float32)
    lam_full = jnp.exp(jnp.sum(lf[0] * lf[1])) - jnp.exp(jnp.sum(lf[2] * lf[3])) + lambda_init
    scale = DA_QK_DIM ** -0.5

    def attend(qb, kb, vb):
        sc = jnp.einsum('blhid,bkhid->bihlk', qb, kb, preferred_element_type=jnp.float32) * scale
        p = jax.nn.softmax(sc, axis=-1)
        p = p[:, 0] - lam_full * p[:, 1]
        o = jnp.einsum('bhlk,bkhd->blhd', p.astype(vb.dtype), vb)
        o = rmsnorm(o, subln) * (1.0 - lambda_init)
        return o.reshape(o.shape[0], o.shape[1], DA_HEADS * DA_V_DIM)

    k_all = jnp.concatenate([kc, k], axis=1)
    v_all = jnp.concatenate([vc, v], axis=1)
    y = sweep_query_blocks(lambda i, qb: attend(qb, k_all, v_all), q) @ w_out
    yc = attend(qc, kc, vc) @ w_out if ctx_out else None
    return y, yc


def gqa_axial_mixer(h, hc, w_in, q_norm, k_norm, w_out, ctx_out):
    b, s, _ = h.shape
    cos, sin = axial_rope_tables(s, GA_HEAD_DIM)
    grp = GA_HEADS // GA_KV_HEADS

    def project(t):
        n = t.shape[1]
        q, k, v = jnp.split(t @ w_in, [GA_HEADS * GA_HEAD_DIM, (GA_HEADS + GA_KV_HEADS) * GA_HEAD_DIM], axis=-1)
        q = rmsnorm(q.reshape(b, n, GA_KV_HEADS, grp, GA_HEAD_DIM), q_norm)
        k = rmsnorm(k.reshape(b, n, GA_KV_HEADS, GA_HEAD_DIM), k_norm)
        return q, k, v.reshape(b, n, GA_KV_HEADS, GA_HEAD_DIM)

    q, k, v = project(h)
    qc, kc, vc = project(hc)
    q, k = apply_rope(q, cos, sin), apply_rope(k, cos, sin)
    scale = GA_HEAD_DIM ** -0.5
    k_all = jnp.concatenate([kc, k], axis=1)
    v_all = jnp.concatenate([vc, v], axis=1)
    y = sweep_query_blocks(lambda i, qb: grouped_attend(qb, k_all, v_all, scale), q) @ w_out
    yc = grouped_attend(qc, kc, vc, scale) @ w_out if ctx_out else None
    return y, yc


def mla_mixer(h, hc, w_in, q_norm, kv_norm, w_uq, w_ukv, w_out, ctx_out):
    b, s, _ = h.shape
    cos, sin = axial_rope_tables(s, MLA_ROPE_DIM)

    def project(t):
        n = t.shape[1]
        cq, ckv, kr = jnp.split(t @ w_in, [MLA_Q_RANK, MLA_Q_RANK + MLA_KV_RANK], axis=-1)
        q = (rmsnorm(cq, q_norm) @ w_uq).reshape(b, n, MLA_HEADS, MLA_NOPE_DIM + MLA_ROPE_DIM)
        kv = (rmsnorm(ckv, kv_norm) @ w_ukv).reshape(b, n, MLA_HEADS, MLA_NOPE_DIM + MLA_V_DIM)
        qn, qr = jnp.split(q, [MLA_NOPE_DIM], axis=-1)
        kn, v = jnp.split(kv, [MLA_NOPE_DIM], axis=-1)
        return qn, qr, kn, kr, v

    qn, qr, kn, kr, v = project(h)
    qnc, qrc, knc, krc, vc = project(hc)
    qr, kr = apply_rope(qr, cos, sin), apply_rope(kr, cos, sin)
    scale = (MLA_NOPE_DIM + MLA_ROPE_DIM) ** -0.5

    def attend(qnb, qrb, knb, krb, vb):
        sc = (jnp.einsum('blhd,bkhd->bhlk', qnb, knb, preferred_element_type=jnp.float32)
              + jnp.einsum('blhd,bkd->bhlk', qrb, krb, preferred_element_type=jnp.float32)) * scale
        p = jax.nn.softmax(sc, axis=-1)
        o = jnp.einsum('bhlk,bkhd->blhd', p.astype(vb.dtype), vb)
        return o.reshape(o.shape[0], o.shape[1], MLA_HEADS * MLA_V_DIM)

    kn_all = jnp.concatenate([knc, kn], axis=1)
    kr_all = jnp.concatenate([krc, kr], axis=1)
    v_all = jnp.concatenate([vc, v], axis=1)
    y = sweep_query_blocks(lambda i, a, r: attend(a, r, kn_all, kr_all, v_all), qn, qr) @ w_out
    yc = attend(qnc, qrc, knc, krc, vc) @ w_out if ctx_out else None
    return y, yc


def window_sink_mixer(h, hc, w_in, sink, w_out, ctx_out):
    b, s, _ = h.shape
    cos, sin = axial_rope_tables(s, SWA_HEAD_DIM)
    grp = SWA_HEADS // SWA_KV_HEADS

    def project(t):
        n = t.shape[1]
        q, k, v = jnp.split(t @ w_in, [SWA_HEADS * SWA_HEAD_DIM, (SWA_HEADS + SWA_KV_HEADS) * SWA_HEAD_DIM], axis=-1)
        return (q.reshape(b, n, SWA_KV_HEADS, grp, SWA_HEAD_DIM), k.reshape(b, n, SWA_KV_HEADS, SWA_HEAD_DIM),
                v.reshape(b, n, SWA_KV_HEADS, SWA_HEAD_DIM))

    q, k, v = project(h)
    qc, kc, vc = project(hc)
    q, k = apply_rope(q, cos, sin), apply_rope(k, cos, sin)
    scale = SWA_HEAD_DIM ** -0.5
    sink_l = sink.astype(jnp.float32).reshape(SWA_KV_HEADS, grp)[None, :, :, None, None]

    def sink_softmax_mix(sc, vals):
        sc = jnp.concatenate([sc, jnp.broadcast_to(sink_l, sc.shape[:-1] + (1,))], axis=-1)
        p = jax.nn.softmax(sc, axis=-1)[..., :-1]
        o = jnp.einsum('bhglk,bkhd->blhgd', p.astype(vals.dtype), vals)
        return o.reshape(o.shape[:2] + (-1,))

    pad = ((0, 0), (WINDOW, WINDOW), (0, 0), (0, 0))
    k_pad, v_pad = jnp.pad(k, pad), jnp.pad(v, pad)

    def latent_block(i, qb):
        start = i * Q_BLOCK
        kb = lax.dynamic_slice_in_dim(k_pad, start, BAND, axis=1)
        vb = lax.dynamic_slice_in_dim(v_pad, start, BAND, axis=1)
        qpos = start + jnp.arange(Q_BLOCK)
        kpos = start - WINDOW + jnp.arange(BAND)
        allowed = (kpos[None, :] >= 0) & (kpos[None, :] < s) & (jnp.abs(qpos[:, None] - kpos[None, :]) <= WINDOW)
        s_band = jnp.einsum('blhgd,bkhd->bhglk', qb, kb, preferred_element_type=jnp.float32) * scale
        s_band = jnp.where(allowed, s_band, NEG_INF)
        s_ctx = jnp.einsum('blhgd,bkhd->bhglk', qb, kc, preferred_element_type=jnp.float32) * scale
        return sink_softmax_mix(jnp.concatenate([s_ctx, s_band], axis=-1), jnp.concatenate([vc, vb], axis=1))

    y = sweep_query_blocks(latent_block, q) @ w_out
    yc = None
    if ctx_out:
        s_cc = jnp.einsum('blhgd,bkhd->bhglk', qc, kc, preferred_element_type=jnp.float32) * scale
        yc = sink_softmax_mix(s_cc, vc) @ w_out
    return y, yc


def setup_inputs(seed: int = 0) -> dict:
    key = jax.random.key(seed)
    ks = iter(jax.random.split(key, 32))
    f32 = jnp.float32

    def nrm(shape, scale=1.0):
        return jax.random.normal(next(ks), shape, f32) * scale

    def w(shape, fan_in):
        return nrm(shape, fan_in ** -0.5)

    def gain(shape):
        return 1.0 + nrm(shape, 0.02)

    n0, n1, n2, n3 = [len(range(k, DEPTH, N_MIXERS)) for k in range(N_MIXERS)]
    return {
        "x": nrm((BATCH, SEQ, D_MODEL)),
        "c": nrm((BATCH, D_MODEL)),
        "ctx": nrm((BATCH, CTX_LEN, D_MODEL)),
        "c_ctx": nrm((D_MODEL,)),
        "ada_w": w((DEPTH, D_MODEL, N_MOD * D_MODEL), D_MODEL),
        "ada_b": nrm((DEPTH, N_MOD * D_MODEL), 0.02),
        "norm_g": gain((DEPTH, 6, D_MODEL)),
        "ffn_w_in": w((DEPTH, 2, D_MODEL, 2 * D_FF), D_MODEL),
        "ffn_w_out": w((DEPTH, 2, D_FF, D_MODEL), D_FF),
        "da_w_in": w((n0, D_MODEL, DA_QKV), D_MODEL),
        "da_lambda": nrm((n0, 4, DA_QK_DIM), 0.1),
        "da_subln": gain((n0, DA_V_DIM)),
        "da_w_out": w((n0, DA_HEADS * DA_V_DIM, D_MODEL), DA_HEADS * DA_V_DIM),
        "ga_w_in": w((n1, D_MODEL, (GA_HEADS + 2 * GA_KV_HEADS) * GA_HEAD_DIM), D_MODEL),
        "ga_q_norm": gain((n1, GA_HEAD_DIM)),
        "ga_k_norm": gain((n1, GA_HEAD_DIM)),
        "ga_w_out": w((n1, GA_HEADS * GA_HEAD_DIM, D_MODEL), GA_HEADS * GA_HEAD_DIM),
        "mla_w_in": w((n2, D_MODEL, MLA_Q_RANK + MLA_KV_RANK + MLA_ROPE_DIM), D_MODEL),
        "mla_q_norm": gain((n2, MLA_Q_RANK)),
        "mla_kv_norm": gain((n2, MLA_KV_RANK)),
        "mla_w_uq": w((n2, MLA_Q_RANK, MLA_HEADS * (MLA_NOPE_DIM + MLA_ROPE_DIM)), MLA_Q_RANK),
        "mla_w_ukv": w((n2, MLA_KV_RANK, MLA_HEADS * (MLA_NOPE_DIM + MLA_V_DIM)), MLA_KV_RANK),
        "mla_w_out": w((n2, MLA_HEADS * MLA_V_DIM, D_MODEL), MLA_HEADS * MLA_V_DIM),
        "swa_w_in": w((n3, D_MODEL, (SWA_HEADS + 2 * SWA_KV_HEADS) * SWA_HEAD_DIM), D_MODEL),
        "swa_sink": nrm((n3, SWA_HEADS), 0.5),
        "swa_w_out": w((n3, SWA_HEADS * SWA_HEAD_DIM, D_MODEL), SWA_HEADS * SWA_HEAD_DIM),
    }


def reference(x, c, ctx, c_ctx, ada_w, ada_b, norm_g, ffn_w_in, ffn_w_out,
              da_w_in, da_lambda, da_subln, da_w_out,
              ga_w_in, ga_q_norm, ga_k_norm, ga_w_out,
              mla_w_in, mla_q_norm, mla_kv_norm, mla_w_uq, mla_w_ukv, mla_w_out,
              swa_w_in, swa_sink, swa_w_out):
    h, hc = x, ctx
    for i in range(DEPTH):
        kind, occ = i % N_MIXERS, i // N_MIXERS
        ctx_out = i < DEPTH - 1
        m = adaln(c, ada_w[i], ada_b[i])[:, None]
        mc = adaln(c_ctx, ada_w[i], ada_b[i])[None]
        g = norm_g[i]
        h = half_ffn(h, m, 0, g[0], g[1], ffn_w_in[i, 0], ffn_w_out[i, 0])
        hc = half_ffn(hc, mc, 0, g[0], g[1], ffn_w_in[i, 0], ffn_w_out[i, 0])
        u = modulated_norm(h, g[2], m, 3)
        uc = modulated_norm(hc, g[2], mc, 3)
        if kind == 0:
            lambda_init = 0.8 - 0.6 * math.exp(-0.3 * i)
            y, yc = diff_attention_mixer(u, uc, da_w_in[occ], da_lambda[occ], da_subln[occ], da_w_out[occ],
                                         lambda_init, ctx_out)
        elif kind == 1:
            y, yc = gqa_axial_mixer(u, uc, ga_w_in[occ], ga_q_norm[occ], ga_k_norm[occ], ga_w_out[occ], ctx_out)
        elif kind == 2:
            y, yc = mla_mixer(u, uc, mla_w_in[occ], mla_q_norm[occ], mla_kv_norm[occ], mla_w_uq[occ],
                              mla_w_ukv[occ], mla_w_out[occ], ctx_out)
        else:
            y, yc = window_sink_mixer(u, uc, swa_w_in[occ], swa_sink[occ], swa_w_out[occ], ctx_out)
        h = gated_residual(h, y, g[3], m, 3, 1.0)
        h = half_ffn(h, m, 6, g[4], g[5], ffn_w_in[i, 1], ffn_w_out[i, 1])
        if ctx_out:
            hc = gated_residual(hc, yc, g[3], mc, 3, 1.0)
            hc = half_ffn(hc, mc, 6, g[4], g[5], ffn_w_in[i, 1], ffn_w_out[i, 1])
    return h
```

```python
import math
import numpy as np
from contextlib import ExitStack
import concourse.bass as bass
import concourse.mybir as mybir
from concourse.bass_utils import run_bass_kernel_spmd

F32 = mybir.dt.float32
BF16 = mybir.dt.bfloat16
AF = mybir.ActivationFunctionType
ALU = mybir.AluOpType

D = 1024
KC = 8
CTX = 256
S = 4096
T = CTX + S
DFF = 2816
NJ = 22
EPS = 1e-6
NCH = 9
NCORES = 8


def chunk_range(ci):
    if ci == 0:
        return 0, CTX
    return CTX + 512 * (ci - 1), 512


class Buf:
    __slots__ = ("w", "r", "excl")

    def __init__(self, excl=False):
        self.w = None
        self.r = {}
        self.excl = excl


class Sched:
    ENGS = ("pe", "act", "dve", "pool", "sp")

    def __init__(self, nc, st):
        self.nc = nc
        self.st = st
        self.prog = {e: [] for e in self.ENGS}
        self.sems = {}
        self.cnt = {}
        self.seen = {e: {} for e in self.ENGS}
        for e in ("pe", "act", "dve", "pool"):
            self.new_sem(e)

    def new_sem(self, key):
        if key in self.sems:
            return key
        self.sems[key] = self.st.enter_context(self.nc.semaphore(key))
        self.cnt[key] = 0
        return key

    def dsem(self, idx):
        return self.new_sem(f"d{idx}")

    def _waits(self, e, reads, writes):
        need = {}
        for b in reads:
            if b.w is not None:
                k, v = b.w
                if v > need.get(k, 0):
                    need[k] = v
            if b.excl:
                for k, v in b.r.items():
                    if k != e and v > need.get(k, 0):
                        need[k] = v
        for b in writes:
            if b.w is not None:
                k, v = b.w
                if v > need.get(k, 0):
                    need[k] = v
            for k, v in b.r.items():
                if v > need.get(k, 0):
                    need[k] = v
        out = []
        seen = self.seen[e]
        for k, v in need.items():
            if k == e and e == "pe":
                continue
            if seen.get(k, 0) >= v:
                continue
            seen[k] = v
            out.append((self.sems[k], v))
        return out

    def op(self, e, fn, reads=(), writes=(), signal=True):
        waits = self._waits(e, reads, writes)
        sem = self.sems[e]
        if signal:
            self.cnt[e] += 1
            tv = self.cnt[e]
        else:
            tv = self.cnt[e] + 1
        self.prog[e].append((waits, fn, sem if signal else None, 1))
        for b in reads:
            if tv > b.r.get(e, 0):
                b.r[e] = tv
        for b in writes:
            b.w = (e, tv)
            b.r = {}

    def dma(self, q, out, in_, semkey, reads=(), writes=()):
        waits = self._waits(q, reads, writes)
        self.cnt[semkey] += 16
        tv = self.cnt[semkey]
        self.prog[q].append((waits, lambda e, out=out, in_=in_: e.dma_start(out=out, in_=in_), self.sems[semkey], 16))
        for b in reads:
            if tv > b.r.get(semkey, 0):
                b.r[semkey] = tv
        for b in writes:
            b.w = (semkey, tv)
            b.r = {}

    def barrier(self):
        for e in self.ENGS:
            waits = []
            for k, h in self.sems.items():
                v = self.cnt[k]
                if v == 0 or self.seen[e].get(k, 0) >= v or k.startswith("cv"):
                    continue
                if k == e and e == "pe":
                    continue
                self.seen[e][k] = v
                waits.append((h, v))
            if waits:
                self.prog[e].append((waits, None, None, 0))

    def emit(self):
        nc = self.nc
        with nc.Block() as block:
            def run(eng, lst):
                for waits, fn, sem, inc in lst:
                    for h, v in waits:
                        eng.wait_ge(h, v)
                    if fn is not None:
                        ins = fn(eng)
                        if sem is not None:
                            ins.then_inc(sem, inc)

            @block.tensor
            def _(e):
                run(e, self.prog["pe"])

            @block.scalar
            def _(e):
                run(e, self.prog["act"])

            @block.vector
            def _(e):
                run(e, self.prog["dve"])

            @block.gpsimd
            def _(e):
                run(e, self.prog["pool"])

            @block.sync
            def _(e):
                run(e, self.prog["sp"])


class Arena:
    def __init__(self, ap_f32, nwords):
        self.ap = ap_f32
        self.n = nwords
        self.off = 0
        self.base = 0

    def set_base(self):
        self.base = self.off

    def reset(self):
        self.off = self.base

    def f32(self, n):
        a = self.off
        self.off += n
        assert self.off <= self.n, ("arena overflow", self.off, self.n)
        return self.ap[:, a:a + n]

    def bf16(self, n):
        w = (n + 1) // 2
        return self.f32(w).bitcast(BF16)[:, 0:n]


class Ring:
    def __init__(self, items):
        self.items = items
        self.i = 0

    def get(self):
        it = self.items[self.i % len(self.items)]
        self.i += 1
        return it


class Stream:
    def __init__(self, sch, q, slots, semkeys, extra_reads=()):
        self.sch = sch
        self.extra_reads = list(extra_reads)
        self.q = q
        self.slots = slots
        self.sems = semkeys
        self.fills = []
        self.issued = 0
        self.taken = 0

    def add(self, fn):
        self.fills.append(fn)

    def _issue(self):
        i = self.issued
        ap, buf = self.slots[i % len(self.slots)]
        for (o, in_) in self.fills[i](ap):
            self.sch.dma(self.q, o, in_, self.sems[i % len(self.slots)], reads=self.extra_reads, writes=[buf])
        self.issued += 1

    def get(self, ahead=None):
        i = self.taken
        if ahead is None:
            ahead = len(self.slots)
        while self.issued < min(len(self.fills), i + ahead):
            self._issue()
        self.taken += 1
        return self.slots[i % len(self.slots)]

    def peek(self, off=0):
        return self.slots[(self.taken + off) % len(self.slots)]

    def prefetch(self, ahead=None):
        if ahead is None:
            ahead = len(self.slots)
        while self.issued < min(len(self.fills), self.taken + ahead):
            self._issue()


def fmv(v):
    v = np.asarray(v, np.float32)
    return np.ascontiguousarray(v.reshape(-1, 128).T)


def lhs_blocks(W, col_lists, kc):
    W = np.asarray(W, np.float32)
    out = np.zeros((128, len(col_lists), kc, 128), np.float32)
    Wr = W.reshape(kc, 128, W.shape[1])
    for b, cols in enumerate(col_lists):
        out[:, b, :, :len(cols)] = Wr[:, :, cols].transpose(1, 0, 2)
    return out.reshape(128, -1)


def rhs_fmt(W, cols, kc):
    W = np.asarray(W, np.float32)
    Wr = W.reshape(kc, 128, W.shape[1])[:, :, cols]
    return np.ascontiguousarray(Wr.transpose(1, 0, 2)).reshape(128, -1)


def swap_pairs(cols):
    c = np.asarray(cols).reshape(-1, 2)[:, ::-1].reshape(-1)
    return c


def rope_tables(rot_dim, row_dims):
    n_axis = rot_dim // 4
    pos = np.arange(S)
    row = (pos // 64).astype(np.float32)
    col = (pos % 64).astype(np.float32)
    freqs = (np.float32(10000.0) ** (-np.arange(n_axis, dtype=np.float32) / np.float32(n_axis))).astype(np.float32)
    ang = np.concatenate([row[:, None] * freqs, col[:, None] * freqs], axis=-1).astype(np.float32)
    cos = np.cos(ang).astype(np.float32)
    sin = np.sin(ang).astype(np.float32)
    C = np.ones((128, T), np.float32)
    Sg = np.zeros((128, T), np.float32)
    for p in range(128):
        d = row_dims[p]
        if d < 0:
            continue
        j = d // 2
        C[p, CTX:] = cos[:, j]
        Sg[p, CTX:] = -sin[:, j] if d % 2 == 0 else sin[:, j]
    return C, Sg


LAMBDA_INIT0 = 0.8 - 0.6 * math.exp(-0.3 * 0)


def prep_shared(inp):
    sh = {}
    sh["ada_w"] = np.ascontiguousarray(inp["ada_w"], np.float32)
    sh["ada_b"] = np.ascontiguousarray(inp["ada_b"], np.float32)
    sh["sm_i2"] = np.eye(2, dtype=np.float32)
    g = np.asarray(inp["norm_g"], np.float32).reshape(4, 6, 8, 128)
    sh["gT"] = np.ascontiguousarray(g.transpose(3, 0, 1, 2)).reshape(128, 4 * 6 * 8)
    wi = np.asarray(inp["ffn_w_in"], np.float32)
    wo = np.asarray(inp["ffn_w_out"], np.float32)
    for i in range(4):
        for s in range(2):
            W = wi[i, s]
            Wg = W[:, :DFF].reshape(8, 128, NJ, 128)
            Wu = W[:, DFF:].reshape(8, 128, NJ, 128)
            Wc = np.concatenate([Wg, Wu], axis=-1)
            sh[f"win_{i}_{s}"] = np.ascontiguousarray(Wc.transpose(1, 2, 0, 3)).reshape(128, NJ * 8 * 256)
            Wo = wo[i, s].reshape(NJ, 128, 8, 128)
            sh[f"wout_{i}_{s}"] = np.ascontiguousarray(Wo.transpose(1, 2, 0, 3)).reshape(128, 8 * NJ * 128)
    prep_mixers(inp, sh)
    return sh


def wo_fmt(W):
    W = np.asarray(W, np.float32).reshape(8, 128, 8, 128)
    return np.ascontiguousarray(W.transpose(1, 2, 0, 3)).reshape(128, -1)


def prep_mixers(inp, sh):
    ar = np.arange
    Wi = inp["da_w_in"][0]
    qc = [h * 128 + ar(128) for h in range(8)]
    kc_ = [1024 + h * 128 + ar(128) for h in range(8)]
    sh["mx_qk_0"] = lhs_blocks(Wi, qc + [swap_pairs(c) for c in qc] + kc_ + [swap_pairs(c) for c in kc_], 8)
    sh["mx_v_0"] = rhs_fmt(Wi, 2048 + ar(1024), 8)
    sh["mx_o_0"] = wo_fmt(inp["da_w_out"][0])
    sh["sm_lam_0"] = np.ascontiguousarray(np.broadcast_to(np.asarray(inp["da_lambda"][0], np.float32).reshape(1, 256), (128, 256)))
    sh["sm_subln_0"] = np.asarray(inp["da_subln"][0], np.float32).reshape(128, 1).copy()
    Wi = inp["ga_w_in"][0]
    qc = [h * 128 + ar(128) for h in range(8)]
    kc_ = [1024 + g * 128 + ar(128) for g in range(2)]
    sh["mx_qk_1"] = lhs_blocks(Wi, qc + [swap_pairs(c) for c in qc] + kc_ + [swap_pairs(c) for c in kc_], 8)
    sh["mx_v_1"] = rhs_fmt(Wi, 1280 + ar(256), 8)
    sh["mx_o_1"] = wo_fmt(inp["ga_w_out"][0])
    gq = np.asarray(inp["ga_q_norm"][0], np.float32)
    gk = np.asarray(inp["ga_k_norm"][0], np.float32)
    sp = swap_pairs(ar(128))
    sh["sm_g_1"] = np.ascontiguousarray(np.stack([gq, gq[sp], gk, gk[sp]], axis=1))
    Wi = inp["mla_w_in"][0]
    pad = np.full(64, 384)
    sh["mx_in_2"] = lhs_blocks(Wi, [ar(128), 128 + ar(128), 256 + ar(128), np.concatenate([pad, 384 + ar(32)]), np.concatenate([pad, 384 + swap_pairs(ar(32))])], 8)
    Wq = inp["mla_w_uq"][0]
    qc = [h * 96 + ar(96) for h in range(16)]
    qs = [h * 96 + np.concatenate([ar(64), 64 + swap_pairs(ar(32))]) for h in range(16)]
    sh["mx_uq_2"] = lhs_blocks(Wq, qc + qs, 2)
    Wkv = inp["mla_w_ukv"][0]
    sh["mx_kn_2"] = lhs_blocks(Wkv, [np.concatenate([(2 * jb) * 128 + ar(64), (2 * jb + 1) * 128 + ar(64)]) for jb in range(8)], 1)
    sh["mx_uv_2"] = rhs_fmt(Wkv, np.concatenate([h * 128 + 64 + ar(64) for h in range(16)]), 1)
    sh["mx_o_2"] = wo_fmt(inp["mla_w_out"][0])
    gq = np.asarray(inp["mla_q_norm"][0], np.float32)
    gkv = np.asarray(inp["mla_kv_norm"][0], np.float32)
    sh["sm_g_2"] = np.ascontiguousarray(np.stack([gq[0:128], gq[128:256], gkv], axis=1))
    Wi = inp["swa_w_in"][0]
    qc = [jb * 128 + ar(128) for jb in range(8)]
    kc_ = [1024 + ar(128)]
    sh["mx_qk_3"] = lhs_blocks(Wi, qc + [swap_pairs(c) for c in qc] + kc_ + [swap_pairs(c) for c in kc_], 8)
    sh["mx_v_3"] = rhs_fmt(Wi, 1152 + ar(128), 8)
    sh["mx_o_3"] = wo_fmt(inp["swa_w_out"][0])
    sh["sm_sink_3"] = np.ascontiguousarray(np.broadcast_to(np.asarray(inp["swa_sink"][0], np.float32).reshape(1, 16), (128, 16)))
    kk = ar(128)[:, None]
    qq = ar(384)[None, :]
    sh["sm_mask_3"] = ((qq >= kk) & (qq <= kk + 256)).astype(np.float32)
    p = ar(128)
    rd2 = np.full(128, -1)
    rd2[64:96] = ar(32)
    for L, (rot, rd) in enumerate([(64, p % 64), (128, p), (32, rd2), (64, p % 64)]):
        C, Sg = rope_tables(rot, rd)
        sh[f"rp_c_{L}"] = C
        sh[f"rp_s_{L}"] = Sg


def prep_core(inp, b):
    pc = {}
    h0 = np.concatenate([np.asarray(inp["ctx"][b], np.float32), np.asarray(inp["x"][b], np.float32)], axis=0)
    pc["xT"] = np.ascontiguousarray(h0.T).reshape(8, 128, T)
    pc["cT"] = np.ascontiguousarray(np.stack([fmv(inp["c"][b]), fmv(inp["c_ctx"])], axis=-1)).reshape(128, 16)
    return pc


class Prog:
    def __init__(self, n_sub, shared_shapes):
        self.n_sub = n_sub
        nc = self.nc = bass.Bass("TRN2", target_bir_lowering=False)
        self.st = ExitStack()
        st = self.st
        self.din = {}
        for name, shp in shared_shapes.items():
            self.din[name] = nc.dram_tensor(name, list(shp), F32, kind="ExternalInput").ap()
        self.xT = nc.dram_tensor("xT", [8, 128, T], F32, kind="ExternalInput").ap()
        self.cT = nc.dram_tensor("cT", [128, 16], F32, kind="ExternalInput").ap()
        self.outT = nc.dram_tensor("outT", [8, 128, T], F32, kind="ExternalOutput").ap()
        self.wb = {}
        for name, shp in shared_shapes.items():
            if name.startswith("win_") or name.startswith("wout_") or name.startswith("mx_"):
                self.wb[name] = nc.dram_tensor(name + "_b", list(shp), BF16, kind="Internal").ap()
        self.scr = {}
        for name, shp in (("KT0", [1024, T]), ("V0", [8, 128, 34 * 128]), ("KT1", [256, T]), ("V1", [2, 128, 34 * 128]),
                          ("KN2", [1024, T]), ("KR2", [32, T]), ("V2", [8, 128, 34 * 128]),
                          ("KT3", [128, T]), ("V3", [2, 128, 34 * 64])):
            self.scr[name] = nc.dram_tensor("scr_" + name, shp, BF16, kind="Internal").ap()
        self.sch = Sched(nc, st)
        NW = 50 * 1024
        self.arena_t = st.enter_context(nc.sbuf_tensor("arena", [128, NW], F32))
        self.ar = Arena(self.arena_t[:, :], NW)
        self.psum = []
        for i in range(8):
            t = st.enter_context(nc.psum_tensor(f"ps{i}", [128, 512], F32))
            self.psum.append((t, Buf(excl=True)))

    def ps_ring(self, idxs):
        return Ring([self.psum[i] for i in idxs])

    def mm_group(self, out_ap, pairs, reads, writes):
        n = len(pairs)
        for i, (l, r) in enumerate(pairs):
            last = i == n - 1
            self.sch.op("pe", lambda e, l=l, r=r, i=i, last=last: e.matmul(out_ap, l, r, start=(i == 0), stop=last),
                        reads=reads if i == 0 else (), writes=writes if i == 0 else (), signal=last)

    def act(self, out, in_, func, reads, writes, bias=None, scale=None):
        kw = {}
        if bias is not None:
            kw["bias"] = bias
        if scale is not None:
            kw["scale"] = scale
        self.sch.op("act", lambda e: e.activation(out=out, in_=in_, func=func, **kw), reads=reads, writes=writes)

    def tt(self, eng, out, in0, in1, op, reads, writes):
        self.sch.op(eng, lambda e: e.tensor_tensor(out=out, in0=in0, in1=in1, op=op), reads=reads, writes=writes)

    def ts(self, eng, out, in0, s1, s2, op0, op1, reads, writes):
        if op1 is None:
            self.sch.op(eng, lambda e: e.tensor_scalar(out=out, in0=in0, scalar1=s1, scalar2=None, op0=op0), reads=reads, writes=writes)
        else:
            self.sch.op(eng, lambda e: e.tensor_scalar(out=out, in0=in0, scalar1=s1, scalar2=s2, op0=op0, op1=op1), reads=reads, writes=writes)

    def stt(self, eng, out, in0, scalar, in1, op0, op1, reads, writes):
        self.sch.op(eng, lambda e: e.scalar_tensor_tensor(out=out, in0=in0, scalar=scalar, in1=in1, op0=op0, op1=op1), reads=reads, writes=writes)

    def cv_setup(self):
        order = []
        for i in range(4):
            order.append((f"F{i}0", [f"win_{i}_0", f"wout_{i}_0"]))
            order.append((f"M{i}", [n for n in self.wb if n.startswith("mx_") and n.endswith(f"_{i}")]))
            order.append((f"F{i}1", [f"win_{i}_1", f"wout_{i}_1"]))
        self.cv_groups = {}
        self.cv_list = []
        for gi, (g, names) in enumerate(order):
            sem = self.sch.new_sem(f"cv{gi}")
            grp = {"sem": sem, "buf": Buf(), "n": 0, "issued": 0}
            self.cv_groups[g] = grp
            for name in names:
                src, dst = self.din[name], self.wb[name]
                n = src.shape[1]
                step = 8192
                for a in range(0, n, step):
                    b_ = min(n, a + step)
                    self.cv_list.append((grp, dst[:, a:b_], src[:, a:b_]))
                    grp["n"] += 1
        self.cv_pos = 0

    def pump(self, k):
        while k > 0 and self.cv_pos < len(self.cv_list):
            grp, dst, src = self.cv_list[self.cv_pos]
            self.sch.dma("pool", dst, src, grp["sem"])
            grp["issued"] += 1
            if grp["issued"] == grp["n"]:
                grp["buf"].w = (grp["sem"], self.sch.cnt[grp["sem"]])
            self.cv_pos += 1
            k -= 1

    def need(self, g):
        grp = self.cv_groups[g]
        while grp["issued"] < grp["n"]:
            self.pump(1)
        return grp["buf"]

    def phase0(self):
        sch, ar, nc = self.sch, self.ar, self.nc
        self.ones = ar.bf16(128)
        self.ones_b = Buf()
        self.epsc = ar.f32(1)
        self.consts_b = Buf()
        self.VEC = ar.f32(4 * 3 * 2 * 3 * 8).rearrange("p (i s w v k) -> p i s w v k", i=4, s=3, w=2, v=3)
        self.vec_b = Buf()
        ar.set_base()
        sch.op("pool", lambda e: e.memset(self.ones, 1.0), writes=[self.ones_b])
        sch.op("pool", lambda e: e.memset(self.epsc, EPS), writes=[self.consts_b])
        self.cv_setup()
        self.need("F00")
        self.need("M0")
        cT = ar.f32(16)
        sc = ar.f32(16)
        scb = Buf()
        gT = ar.f32(4 * 6 * 8).rearrange("p (i r k) -> p i r k", i=4, r=6)
        mT = ar.f32(4 * 2 * 72).rearrange("p (i w n) -> p i w n", i=4, w=2)
        mTb = Buf()
        ldb = Buf()
        brow = ar.f32(9216)
        browb = Buf()
        mrow = ar.f32(9216)
        mrowb = Buf()
        ident = ar.f32(2)
        identb = Buf()
        sch.dsem(15)
        sch.dma("sp", cT, self.cT, "d15", writes=[ldb])
        sch.dma("sp", gT, self.din["gT"].rearrange("p (i r k) -> p i r k", i=4, r=6), "d15", writes=[ldb])
        self.act(sc, cT, AF.Silu, [ldb], [scb])
        sc3 = sc.rearrange("p (k w) -> p k w", w=2)
        sch.dsem(13)
        sch.dma("sp", ident[0:2, 0:2], self.din["sm_i2"], "d13", writes=[identb])
        nslot = 4
        slots = []
        semk = []
        for i in range(nslot):
            slots.append((ar.f32(8 * 512).rearrange("p (k n) -> p k n", k=8), Buf()))
            semk.append(sch.dsem(i))
        strm = Stream(sch, "sp", slots, semk)
        aw = self.din["ada_w"]
        for i in range(4):
            v = aw[i].rearrange("(kc p) n -> p kc n", p=128)
            for pc in range(18):
                strm.add(lambda ap, v=v, pc=pc: [(ap, v[:, :, pc * 512:(pc + 1) * 512])])
        pr = self.ps_ring([0, 1, 2, 3])
        pt_ring = self.ps_ring([4, 5])
        sch.dsem(14)
        for i in range(4):
            for w in range(2):
                sch.dma("sp", brow[w:w + 1, :], self.din["ada_b"][i:i + 1, :], "d14", writes=[browb])
            for pc in range(18):
                wap, wbuf = strm.get()
                pt, pb = pr.get()
                self.mm_group(pt[0:2, 0:512], [(sc3[:, kc, :], wap[:, kc, :]) for kc in range(8)], reads=[wbuf, scb], writes=[pb])
                self.tt("dve", mrow[0:2, pc * 512:(pc + 1) * 512], pt[0:2, 0:512], brow[0:2, pc * 512:(pc + 1) * 512], ALU.add, [pb, browb], [mrowb])
            tp, tb = pt_ring.get()
            ps3 = tp[:, 0:144].rearrange("p (n w) -> p n w", w=2)
            for n in range(72):
                self.mm_group(ps3[:, n, :], [(mrow[0:2, n * 128:(n + 1) * 128], ident[0:2, 0:2])], reads=[mrowb, identb], writes=[tb])
            for w in range(2):
                sch.op("dve", lambda e, o=mT[:, i, w, :], a=ps3[:, :, w]: e.tensor_copy(out=o, in_=a), reads=[tb], writes=[mTb])
        for i in range(4):
            for s in range(3):
                k0 = 3 * s
                wgt = 0.5 if s != 1 else 1.0
                for w in range(2):
                    m_sh = mT[:, i, w, (k0) * 8:(k0 + 1) * 8]
                    m_sc = mT[:, i, w, (k0 + 1) * 8:(k0 + 2) * 8]
                    m_gt = mT[:, i, w, (k0 + 2) * 8:(k0 + 3) * 8]
                    self.stt("dve", self.VEC[:, i, s, w, 0, :], m_sc, 1.0, gT[:, i, 2 * s, :], ALU.add, ALU.mult, [mTb, ldb], [self.vec_b])
                    self.sch.op("dve", lambda e, o=self.VEC[:, i, s, w, 1, :], a=m_sh: e.tensor_copy(out=o, in_=a), reads=[mTb], writes=[self.vec_b])
                    self.stt("dve", self.VEC[:, i, s, w, 2, :], m_gt, wgt, gT[:, i, 2 * s + 1, :], ALU.mult, ALU.mult, [mTb, ldb], [self.vec_b])
        sch.barrier()
        ar.reset()

    def rstd_from_sq(self, sq_aps, sq_bufs, n, dim, ps_item, out_ap, out_buf, mode, tmp_ap=None, tmp_buf=None):
        pt, pb = ps_item
        self.mm_group(pt[:, 0:n], [(self.ones, a) for a in sq_aps], reads=list(sq_bufs) + [self.ones_b], writes=[pb])
        if mode == "sqrt":
            self.act(out_ap, pt[:, 0:n], AF.Sqrt, [pb, self.consts_b], [out_buf], bias=self.epsc[:, 0:1], scale=1.0 / dim)
            self.sch.op("dve", lambda e: e.reciprocal(out=out_ap, in_=out_ap), reads=[out_buf], writes=[out_buf])
        else:
            self.act(out_ap, pt[:, 0:n], AF.Ln, [pb, self.consts_b], [out_buf], bias=self.epsc[:, 0:1], scale=1.0 / dim)
            self.act(out_ap, out_ap, AF.Exp, [out_buf], [out_buf], scale=-0.5)

    def norm_mod(self, hin, hb, n, vec, sqb, sq_bufs, ps_item, rstd, rstd_b, tmp_ring, uT, uT_bufs, mode):
        for k in range(8):
            self.act(sqb[:, k, 0:n], hin[:, k, 0:n], AF.Square, [hb], [sq_bufs[k]])
        self.rstd_from_sq([sqb[:, k, 0:n] for k in range(8)], sq_bufs, n, float(D), ps_item, rstd[:, 0:n], rstd_b, mode)
        for k in range(8):
            tp, tb = tmp_ring.get()
            self.tt("dve", tp[:, 0:n], hin[:, k, 0:n], rstd[:, 0:n], ALU.mult, [hb, rstd_b], [tb])
            self.act(uT[:, k, 0:n], tp[:, 0:n], AF.Identity, [tb, self.vec_b], [uT_bufs[k]], bias=vec[:, 1, k:k + 1], scale=vec[:, 0, k:k + 1])

    def residual(self, hin, hb, n, vec, ytmp, ysq, y_bufs, ysq_bufs, ps_item, rstd, rstd_b, tmp_ring, mode):
        self.rstd_from_sq([ysq[:, m, 0:n] for m in range(8)], ysq_bufs, n, float(D), ps_item, rstd[:, 0:n], rstd_b, mode)
        for k in range(8):
            tp, tb = tmp_ring.get()
            self.stt("dve", tp[:, 0:n], ytmp[:, k, 0:n], vec[:, 2, k:k + 1], rstd[:, 0:n], ALU.mult, ALU.mult, [y_bufs[k], rstd_b, self.vec_b], [tb])
            self.tt("pool", hin[:, k, 0:n], hin[:, k, 0:n], tp[:, 0:n], ALU.add, [tb], [hb])

    def ffn_phase(self, i, s, src, dst, chunks):
        sch, ar = self.sch, self.ar
        ar.reset()
        svec = 0 if s == 0 else 2
        hslots = [(ar.f32(8 * 512).rearrange("p (k n) -> p k n", k=8), Buf()) for _ in range(2)]
        ld_sems = [sch.dsem(x) for x in range(2)]
        st_sems = [sch.dsem(2 + x) for x in range(2)]
        sqb = ar.bf16(8 * 512).rearrange("p (k n) -> p k n", k=8)
        sq_bufs = [Buf() for _ in range(8)]
        uTs = [(ar.bf16(8 * 512).rearrange("p (k n) -> p k n", k=8), [Buf() for _ in range(8)]) for _ in range(2)]
        aTs = [(ar.bf16(NJ * 512).rearrange("p (j n) -> p j n", j=NJ), [Buf() for _ in range(NJ)]) for _ in range(2)]
        tmp_ring = Ring([(ar.f32(512), Buf()) for _ in range(3)])
        sg_ring = Ring([(ar.f32(512), Buf()) for _ in range(3)])
        rstd_ring = Ring([(ar.f32(512), Buf()) for _ in range(2)])
        ytmp = ar.f32(8 * 512).rearrange("p (k n) -> p k n", k=8)
        y_bufs = [Buf() for _ in range(8)]
        ysq = ar.bf16(8 * 512).rearrange("p (k n) -> p k n", k=8)
        ysq_bufs = [Buf() for _ in range(8)]
        win_slots = [(ar.bf16(2 * 2048).rearrange("p (jj kc c) -> p jj kc c", jj=2, kc=8), Buf()) for _ in range(3)]
        win_sems = [sch.dsem(4 + x) for x in range(3)]
        wout_slots = [(ar.bf16(2 * NJ * 128).rearrange("p (mm j c) -> p mm j c", mm=2, j=NJ), Buf()) for _ in range(2)]
        wout_sems = [sch.dsem(7 + x) for x in range(2)]
        win_d = self.wb[f"win_{i}_{s}"].rearrange("p (j kc c) -> p j kc c", j=NJ, kc=8)
        wout_d = self.wb[f"wout_{i}_{s}"].rearrange("p (m j c) -> p m j c", m=8, j=NJ)
        cvb = self.need(f"F{i}{s}")
        wstream = Stream(sch, "sp", win_slots, win_sems, [cvb])
        ostream = Stream(sch, "sp", wout_slots, wout_sems, [cvb])
        for ci in chunks:
            for jg in range(NJ // 2):
                wstream.add(lambda ap, jg=jg: [(ap, win_d[:, 2 * jg:2 * jg + 2, :, :])])
            for mg in range(4):
                ostream.add(lambda ap, mg=mg: [(ap, wout_d[:, 2 * mg:2 * mg + 2, :, :])])
        ps_misc = self.ps_ring([0])
        ps_gu = self.ps_ring([1, 2, 3, 4])
        ps_y = self.ps_ring([5, 6])

        def load(idx):
            ci = chunks[idx]
            t0, n = chunk_range(ci)
            hp, hb = hslots[idx % 2]
            sch.dma("pool", hp[:, :, 0:n], src[:, :, t0:t0 + n].rearrange("k p n -> p k n"), ld_sems[idx % 2], writes=[hb])

        load(0)
        for idx, ci in enumerate(chunks):
            t0, n = chunk_range(ci)
            w = 1 if ci == 0 else 0
            vec = self.VEC[:, i, svec, w]
            hp, hb = hslots[idx % 2]
            if idx + 1 < len(chunks):
                load(idx + 1)
            uT, uT_bufs = uTs[idx % 2]
            aT, aT_bufs = aTs[idx % 2]
            ostream.prefetch()
            self.pump(2)
            rs, rsb = rstd_ring.get()
            self.norm_mod(hp, hb, n, vec, sqb, sq_bufs, ps_misc.get(), rs, rsb, tmp_ring, uT, uT_bufs, "sqrt")
            for jg in range(NJ // 2):
                wap, wbuf = wstream.get()
                for jj in range(2):
                    j = 2 * jg + jj
                    gp, gb = ps_gu.get()
                    up, ub = ps_gu.get()
                    self.mm_group(gp[:, 0:n], [(wap[:, jj, kc, 0:128], uT[:, kc, 0:n]) for kc in range(8)], reads=[wbuf] + uT_bufs, writes=[gb])
                    self.mm_group(up[:, 0:n], [(wap[:, jj, kc, 128:256], uT[:, kc, 0:n]) for kc in range(8)], reads=[wbuf] + uT_bufs, writes=[ub])
                    sgp, sgb = sg_ring.get()
                    self.act(sgp[:, 0:n], gp[:, 0:n], AF.Silu, [gb], [sgb])
                    self.tt("dve", aT[:, j, 0:n], up[:, 0:n], sgp[:, 0:n], ALU.mult, [ub, sgb], [aT_bufs[j]])
            for mg in range(4):
                oap, obuf = ostream.get()
                for mm in range(2):
                    m = 2 * mg + mm
                    yp, yb = ps_y.get()
                    self.mm_group(yp[:, 0:n], [(oap[:, mm, j, :], aT[:, j, 0:n]) for j in range(NJ)], reads=[obuf] + aT_bufs, writes=[yb])
                    self.act(ytmp[:, m, 0:n], yp[:, 0:n], AF.Copy, [yb], [y_bufs[m]])
                    self.tt("dve", ysq[:, m, 0:n], yp[:, 0:n], ytmp[:, m, 0:n], ALU.mult, [yb, y_bufs[m]], [ysq_bufs[m]])
            rs, rsb = rstd_ring.get()
            self.residual(hp, hb, n, vec, ytmp, ysq, y_bufs, ysq_bufs, ps_misc.get(), rs, rsb, tmp_ring, "sqrt")
            sch.dma("pool", dst[:, :, t0:t0 + n].rearrange("k p n -> p k n"), hp[:, :, 0:n], st_sems[idx % 2], reads=[hb])
        sch.barrier()

    def build(self):
        self.phase0()
        sub = 0
        for i in range(4):
            last = i == 3
            for s in range(3):
                if sub >= self.n_sub:
                    break
                src = self.xT if sub == 0 else self.outT
                if s == 0:
                    self.ffn_phase(i, 0, src, self.outT, list(range(NCH)))
                elif s == 2:
                    self.ffn_phase(i, 1, src, self.outT, list(range(1, NCH)) if last else list(range(NCH)))
                else:
                    self.mixer_phase(i, src, self.outT)
                sub += 1
        self.sch.emit()
        return self.nc


    def rope(self, pa, pab, pb_, pbb, M, n, tC, tS, tabb, dests, tmps, gains=None, rstd=None):
        (t1, t1b), (t2, t2b) = tmps
        if gains is None:
            self.tt("dve", t1[0:M, 0:n], pa[0:M, 0:n], tC[0:M, 0:n], ALU.mult, [pab, tabb], [t1b])
            self.tt("dve", t2[0:M, 0:n], pb_[0:M, 0:n], tS[0:M, 0:n], ALU.mult, [pbb, tabb], [t2b])
        else:
            ga, gb, gbuf = gains
            self.stt("dve", t1[0:M, 0:n], pa[0:M, 0:n], ga[0:M, :], tC[0:M, 0:n], ALU.mult, ALU.mult, [pab, tabb, gbuf], [t1b])
            self.stt("dve", t2[0:M, 0:n], pb_[0:M, 0:n], gb[0:M, :], tS[0:M, 0:n], ALU.mult, ALU.mult, [pbb, tabb, gbuf], [t2b])
        if rstd is None:
            for (r0, r1, dap, db) in dests:
                self.tt("pool", dap, t1[r0:r1, 0:n], t2[r0:r1, 0:n], ALU.add, [t1b, t2b], [db])
        else:
            rs, rsb = rstd
            self.tt("pool", t1[0:M, 0:n], t1[0:M, 0:n], t2[0:M, 0:n], ALU.add, [t2b], [t1b])
            for (r0, r1, dap, db) in dests:
                self.tt("dve", dap, t1[r0:r1, 0:n], rs[r0:r1, 0:n], ALU.mult, [t1b, rsb], [db])

    def load_const(self, q, dst, src, semkey, buf):
        self.sch.dma(q, dst, src, semkey, writes=[buf])

    def k_phase(self, L, src):
        sch, ar = self.sch, self.ar
        ar.reset()
        hp = ar.f32(8 * 512).rearrange("p (k n) -> p k n", k=8)
        hb = Buf()
        sqb = ar.bf16(8 * 512).rearrange("p (k n) -> p k n", k=8)
        sq_bufs = [Buf() for _ in range(8)]
        uT = ar.bf16(8 * 512).rearrange("p (k n) -> p k n", k=8)
        uT_bufs = [Buf() for _ in range(8)]
        rstd_ring = Ring([(ar.f32(512), Buf()) for _ in range(2)])
        tmp_ring = Ring([(ar.f32(512), Buf()) for _ in range(4)])
        tabC = ar.f32(512)
        tabS = ar.f32(512)
        tabb = Buf()
        kout_ring = Ring([(ar.bf16(512), Buf(), sch.dsem(3 + x)) for x in range(3)])
        VC = {0: 1024, 1: 256, 2: 1024, 3: 128}[L]
        dv = {0: 128, 1: 128, 2: 128, 3: 64}[L]
        nun = VC // dv
        vouts = [(ar.bf16(4 * VC).rearrange("p (t c) -> p t c", t=4), Buf(), sch.dsem(6 + x)) for x in range(2)]
        Vs = self.scr[f"V{L}"]
        wbuf = Buf()
        cvb = self.need(f"M{L}")
        sch.dsem(0); sch.dsem(1); sch.dsem(2)
        psr = self.ps_ring([0, 1, 2, 3, 4, 5, 6, 7])
        if L in (0, 1, 3):
            nkb = {0: 8, 1: 2, 3: 1}[L]
            nqb = 8
            wk = ar.bf16(2 * nkb * 1024).rearrange("p (b kc c) -> p b kc c", b=2 * nkb, kc=8)
            wsrc = self.wb[f"mx_qk_{L}"].rearrange("p (b kc c) -> p b kc c", kc=8, c=128)
            sch.dma("sp", wk, wsrc[:, 2 * nqb:2 * nqb + 2 * nkb, :, :], "d2", reads=[cvb], writes=[wbuf])
            wv = ar.bf16(8 * VC).rearrange("p (kc c) -> p kc c", kc=8)
            sch.dma("sp", wv, self.wb[f"mx_v_{L}"].rearrange("p (kc c) -> p kc c", kc=8), "d2", reads=[cvb], writes=[wbuf])
            KTs = self.scr[f"KT{L}"]
            if L == 1:
                g1 = ar.f32(4)
                sch.dma("sp", g1, self.din["sm_g_1"], "d2", reads=[cvb], writes=[wbuf])
                sqk_ring = Ring([(ar.bf16(512), Buf()) for _ in range(2)])
        else:
            win = ar.bf16(3 * 1024).rearrange("p (b kc c) -> p b kc c", b=3, kc=8)
            wsrc = self.wb["mx_in_2"].rearrange("p (b kc c) -> p b kc c", kc=8, c=128)
            sch.dma("sp", win, wsrc[:, 2:5, :, :], "d2", reads=[cvb], writes=[wbuf])
            wkn = ar.bf16(8 * 128).rearrange("p (b c) -> p b c", b=8)
            sch.dma("sp", wkn, self.wb["mx_kn_2"].rearrange("p (b c) -> p b c", b=8), "d2", reads=[cvb], writes=[wbuf])
            wuv = ar.bf16(1024)
            sch.dma("sp", wuv, self.wb["mx_uv_2"], "d2", reads=[cvb], writes=[wbuf])
            g2 = ar.f32(3)
            sch.dma("sp", g2, self.din["sm_g_2"], "d2", reads=[cvb], writes=[wbuf])
            sqk_ring = Ring([(ar.bf16(512), Buf()) for _ in range(2)])
            ckvn = ar.bf16(512)
            ckvn_b = Buf()
        rc = self.din[f"rp_c_{L}"]
        rs_ = self.din[f"rp_s_{L}"]
        for ci in range(NCH):
            t0, n = chunk_range(ci)
            w = 1 if ci == 0 else 0
            nt = n // 128
            tile0 = t0 // 128
            vec = self.VEC[:, L, 1, w]
            self.pump(2)
            sch.dma("pool", hp[:, :, 0:n], src[:, :, t0:t0 + n].rearrange("k p n -> p k n"), "d0", writes=[hb])
            sch.dma("pool", tabC[:, 0:n], rc[:, t0:t0 + n], "d1", writes=[tabb])
            sch.dma("pool", tabS[:, 0:n], rs_[:, t0:t0 + n], "d1", writes=[tabb])
            rsd, rsdb = rstd_ring.get()
            self.norm_mod(hp, hb, n, vec, sqb, sq_bufs, psr.get(), rsd, rsdb, tmp_ring, uT, uT_bufs, "lnexp")
            vout, vob, vsem = vouts[ci % 2]
            if L in (0, 1, 3):
                for b in range(nkb):
                    pa, pab = psr.get()
                    pb_, pbb = psr.get()
                    self.mm_group(pa[:, 0:n], [(wk[:, b, kc, :], uT[:, kc, 0:n]) for kc in range(8)], reads=[wbuf] + uT_bufs, writes=[pab])
                    self.mm_group(pb_[:, 0:n], [(wk[:, nkb + b, kc, :], uT[:, kc, 0:n]) for kc in range(8)], reads=[wbuf] + uT_bufs, writes=[pbb])
                    ko, kob, ksem = kout_ring.get()
                    gains = None
                    rstd = None
                    if L == 1:
                        sqk, sqkb = sqk_ring.get()
                        self.act(sqk[:, 0:n], pa[:, 0:n], AF.Square, [pab], [sqkb])
                        rk, rkb = rstd_ring.get()
                        self.rstd_from_sq([sqk[:, 0:n]], [sqkb], n, 128.0, psr.get(), rk[:, 0:n], rkb, "lnexp")
                        gains = (g1[:, 2:3], g1[:, 3:4], wbuf)
                        rstd = (rk, rkb)
                    self.rope(pa, pab, pb_, pbb, 128, n, tabC, tabS, tabb, [(0, 128, ko[:, 0:n], kob)], (tmp_ring.get(), tmp_ring.get()), gains, rstd)
                    sch.dma("pool", KTs[b * 128:(b + 1) * 128, t0:t0 + n], ko[:, 0:n], ksem, reads=[kob])
                for tt_ in range(nt):
                    for cg in range((VC + 511) // 512):
                        cw = min(512, VC - cg * 512)
                        pv, pvb = psr.get()
                        self.mm_group(pv[:, 0:cw], [(uT[:, kc, tt_ * 128:(tt_ + 1) * 128], wv[:, kc, cg * 512:cg * 512 + cw]) for kc in range(8)],
                                      reads=[wbuf] + uT_bufs, writes=[pvb])
                        self.act(vout[:, tt_, cg * 512:cg * 512 + cw], pv[:, 0:cw], AF.Copy, [pvb], [vob])
            else:
                pc, pcb = psr.get()
                self.mm_group(pc[:, 0:n], [(win[:, 0, kc, :], uT[:, kc, 0:n]) for kc in range(8)], reads=[wbuf] + uT_bufs, writes=[pcb])
                sqk, sqkb = sqk_ring.get()
                self.act(sqk[:, 0:n], pc[:, 0:n], AF.Square, [pcb], [sqkb])
                rk, rkb = rstd_ring.get()
                self.rstd_from_sq([sqk[:, 0:n]], [sqkb], n, 128.0, psr.get(), rk[:, 0:n], rkb, "lnexp")
                self.stt("dve", ckvn[:, 0:n], pc[:, 0:n], g2[:, 2:3], rk[:, 0:n], ALU.mult, ALU.mult, [pcb, rkb, wbuf], [ckvn_b])
                for jb in range(8):
                    pk, pkb = psr.get()
                    self.mm_group(pk[:, 0:n], [(wkn[:, jb, :], ckvn[:, 0:n])], reads=[wbuf, ckvn_b], writes=[pkb])
                    ko, kob, ksem = kout_ring.get()
                    self.act(ko[:, 0:n], pk[:, 0:n], AF.Copy, [pkb], [kob])
                    sch.dma("pool", self.scr["KN2"][jb * 128:(jb + 1) * 128, t0:t0 + n], ko[:, 0:n], ksem, reads=[kob])
                pa, pab = psr.get()
                pb_, pbb = psr.get()
                self.mm_group(pa[0:96, 0:n], [(win[:, 1, kc, 0:96], uT[:, kc, 0:n]) for kc in range(8)], reads=[wbuf] + uT_bufs, writes=[pab])
                self.mm_group(pb_[0:96, 0:n], [(win[:, 2, kc, 0:96], uT[:, kc, 0:n]) for kc in range(8)], reads=[wbuf] + uT_bufs, writes=[pbb])
                ko, kob, ksem = kout_ring.get()
                self.rope(pa, pab, pb_, pbb, 96, n, tabC, tabS, tabb, [(64, 96, ko[64:96, 0:n], kob)], (tmp_ring.get(), tmp_ring.get()))
                sch.dma("pool", self.scr["KR2"][:, t0:t0 + n], ko[64:96, 0:n], ksem, reads=[kob])
                for tt_ in range(nt):
                    for cg in range(2):
                        pv, pvb = psr.get()
                        self.mm_group(pv[:, 0:512], [(ckvn[:, tt_ * 128:(tt_ + 1) * 128], wuv[:, cg * 512:(cg + 1) * 512])], reads=[wbuf, ckvn_b], writes=[pvb])
                        self.act(vout[:, tt_, cg * 512:(cg + 1) * 512], pv[:, 0:512], AF.Copy, [pvb], [vob])
            for u in range(nun):
                vd = Vs[u].rearrange("p (t c) -> p t c", t=34)
                sch.dma("pool", vd[:, tile0:tile0 + nt, :], vout[:, 0:nt, u * dv:(u + 1) * dv], vsem, reads=[vob])
        sch.barrier()

    def run_jobs(self, jobs, scale, pt_ring, st_ring, kvs, mask, D=3):
        sch = self.sch
        nj = len(jobs)
        for idx in range(nj + D):
            if idx < nj:
                j = jobs[idx]
                if j.get("unit_first"):
                    slot = kvs.get(ahead=1)
                    assert slot is j["slot"]
                kt, lo, hi, moff = j["tile"]
                stp, stb = st_ring.get()
                sch.op("pe", lambda e, o=stp[:, lo:hi], l=j["K"][:, kt * 128:(kt + 1) * 128], r=j["Q"][:, lo:hi]: e.matmul(o, l, r, start=True, stop=True),
                       reads=[j["kb"], j["qb"]], writes=[stb])
                ptp, ptb = pt_ring.get()
                self.act(ptp[:, lo:hi], stp[:, lo:hi], AF.Exp, [stb], [ptb], scale=scale)
                if moff is not None:
                    mk, mkb = mask
                    self.tt("pool", ptp[:, lo:hi], ptp[:, lo:hi], mk[:, moff:moff + (hi - lo)], ALU.mult, [mkb], [ptb])
                j["pt"] = (ptp, ptb)
            k = idx - D
            if k >= 0:
                j = jobs[k]
                kt, lo, hi, moff = j["tile"]
                ptp, ptb = j["pt"]
                (otp, otb), (dnp, dnb) = j["acc"]
                first, last = j["first"], j["last"]
                sch.op("pe", lambda e, o=otp[:, lo:hi], l=j["V"][:, kt, :], r=ptp[:, lo:hi], first=first, last=last: e.matmul(o, l, r, start=first, stop=last),
                       reads=[ptb, j["kb"]], writes=[otb], signal=False)
                sch.op("pe", lambda e, o=dnp[:, lo:hi], r=ptp[:, lo:hi], first=first, last=last: e.matmul(o, self.ones, r, start=first, stop=last),
                       reads=[ptb, self.ones_b], writes=[dnb])
                if j.get("post") is not None:
                    j["post"]()
                if j.get("unit_last"):
                    kvs.prefetch(ahead=1)

    def q_phase(self, L, src, dst):
        sch, ar = self.sch, self.ar
        ar.reset()
        chunks = list(range(1, NCH)) if L == 3 else list(range(NCH))
        hp = ar.f32(8 * 512).rearrange("p (k n) -> p k n", k=8)
        hb = Buf()
        sqb = ar.bf16(8 * 512).rearrange("p (k n) -> p k n", k=8)
        sq_bufs = [Buf() for _ in range(8)]
        uT = ar.bf16(8 * 512).rearrange("p (k n) -> p k n", k=8)
        uT_bufs = [Buf() for _ in range(8)]
        rstd_ring = Ring([(ar.f32(512), Buf()) for _ in range(2)])
        tmp_ring = Ring([(ar.f32(512), Buf()) for _ in range(5)])
        tabC = ar.f32(512)
        tabS = ar.f32(512)
        tabb = Buf()
        wbuf = Buf()
        cvb = self.need(f"M{L}")
        for x in range(8):
            sch.dsem(x)
        wo = ar.bf16(8 * 8 * 128).rearrange("p (m ko c) -> p m ko c", m=8, ko=8)
        sch.dma("sp", wo, self.wb[f"mx_o_{L}"].rearrange("p (m ko c) -> p m ko c", m=8, ko=8), "d2", reads=[cvb], writes=[wbuf])
        nQ = {0: 16, 1: 8, 2: 16, 3: 16}[L]
        QT = ar.bf16(nQ * 512).rearrange("p (q n) -> p q n", q=nQ)
        QT_bufs = [Buf() for _ in range(nQ)]
        if L in (0, 3):
            sch.op("pool", lambda e: e.memset(QT, 0.0), writes=QT_bufs)
        if L in (0, 1, 3):
            wq = ar.bf16(16 * 1024).rearrange("p (b kc c) -> p b kc c", b=16, kc=8)
            wsrc = self.wb[f"mx_qk_{L}"].rearrange("p (b kc c) -> p b kc c", kc=8, c=128)
            sch.dma("sp", wq, wsrc[:, 0:16, :, :], "d2", reads=[cvb], writes=[wbuf])
        else:
            win = ar.bf16(2 * 1024).rearrange("p (b kc c) -> p b kc c", b=2, kc=8)
            wsrc = self.wb["mx_in_2"].rearrange("p (b kc c) -> p b kc c", kc=8, c=128)
            sch.dma("sp", win, wsrc[:, 0:2, :, :], "d2", reads=[cvb], writes=[wbuf])
            wuq = ar.bf16(32 * 256).rearrange("p (b kc c) -> p b kc c", b=32, kc=2)
            sch.dma("sp", wuq, self.wb["mx_uq_2"].rearrange("p (b kc c) -> p b kc c", b=32, kc=2), "d2", reads=[cvb], writes=[wbuf])
            g2 = ar.f32(3)
            sch.dma("sp", g2, self.din["sm_g_2"], "d2", reads=[cvb], writes=[wbuf])
            cqn = ar.bf16(2 * 512).rearrange("p (c n) -> p c n", c=2)
            cqn_b = Buf()
        sqk_ring = Ring([(ar.bf16(512), Buf()) for _ in range(2)])
        if L == 1:
            g1 = ar.f32(4)
            sch.dma("sp", g1, self.din["sm_g_1"], "d2", reads=[cvb], writes=[wbuf])
        mask = None
        if L == 3:
            mk = ar.bf16(384)
            mkb = Buf()
            sch.dma("pool", mk, self.din["sm_mask_3"], "d2", reads=[cvb], writes=[wbuf])
            mask = (mk, wbuf)
            esink = ar.f32(16)
            esb = Buf()
            sch.dma("sp", esink, self.din["sm_sink_3"], "d2", reads=[cvb], writes=[wbuf])
            self.act(esink, esink, AF.Exp, [wbuf], [esb])
        if L == 0:
            lam = ar.f32(256)
            lamb = Buf()
            sch.dma("sp", lam, self.din["sm_lam_0"], "d2", reads=[cvb], writes=[wbuf])
            subg = ar.f32(1)
            sch.dma("sp", subg, self.din["sm_subln_0"], "d2", reads=[cvb], writes=[wbuf])
            pp = ar.f32(128)
            sc2 = ar.f32(4)
            self.tt("dve", pp[:, 0:64], lam[:, 0:64], lam[:, 64:128], ALU.mult, [wbuf], [lamb])
            self.tt("dve", pp[:, 64:128], lam[:, 128:192], lam[:, 192:256], ALU.mult, [lamb], [lamb])
            sch.op("dve", lambda e: e.reduce_sum(out=sc2[:, 0:1], in_=pp[:, 0:64], axis=mybir.AxisListType.X), reads=[lamb], writes=[lamb])
            sch.op("dve", lambda e: e.reduce_sum(out=sc2[:, 1:2], in_=pp[:, 64:128], axis=mybir.AxisListType.X), reads=[lamb], writes=[lamb])
            self.act(sc2[:, 0:2], sc2[:, 0:2], AF.Exp, [lamb], [lamb])
            self.tt("dve", sc2[:, 2:3], sc2[:, 1:2], sc2[:, 0:1], ALU.subtract, [lamb], [lamb])
            self.ts("dve", sc2[:, 2:3], sc2[:, 2:3], -LAMBDA_INIT0, None, ALU.add, None, [lamb], [lamb])
            self.ts("dve", subg, subg, 1.0 - LAMBDA_INIT0, None, ALU.mult, None, [lamb], [lamb])
            neglam = sc2[:, 2:3]
        nslot = 2
        kv_slots = []
        for x in range(nslot):
            if L == 2:
                item = {"KA": ar.bf16(T), "KB": ar.bf16(T), "V": ar.bf16(34 * 128).rearrange("p (t c) -> p t c", t=34)}
            else:
                item = {"K": ar.bf16(T), "V": ar.bf16(34 * 128).rearrange("p (t c) -> p t c", t=34)}
            kv_slots.append((item, Buf()))
        kvs = Stream(sch, "sp", kv_slots, [sch.dsem(3), sch.dsem(4)])
        nunit = {0: 8, 1: 2, 2: 8, 3: 2}[L]

        def tiles_for(ci):
            t0, n = chunk_range(ci)
            if ci == 0:
                return [(0, 0, n, None), (1, 0, n, None)]
            if L != 3:
                return [(kt, 0, n, None) for kt in range(34)]
            cs = t0 - CTX
            out = [(0, 0, n, None), (1, 0, n, None)]
            for ktl in range(32):
                koff = ktl * 128 - cs
                lo = max(0, koff - 128)
                hi = min(512, koff + 256)
                if hi <= lo:
                    continue
                out.append((2 + ktl, lo, hi, lo - (koff - 128)))
            return out

        def add_fill(ci, u):
            tl = tiles_for(ci)
            kts = sorted(set(t[0] for t in tl))
            runs = []
            for kt in kts:
                if runs and runs[-1][1] == kt:
                    runs[-1][1] = kt + 1
                else:
                    runs.append([kt, kt + 1])

            def fn(item, runs=runs, u=u):
                out = []
                for a, b_ in runs:
                    ca, cb = a * 128, b_ * 128
                    if L in (0, 1):
                        out.append((item["K"][:, ca:cb], self.scr[f"KT{L}"][u * 128:(u + 1) * 128, ca:cb]))
                        vd = self.scr[f"V{L}"][u].rearrange("p (t c) -> p t c", t=34)
                        out.append((item["V"][:, a:b_, :], vd[:, a:b_, :]))
                    elif L == 2:
                        out.append((item["KA"][0:64, ca:cb], self.scr["KN2"][(2 * u) * 64:(2 * u) * 64 + 64, ca:cb]))
                        out.append((item["KA"][64:96, ca:cb], self.scr["KR2"][:, ca:cb]))
                        out.append((item["KB"][0:64, ca:cb], self.scr["KN2"][(2 * u + 1) * 64:(2 * u + 1) * 64 + 64, ca:cb]))
                        out.append((item["KB"][64:96, ca:cb], self.scr["KR2"][:, ca:cb]))
                        vd = self.scr["V2"][u].rearrange("p (t c) -> p t c", t=34)
                        out.append((item["V"][:, a:b_, :], vd[:, a:b_, :]))
                    else:
                        out.append((item["K"][0:64, ca:cb], self.scr["KT3"][u * 64:(u + 1) * 64, ca:cb]))
                        out.append((item["K"][64:128, ca:cb], self.scr["KT3"][u * 64:(u + 1) * 64, ca:cb]))
                        vd = self.scr["V3"][u].rearrange("p (t c) -> p t c", t=34)
                        out.append((item["V"][:, a:b_, 0:64], vd[:, a:b_, :]))
                        out.append((item["V"][:, a:b_, 64:128], vd[:, a:b_, :]))
                return out
            kvs.add(fn)

        for ci in chunks:
            for u in range(nunit):
                add_fill(ci, u)
        pt_ring = Ring([(ar.bf16(512), Buf()) for _ in range(6)])
        OTall = ar.bf16(8 * 512).rearrange("p (k n) -> p k n", k=8)
        OT_bufs = [Buf() for _ in range(8)]
        ytmp = ar.f32(8 * 512).rearrange("p (k n) -> p k n", k=8)
        y_bufs = [Buf() for _ in range(8)]
        ysq, ysq_bufs = sqb, sq_bufs
        st_ring = self.ps_ring([0, 1, 6, 7])
        aux_ring = self.ps_ring([0, 1])
        acc_ring = Ring([(self.psum[2], self.psum[3]), (self.psum[4], self.psum[5])])
        misc = self.ps_ring([6, 7, 2, 3, 4, 5])
        scale = {0: 64 ** -0.5, 1: 128 ** -0.5, 2: 96 ** -0.5, 3: 64 ** -0.5}[L]
        rc = self.din[f"rp_c_{L}"]
        rs_ = self.din[f"rp_s_{L}"]

        for ci in chunks:
            t0, n = chunk_range(ci)
            w = 1 if ci == 0 else 0
            vec = self.VEC[:, L, 1, w]
            tiles = tiles_for(ci)
            self.pump(2)
            sch.dma("pool", hp[:, :, 0:n], src[:, :, t0:t0 + n].rearrange("k p n -> p k n"), "d0", writes=[hb])
            sch.dma("pool", tabC[:, 0:n], rc[:, t0:t0 + n], "d1", writes=[tabb])
            sch.dma("pool", tabS[:, 0:n], rs_[:, t0:t0 + n], "d1", writes=[tabb])
            kvs.prefetch(ahead=2)
            rsd, rsdb = rstd_ring.get()
            self.norm_mod(hp, hb, n, vec, sqb, sq_bufs, aux_ring.get(), rsd, rsdb, tmp_ring, uT, uT_bufs, "lnexp")
            if L in (0, 1, 3):
                for b in range(8):
                    pa, pab = misc.get()
                    pb_, pbb = misc.get()
                    self.mm_group(pa[:, 0:n], [(wq[:, b, kc, :], uT[:, kc, 0:n]) for kc in range(8)], reads=[wbuf] + uT_bufs, writes=[pab])
                    self.mm_group(pb_[:, 0:n], [(wq[:, 8 + b, kc, :], uT[:, kc, 0:n]) for kc in range(8)], reads=[wbuf] + uT_bufs, writes=[pbb])
                    if L == 1:
                        sqk, sqkb = sqk_ring.get()
                        self.act(sqk[:, 0:n], pa[:, 0:n], AF.Square, [pab], [sqkb])
                        rk, rkb = rstd_ring.get()
                        self.rstd_from_sq([sqk[:, 0:n]], [sqkb], n, 128.0, aux_ring.get(), rk[:, 0:n], rkb, "lnexp")
                        self.rope(pa, pab, pb_, pbb, 128, n, tabC, tabS, tabb, [(0, 128, QT[:, b, 0:n], QT_bufs[b])],
                                  (tmp_ring.get(), tmp_ring.get()), (g1[:, 0:1], g1[:, 1:2], wbuf), (rk, rkb))
                    else:
                        self.rope(pa, pab, pb_, pbb, 128, n, tabC, tabS, tabb,
                                  [(0, 64, QT[0:64, 2 * b, 0:n], QT_bufs[2 * b]), (64, 128, QT[64:128, 2 * b + 1, 0:n], QT_bufs[2 * b + 1])],
                                  (tmp_ring.get(), tmp_ring.get()))
            else:
                pcs = []
                sqs = []
                for c in range(2):
                    pc, pcb = misc.get()
                    self.mm_group(pc[:, 0:n], [(win[:, c, kc, :], uT[:, kc, 0:n]) for kc in range(8)], reads=[wbuf] + uT_bufs, writes=[pcb])
                    sqk, sqkb = sqk_ring.get()
                    self.act(sqk[:, 0:n], pc[:, 0:n], AF.Square, [pcb], [sqkb])
                    pcs.append((pc, pcb))
                    sqs.append((sqk, sqkb))
                rk, rkb = rstd_ring.get()
                self.rstd_from_sq([q[0][:, 0:n] for q in sqs], [q[1] for q in sqs], n, 256.0, aux_ring.get(), rk[:, 0:n], rkb, "lnexp")
                for c in range(2):
                    self.stt("dve", cqn[:, c, 0:n], pcs[c][0][:, 0:n], g2[:, c:c + 1], rk[:, 0:n], ALU.mult, ALU.mult, [pcs[c][1], rkb, wbuf], [cqn_b])
                for h in range(16):
                    pa, pab = misc.get()
                    pb_, pbb = misc.get()
                    self.mm_group(pa[0:96, 0:n], [(wuq[:, h, c, 0:96], cqn[:, c, 0:n]) for c in range(2)], reads=[wbuf, cqn_b], writes=[pab])
                    self.mm_group(pb_[0:96, 0:n], [(wuq[:, 16 + h, c, 0:96], cqn[:, c, 0:n]) for c in range(2)], reads=[wbuf, cqn_b], writes=[pbb])
                    self.rope(pa, pab, pb_, pbb, 96, n, tabC, tabS, tabb, [(0, 96, QT[0:96, h, 0:n], QT_bufs[h])], (tmp_ring.get(), tmp_ring.get()))
            jobs = []
            n_ = n

            def add_map(slot, Kap, Qap, qb, acc, post, ufirst, ulast):
                item, kvb = slot
                for ti, tl in enumerate(tiles):
                    jobs.append({"slot": slot, "K": Kap, "kb": kvb, "Q": Qap, "qb": qb, "V": item["V"], "tile": tl, "acc": acc,
                                 "first": ti == 0, "last": ti == len(tiles) - 1,
                                 "unit_first": ufirst and ti == 0, "unit_last": ulast and ti == len(tiles) - 1,
                                 "post": post if ti == len(tiles) - 1 else None})

            def simple_post(acc, r0, r1, ko, sink_h):
                def f():
                    (ot, otb), (dn, dnb) = acc
                    r_, rb_ = tmp_ring.get()
                    if sink_h is not None:
                        self.ts("dve", r_[r0:r1, 0:n_], dn[r0:r1, 0:n_], esink[r0:r1, sink_h:sink_h + 1], None, ALU.add, None, [dnb, esb], [rb_])
                        sch.op("dve", lambda e, o=r_[r0:r1, 0:n_]: e.reciprocal(out=o, in_=o), reads=[rb_], writes=[rb_])
                    else:
                        sch.op("dve", lambda e, o=r_[r0:r1, 0:n_], i_=dn[r0:r1, 0:n_]: e.reciprocal(out=o, in_=i_), reads=[dnb], writes=[rb_])
                    self.tt("dve", OTall[r0:r1, ko, 0:n_], ot[r0:r1, 0:n_], r_[r0:r1, 0:n_], ALU.mult, [otb, rb_], [OT_bufs[ko]])
                return f

            def da_post(acc, h, which, store):
                def f():
                    (ot, otb), (dn, dnb) = acc
                    r_, rb_ = tmp_ring.get()
                    sch.op("dve", lambda e, o=r_[:, 0:n_], i_=dn[:, 0:n_]: e.reciprocal(out=o, in_=i_), reads=[dnb], writes=[rb_])
                    self.tt("dve", r_[:, 0:n_], ot[:, 0:n_], r_[:, 0:n_], ALU.mult, [otb], [rb_])
                    store[which] = (r_, rb_)
                    if which == 1:
                        (t0_, t0b), (t1_, t1b) = store[0], store[1]
                        self.stt("dve", t0_[:, 0:n_], t1_[:, 0:n_], neglam, t0_[:, 0:n_], ALU.mult, ALU.add, [t1b, lamb], [t0b])
                        sqk, sqkb = sqk_ring.get()
                        self.act(sqk[:, 0:n_], t0_[:, 0:n_], AF.Square, [t0b], [sqkb])
                        rk, rkb = rstd_ring.get()
                        self.rstd_from_sq([sqk[:, 0:n_]], [sqkb], n_, 128.0, st_ring.get(), rk[:, 0:n_], rkb, "lnexp")
                        self.stt("dve", OTall[:, h, 0:n_], t0_[:, 0:n_], subg[:, 0:1], rk[:, 0:n_], ALU.mult, ALU.mult, [t0b, rkb, lamb], [OT_bufs[h]])
                return f

            for u in range(nunit):
                slot = kvs.peek(u)
                item = slot[0]
                if L == 0:
                    h = u
                    store = {}
                    acc0 = acc_ring.get()
                    acc1 = acc_ring.get()
                    add_map(slot, item["K"], QT[:, 2 * h, :], QT_bufs[2 * h], acc0, da_post(acc0, h, 0, store), True, False)
                    add_map(slot, item["K"], QT[:, 2 * h + 1, :], QT_bufs[2 * h + 1], acc1, da_post(acc1, h, 1, store), False, True)
                else:
                    if L == 1:
                        maps = [(item["K"], QT[:, 4 * u + x, :], QT_bufs[4 * u + x], 0, 128, 4 * u + x, None) for x in range(4)]
                    elif L == 2:
                        maps = [(item["KA"][0:96, :], QT[0:96, 2 * u, :], QT_bufs[2 * u], 0, 64, u, None),
                                (item["KB"][0:96, :], QT[0:96, 2 * u + 1, :], QT_bufs[2 * u + 1], 64, 128, u, None)]
                    else:
                        maps = []
                        for x in range(8):
                            hh = 8 * u + x
                            maps.append((item["K"], QT[:, hh, :], QT_bufs[hh], (hh % 2) * 64, (hh % 2) * 64 + 64, hh // 2, hh))
                    for mi, (Kap, Qap, qb, r0, r1, ko, sink_h) in enumerate(maps):
                        acc = acc_ring.get()
                        add_map(slot, Kap, Qap, qb, acc, simple_post(acc, r0, r1, ko, sink_h), mi == 0, mi == len(maps) - 1)
            self.run_jobs(jobs, scale, pt_ring, st_ring, kvs, mask)
            for m in range(8):
                yp, yb = misc.get()
                self.mm_group(yp[:, 0:n], [(wo[:, m, ko, :], OTall[:, ko, 0:n]) for ko in range(8)], reads=[wbuf] + OT_bufs, writes=[yb])
                self.act(ytmp[:, m, 0:n], yp[:, 0:n], AF.Copy, [yb], [y_bufs[m]])
                self.tt("dve", ysq[:, m, 0:n], yp[:, 0:n], ytmp[:, m, 0:n], ALU.mult, [yb, y_bufs[m]], [ysq_bufs[m]])
            rsd, rsdb = rstd_ring.get()
            self.residual(hp, hb, n, vec, ytmp, ysq, y_bufs, ysq_bufs, aux_ring.get(), rsd, rsdb, tmp_ring, "lnexp")
            sch.dma("pool", dst[:, :, t0:t0 + n].rearrange("k p n -> p k n"), hp[:, :, 0:n], "d5", reads=[hb])
        sch.barrier()

    def mixer_phase(self, i, src, dst):
        self.k_phase(i, src)
        self.q_phase(i, src, dst)


def run(inp, n_sub=12, cores=NCORES):
    shared = prep_shared(inp)
    prog = Prog(n_sub, {k: v.shape for k, v in shared.items()})
    nc = prog.build()
    in_maps = []
    for b in range(cores):
        m = dict(shared)
        m.update(prep_core(inp, b))
        in_maps.append(m)
    res = run_bass_kernel_spmd(nc, in_maps, core_ids=list(range(cores)))
    return [r["outT"] for r in res.results]


def kernel(**inputs):
    outs = run(inputs, 12, NCORES)
    full = np.stack([np.ascontiguousarray(o.reshape(D, T)[:, CTX:].T) for o in outs], axis=0)
    return full.astype(np.float32)
```

```python
import math
import numpy as np
from contextlib import ExitStack
import concourse.bass as bass
import concourse.mybir as mybir
from concourse.bass_utils import run_bass_kernel_spmd

F32 = mybir.dt.float32
BF16 = mybir.dt.bfloat16
AF = mybir.ActivationFunctionType
ALU = mybir.AluOpType

D = 1024
KC = 8
CTX = 256
S = 4096
T = CTX + S
DFF = 2816
NJ = 22
EPS = 1e-6
NCH = 9
NCORES = 8


def chunk_range(ci):
    if ci == 0:
        return 0, CTX
    return CTX + 512 * (ci - 1), 512


class Buf:
    __slots__ = ("w", "r", "excl")

    def __init__(self, excl=False):
        self.w = None
        self.r = {}
        self.excl = excl


class Sched:
    ENGS = ("pe", "act", "dve", "pool", "sp")

    def __init__(self, nc, st):
        self.nc = nc
        self.st = st
        self.prog = {e: [] for e in self.ENGS}
        self.sems = {}
        self.cnt = {}
        self.seen = {e: {} for e in self.ENGS}
        for e in ("pe", "act", "dve", "pool"):
            self.new_sem(e)

    def new_sem(self, key):
        if key in self.sems:
            return key
        self.sems[key] = self.st.enter_context(self.nc.semaphore(key))
        self.cnt[key] = 0
        return key

    def dsem(self, idx):
        return self.new_sem(f"d{idx}")

    def _waits(self, e, reads, writes):
        need = {}
        for b in reads:
            if b.w is not None:
                k, v = b.w
                if v > need.get(k, 0):
                    need[k] = v
            if b.excl:
                for k, v in b.r.items():
                    if k != e and v > need.get(k, 0):
                        need[k] = v
        for b in writes:
            if b.w is not None:
                k, v = b.w
                if v > need.get(k, 0):
                    need[k] = v
            for k, v in b.r.items():
                if v > need.get(k, 0):
                    need[k] = v
        out = []
        seen = self.seen[e]
        for k, v in need.items():
            if k == e and e == "pe":
                continue
            if seen.get(k, 0) >= v:
                continue
            seen[k] = v
            out.append((self.sems[k], v))
        return out

    def op(self, e, fn, reads=(), writes=(), signal=True):
        waits = self._waits(e, reads, writes)
        sem = self.sems[e]
        if signal:
            self.cnt[e] += 1
            tv = self.cnt[e]
        else:
            tv = self.cnt[e] + 1
        self.prog[e].append((waits, fn, sem if signal else None, 1))
        for b in reads:
            if tv > b.r.get(e, 0):
                b.r[e] = tv
        for b in writes:
            b.w = (e, tv)
            b.r = {}

    def dma(self, q, out, in_, semkey, reads=(), writes=()):
        waits = self._waits(q, reads, writes)
        self.cnt[semkey] += 16
        tv = self.cnt[semkey]
        self.prog[q].append((waits, lambda e, out=out, in_=in_: e.dma_start(out=out, in_=in_), self.sems[semkey], 16))
        for b in reads:
            if tv > b.r.get(semkey, 0):
                b.r[semkey] = tv
        for b in writes:
            b.w = (semkey, tv)
            b.r = {}

    def barrier(self):
        for e in self.ENGS:
            waits = []
            for k, h in self.sems.items():
                v = self.cnt[k]
                if v == 0 or self.seen[e].get(k, 0) >= v or k.startswith("cv"):
                    continue
                if k == e and e == "pe":
                    continue
                self.seen[e][k] = v
                waits.append((h, v))
            if waits:
                self.prog[e].append((waits, None, None, 0))

    def emit(self):
        nc = self.nc
        with nc.Block() as block:
            def run(eng, lst):
                for waits, fn, sem, inc in lst:
                    for h, v in waits:
                        eng.wait_ge(h, v)
                    if fn is not None:
                        ins = fn(eng)
                        if sem is not None:
                            ins.then_inc(sem, inc)

            @block.tensor
            def _(e):
                run(e, self.prog["pe"])

            @block.scalar
            def _(e):
                run(e, self.prog["act"])

            @block.vector
            def _(e):
                run(e, self.prog["dve"])

            @block.gpsimd
            def _(e):
                run(e, self.prog["pool"])

            @block.sync
            def _(e):
                run(e, self.prog["sp"])


class Arena:
    def __init__(self, ap_f32, nwords):
        self.ap = ap_f32
        self.n = nwords
        self.off = 0
        self.base = 0

    def set_base(self):
        self.base = self.off

    def reset(self):
        self.off = self.base

    def f32(self, n):
        a = self.off
        self.off += n
        assert self.off <= self.n, ("arena overflow", self.off, self.n)
        return self.ap[:, a:a + n]

    def bf16(self, n):
        w = (n + 1) // 2
        return self.f32(w).bitcast(BF16)[:, 0:n]


class Ring:
    def __init__(self, items):
        self.items = items
        self.i = 0

    def get(self):
        it = self.items[self.i % len(self.items)]
        self.i += 1
        return it


class Stream:
    def __init__(self, sch, q, slots, semkeys, extra_reads=()):
        self.sch = sch
        self.extra_reads = list(extra_reads)
        self.q = q
        self.slots = slots
        self.sems = semkeys
        self.fills = []
        self.issued = 0
        self.taken = 0

    def add(self, fn):
        self.fills.append(fn)

    def _issue(self):
        i = self.issued
        ap, buf = self.slots[i % len(self.slots)]
        for (o, in_) in self.fills[i](ap):
            self.sch.dma(self.q, o, in_, self.sems[i % len(self.slots)], reads=self.extra_reads, writes=[buf])
        self.issued += 1

    def get(self, ahead=None):
        i = self.taken
        if ahead is None:
            ahead = len(self.slots)
        while self.issued < min(len(self.fills), i + ahead):
            self._issue()
        self.taken += 1
        return self.slots[i % len(self.slots)]

    def peek(self, off=0):
        return self.slots[(self.taken + off) % len(self.slots)]

    def prefetch(self, ahead=None):
        if ahead is None:
            ahead = len(self.slots)
        while self.issued < min(len(self.fills), self.taken + ahead):
            self._issue()


def fmv(v):
    v = np.asarray(v, np.float32)
    return np.ascontiguousarray(v.reshape(-1, 128).T)


def lhs_blocks(W, col_lists, kc):
    W = np.asarray(W, np.float32)
    out = np.zeros((128, len(col_lists), kc, 128), np.float32)
    Wr = W.reshape(kc, 128, W.shape[1])
    for b, cols in enumerate(col_lists):
        out[:, b, :, :len(cols)] = Wr[:, :, cols].transpose(1, 0, 2)
    return out.reshape(128, -1)


def rhs_fmt(W, cols, kc):
    W = np.asarray(W, np.float32)
    Wr = W.reshape(kc, 128, W.shape[1])[:, :, cols]
    return np.ascontiguousarray(Wr.transpose(1, 0, 2)).reshape(128, -1)


def swap_pairs(cols):
    c = np.asarray(cols).reshape(-1, 2)[:, ::-1].reshape(-1)
    return c


def rope_tables(rot_dim, row_dims):
    n_axis = rot_dim // 4
    pos = np.arange(S)
    row = (pos // 64).astype(np.float32)
    col = (pos % 64).astype(np.float32)
    freqs = (np.float32(10000.0) ** (-np.arange(n_axis, dtype=np.float32) / np.float32(n_axis))).astype(np.float32)
    ang = np.concatenate([row[:, None] * freqs, col[:, None] * freqs], axis=-1).astype(np.float32)
    cos = np.cos(ang).astype(np.float32)
    sin = np.sin(ang).astype(np.float32)
    C = np.ones((128, T), np.float32)
    Sg = np.zeros((128, T), np.float32)
    for p in range(128):
        d = row_dims[p]
        if d < 0:
            continue
        j = d // 2
        C[p, CTX:] = cos[:, j]
        Sg[p, CTX:] = -sin[:, j] if d % 2 == 0 else sin[:, j]
    return C, Sg


LAMBDA_INIT0 = 0.8 - 0.6 * math.exp(-0.3 * 0)


def prep_shared(inp):
    sh = {}
    sh["ada_w"] = np.ascontiguousarray(inp["ada_w"], np.float32)
    sh["ada_b"] = np.ascontiguousarray(inp["ada_b"], np.float32)
    sh["sm_i2"] = np.eye(2, dtype=np.float32)
    sw = np.zeros((128, 128), np.float32)
    sw[(np.arange(128) + 64) % 128, np.arange(128)] = 1.0
    sh["sm_sw"] = sw
    g = np.asarray(inp["norm_g"], np.float32).reshape(4, 6, 8, 128)
    sh["gT"] = np.ascontiguousarray(g.transpose(3, 0, 1, 2)).reshape(128, 4 * 6 * 8)
    wi = np.asarray(inp["ffn_w_in"], np.float32)
    wo = np.asarray(inp["ffn_w_out"], np.float32)
    for i in range(4):
        for s in range(2):
            W = wi[i, s]
            Wg = W[:, :DFF].reshape(8, 128, NJ, 128)
            Wu = W[:, DFF:].reshape(8, 128, NJ, 128)
            Wc = np.concatenate([Wg, Wu], axis=-1)
            sh[f"win_{i}_{s}"] = np.ascontiguousarray(Wc.transpose(1, 2, 0, 3)).reshape(128, NJ * 8 * 256)
            Wo = wo[i, s].reshape(NJ, 128, 8, 128)
            sh[f"wout_{i}_{s}"] = np.ascontiguousarray(Wo.transpose(1, 2, 0, 3)).reshape(128, 8 * NJ * 128)
    prep_mixers(inp, sh)
    return sh


def wo_fmt(W):
    W = np.asarray(W, np.float32).reshape(8, 128, 8, 128)
    return np.ascontiguousarray(W.transpose(1, 2, 0, 3)).reshape(128, -1)


def prep_mixers(inp, sh):
    ar = np.arange
    Wi = inp["da_w_in"][0]
    qc = [h * 128 + ar(128) for h in range(8)]
    kc_ = [1024 + h * 128 + ar(128) for h in range(8)]
    sh["mx_qk_0"] = lhs_blocks(Wi, qc + [swap_pairs(c) for c in qc] + kc_ + [swap_pairs(c) for c in kc_], 8)
    sh["mx_v_0"] = rhs_fmt(Wi, 2048 + ar(1024), 8)
    sh["mx_o_0"] = wo_fmt(inp["da_w_out"][0])
    sh["sm_lam_0"] = np.ascontiguousarray(np.broadcast_to(np.asarray(inp["da_lambda"][0], np.float32).reshape(1, 256), (128, 256)))
    sh["sm_subln_0"] = np.asarray(inp["da_subln"][0], np.float32).reshape(128, 1).copy()
    Wi = inp["ga_w_in"][0]
    qc = [h * 128 + ar(128) for h in range(8)]
    kc_ = [1024 + g * 128 + ar(128) for g in range(2)]
    sh["mx_qk_1"] = lhs_blocks(Wi, qc + [swap_pairs(c) for c in qc] + kc_ + [swap_pairs(c) for c in kc_], 8)
    sh["mx_v_1"] = rhs_fmt(Wi, 1280 + ar(256), 8)
    sh["mx_o_1"] = wo_fmt(inp["ga_w_out"][0])
    gq = np.asarray(inp["ga_q_norm"][0], np.float32)
    gk = np.asarray(inp["ga_k_norm"][0], np.float32)
    sp = swap_pairs(ar(128))
    sh["sm_g_1"] = np.ascontiguousarray(np.stack([gq, gq[sp], gk, gk[sp]], axis=1))
    Wi = inp["mla_w_in"][0]
    pad = np.full(64, 384)
    sh["mx_in_2"] = lhs_blocks(Wi, [ar(128), 128 + ar(128), 256 + ar(128), np.concatenate([pad, 384 + ar(32)]), np.concatenate([pad, 384 + swap_pairs(ar(32))])], 8)
    Wq = inp["mla_w_uq"][0]
    qc = [h * 96 + ar(96) for h in range(16)]
    qs = [h * 96 + np.concatenate([ar(64), 64 + swap_pairs(ar(32))]) for h in range(16)]
    sh["mx_uq_2"] = lhs_blocks(Wq, qc + qs, 2)
    Wkv = inp["mla_w_ukv"][0]
    sh["mx_kn_2"] = lhs_blocks(Wkv, [np.concatenate([(2 * jb) * 128 + ar(64), (2 * jb + 1) * 128 + ar(64)]) for jb in range(8)], 1)
    sh["mx_uv_2"] = rhs_fmt(Wkv, np.concatenate([h * 128 + 64 + ar(64) for h in range(16)]), 1)
    sh["mx_o_2"] = wo_fmt(inp["mla_w_out"][0])
    gq = np.asarray(inp["mla_q_norm"][0], np.float32)
    gkv = np.asarray(inp["mla_kv_norm"][0], np.float32)
    sh["sm_g_2"] = np.ascontiguousarray(np.stack([gq[0:128], gq[128:256], gkv], axis=1))
    Wi = inp["swa_w_in"][0]
    qc = [jb * 128 + ar(128) for jb in range(8)]
    kc_ = [1024 + ar(128)]
    sh["mx_qk_3"] = lhs_blocks(Wi, qc + [swap_pairs(c) for c in qc] + kc_ + [swap_pairs(c) for c in kc_], 8)
    sh["mx_v_3"] = rhs_fmt(Wi, 1152 + ar(128), 8)
    sh["mx_o_3"] = wo_fmt(inp["swa_w_out"][0])
    sh["sm_sink_3"] = np.ascontiguousarray(np.broadcast_to(np.asarray(inp["swa_sink"][0], np.float32).reshape(1, 16), (128, 16)))
    kk = ar(128)[:, None]
    qq = ar(384)[None, :]
    sh["sm_mask_3"] = ((qq >= kk) & (qq <= kk + 256)).astype(np.float32)
    p = ar(128)
    rd2 = np.full(128, -1)
    rd2[64:96] = ar(32)
    for L, (rot, rd) in enumerate([(64, p % 64), (128, p), (32, rd2), (64, p % 64)]):
        C, Sg = rope_tables(rot, rd)
        sh[f"rp_c_{L}"] = C
        sh[f"rp_s_{L}"] = Sg


def prep_core(inp, b):
    pc = {}
    h0 = np.concatenate([np.asarray(inp["ctx"][b], np.float32), np.asarray(inp["x"][b], np.float32)], axis=0)
    pc["xT"] = np.ascontiguousarray(h0.T).reshape(8, 128, T)
    pc["cT"] = np.ascontiguousarray(np.stack([fmv(inp["c"][b]), fmv(inp["c_ctx"])], axis=-1)).reshape(128, 16)
    return pc


class Prog:
    def __init__(self, n_sub, shared_shapes):
        self.n_sub = n_sub
        nc = self.nc = bass.Bass("TRN2", target_bir_lowering=False)
        self.st = ExitStack()
        st = self.st
        self.din = {}
        for name, shp in shared_shapes.items():
            self.din[name] = nc.dram_tensor(name, list(shp), F32, kind="ExternalInput").ap()
        self.xT = nc.dram_tensor("xT", [8, 128, T], F32, kind="ExternalInput").ap()
        self.cT = nc.dram_tensor("cT", [128, 16], F32, kind="ExternalInput").ap()
        self.outT = nc.dram_tensor("outT", [8, 128, T], F32, kind="ExternalOutput").ap()
        self.wb = {}
        for name, shp in shared_shapes.items():
            if name.startswith("win_") or name.startswith("wout_") or name.startswith("mx_"):
                self.wb[name] = nc.dram_tensor(name + "_b", list(shp), BF16, kind="Internal").ap()
        self.scr = {}
        for name, shp in (("KT0", [1024, T]), ("V0", [8, 128, 34 * 128]), ("KT1", [256, T]), ("V1", [2, 128, 34 * 128]),
                          ("KN2", [1024, T]), ("KR2", [32, T]), ("V2", [8, 128, 34 * 128]),
                          ("KT3", [128, T]), ("V3", [2, 128, 34 * 64])):
            self.scr[name] = nc.dram_tensor("scr_" + name, shp, BF16, kind="Internal").ap()
        self.sch = Sched(nc, st)
        NW = 50 * 1024
        self.arena_t = st.enter_context(nc.sbuf_tensor("arena", [128, NW], F32))
        self.ar = Arena(self.arena_t[:, :], NW)
        self.psum = []
        for i in range(8):
            t = st.enter_context(nc.psum_tensor(f"ps{i}", [128, 512], F32))
            self.psum.append((t, Buf(excl=True)))

    def ps_ring(self, idxs):
        return Ring([self.psum[i] for i in idxs])

    def mm_group(self, out_ap, pairs, reads, writes):
        n = len(pairs)
        for i, (l, r) in enumerate(pairs):
            last = i == n - 1
            self.sch.op("pe", lambda e, l=l, r=r, i=i, last=last: e.matmul(out_ap, l, r, start=(i == 0), stop=last),
                        reads=reads if i == 0 else (), writes=writes if i == 0 else (), signal=last)

    def act(self, out, in_, func, reads, writes, bias=None, scale=None):
        kw = {}
        if bias is not None:
            kw["bias"] = bias
        if scale is not None:
            kw["scale"] = scale
        self.sch.op("act", lambda e: e.activation(out=out, in_=in_, func=func, **kw), reads=reads, writes=writes)

    def tt(self, eng, out, in0, in1, op, reads, writes):
        self.sch.op(eng, lambda e: e.tensor_tensor(out=out, in0=in0, in1=in1, op=op), reads=reads, writes=writes)

    def ts(self, eng, out, in0, s1, s2, op0, op1, reads, writes):
        if op1 is None:
            self.sch.op(eng, lambda e: e.tensor_scalar(out=out, in0=in0, scalar1=s1, scalar2=None, op0=op0), reads=reads, writes=writes)
        else:
            self.sch.op(eng, lambda e: e.tensor_scalar(out=out, in0=in0, scalar1=s1, scalar2=s2, op0=op0, op1=op1), reads=reads, writes=writes)

    def stt(self, eng, out, in0, scalar, in1, op0, op1, reads, writes):
        self.sch.op(eng, lambda e: e.scalar_tensor_tensor(out=out, in0=in0, scalar=scalar, in1=in1, op0=op0, op1=op1), reads=reads, writes=writes)

    def cv_setup(self):
        order = []
        for i in range(4):
            order.append((f"F{i}0", [f"win_{i}_0", f"wout_{i}_0"]))
            order.append((f"M{i}", [n for n in self.wb if n.startswith("mx_") and n.endswith(f"_{i}")]))
            order.append((f"F{i}1", [f"win_{i}_1", f"wout_{i}_1"]))
        self.cv_groups = {}
        self.cv_list = []
        for gi, (g, names) in enumerate(order):
            sem = self.sch.new_sem(f"cv{gi}")
            grp = {"sem": sem, "buf": Buf(), "n": 0, "issued": 0}
            self.cv_groups[g] = grp
            for name in names:
                src, dst = self.din[name], self.wb[name]
                n = src.shape[1]
                step = 8192
                for a in range(0, n, step):
                    b_ = min(n, a + step)
                    self.cv_list.append((grp, dst[:, a:b_], src[:, a:b_]))
                    grp["n"] += 1
        self.cv_pos = 0

    def pump(self, k):
        while k > 0 and self.cv_pos < len(self.cv_list):
            grp, dst, src = self.cv_list[self.cv_pos]
            self.sch.dma("pool", dst, src, grp["sem"])
            grp["issued"] += 1
            if grp["issued"] == grp["n"]:
                grp["buf"].w = (grp["sem"], self.sch.cnt[grp["sem"]])
            self.cv_pos += 1
            k -= 1

    def need(self, g):
        grp = self.cv_groups[g]
        while grp["issued"] < grp["n"]:
            self.pump(1)
        return grp["buf"]

    def phase0(self):
        sch, ar, nc = self.sch, self.ar, self.nc
        self.ones = ar.bf16(128)
        self.ones_b = Buf()
        self.epsc = ar.f32(1)
        self.consts_b = Buf()
        self.VEC = ar.f32(4 * 3 * 2 * 3 * 8).rearrange("p (i s w v k) -> p i s w v k", i=4, s=3, w=2, v=3)
        self.vec_b = Buf()
        ar.set_base()
        sch.op("pool", lambda e: e.memset(self.ones, 1.0), writes=[self.ones_b])
        sch.op("pool", lambda e: e.memset(self.epsc, EPS), writes=[self.consts_b])
        self.cv_setup()
        self.need("F00")
        self.need("M0")
        cT = ar.f32(16)
        sc = ar.f32(16)
        scb = Buf()
        gT = ar.f32(4 * 6 * 8).rearrange("p (i r k) -> p i r k", i=4, r=6)
        mT = ar.f32(4 * 2 * 72).rearrange("p (i w n) -> p i w n", i=4, w=2)
        mTb = Buf()
        ldb = Buf()
        brow = ar.f32(9216)
        browb = Buf()
        mrow = ar.f32(9216)
        mrowb = Buf()
        ident = ar.f32(2)
        identb = Buf()
        sch.dsem(15)
        sch.dma("sp", cT, self.cT, "d15", writes=[ldb])
        sch.dma("sp", gT, self.din["gT"].rearrange("p (i r k) -> p i r k", i=4, r=6), "d15", writes=[ldb])
        self.act(sc, cT, AF.Silu, [ldb], [scb])
        sc3 = sc.rearrange("p (k w) -> p k w", w=2)
        sch.dsem(13)
        sch.dma("sp", ident[0:2, 0:2], self.din["sm_i2"], "d13", writes=[identb])
        nslot = 4
        slots = []
        semk = []
        for i in range(nslot):
            slots.append((ar.f32(8 * 512).rearrange("p (k n) -> p k n", k=8), Buf()))
            semk.append(sch.dsem(i))
        strm = Stream(sch, "sp", slots, semk)
        aw = self.din["ada_w"]
        for i in range(4):
            v = aw[i].rearrange("(kc p) n -> p kc n", p=128)
            for pc in range(18):
                strm.add(lambda ap, v=v, pc=pc: [(ap, v[:, :, pc * 512:(pc + 1) * 512])])
        pr = self.ps_ring([0, 1, 2, 3])
        pt_ring = self.ps_ring([4, 5])
        sch.dsem(14)
        for i in range(4):
            for w in range(2):
                sch.dma("sp", brow[w:w + 1, :], self.din["ada_b"][i:i + 1, :], "d14", writes=[browb])
            for pc in range(18):
                wap, wbuf = strm.get()
                pt, pb = pr.get()
                self.mm_group(pt[0:2, 0:512], [(sc3[:, kc, :], wap[:, kc, :]) for kc in range(8)], reads=[wbuf, scb], writes=[pb])
                self.tt("dve", mrow[0:2, pc * 512:(pc + 1) * 512], pt[0:2, 0:512], brow[0:2, pc * 512:(pc + 1) * 512], ALU.add, [pb, browb], [mrowb])
            tp, tb = pt_ring.get()
            ps3 = tp[:, 0:144].rearrange("p (n w) -> p n w", w=2)
            for n in range(72):
                self.mm_group(ps3[:, n, :], [(mrow[0:2, n * 128:(n + 1) * 128], ident[0:2, 0:2])], reads=[mrowb, identb], writes=[tb])
            for w in range(2):
                sch.op("dve", lambda e, o=mT[:, i, w, :], a=ps3[:, :, w]: e.tensor_copy(out=o, in_=a), reads=[tb], writes=[mTb])
        for i in range(4):
            for s in range(3):
                k0 = 3 * s
                wgt = 0.5 if s != 1 else 1.0
                for w in range(2):
                    m_sh = mT[:, i, w, (k0) * 8:(k0 + 1) * 8]
                    m_sc = mT[:, i, w, (k0 + 1) * 8:(k0 + 2) * 8]
                    m_gt = mT[:, i, w, (k0 + 2) * 8:(k0 + 3) * 8]
                    self.stt("dve", self.VEC[:, i, s, w, 0, :], m_sc, 1.0, gT[:, i, 2 * s, :], ALU.add, ALU.mult, [mTb, ldb], [self.vec_b])
                    self.sch.op("dve", lambda e, o=self.VEC[:, i, s, w, 1, :], a=m_sh: e.tensor_copy(out=o, in_=a), reads=[mTb], writes=[self.vec_b])
                    self.stt("dve", self.VEC[:, i, s, w, 2, :], m_gt, wgt, gT[:, i, 2 * s + 1, :], ALU.mult, ALU.mult, [mTb, ldb], [self.vec_b])
        sch.barrier()
        ar.reset()

    def rstd_from_sq(self, sq_aps, sq_bufs, n, dim, ps_item, out_ap, out_buf, mode, tmp_ap=None, tmp_buf=None):
        pt, pb = ps_item
        self.mm_group(pt[:, 0:n], [(self.ones, a) for a in sq_aps], reads=list(sq_bufs) + [self.ones_b], writes=[pb])
        if mode == "sqrt":
            self.act(out_ap, pt[:, 0:n], AF.Sqrt, [pb, self.consts_b], [out_buf], bias=self.epsc[:, 0:1], scale=1.0 / dim)
            self.sch.op("dve", lambda e: e.reciprocal(out=out_ap, in_=out_ap), reads=[out_buf], writes=[out_buf])
        else:
            self.act(out_ap, pt[:, 0:n], AF.Ln, [pb, self.consts_b], [out_buf], bias=self.epsc[:, 0:1], scale=1.0 / dim)
            self.act(out_ap, out_ap, AF.Exp, [out_buf], [out_buf], scale=-0.5)

    def norm_mod(self, hin, hb, n, vec, sqb, sq_bufs, ps_item, rstd, rstd_b, tmp_ring, uT, uT_bufs, mode):
        for k in range(8):
            self.act(sqb[:, k, 0:n], hin[:, k, 0:n], AF.Square, [hb], [sq_bufs[k]])
        self.rstd_from_sq([sqb[:, k, 0:n] for k in range(8)], sq_bufs, n, float(D), ps_item, rstd[:, 0:n], rstd_b, mode)
        for k in range(8):
            tp, tb = tmp_ring.get()
            self.tt("dve", tp[:, 0:n], hin[:, k, 0:n], rstd[:, 0:n], ALU.mult, [hb, rstd_b], [tb])
            self.act(uT[:, k, 0:n], tp[:, 0:n], AF.Identity, [tb, self.vec_b], [uT_bufs[k]], bias=vec[:, 1, k:k + 1], scale=vec[:, 0, k:k + 1])

    def residual(self, hin, hb, n, vec, ytmp, ysq, y_bufs, ysq_bufs, ps_item, rstd, rstd_b, tmp_ring, mode):
        self.rstd_from_sq([ysq[:, m, 0:n] for m in range(8)], ysq_bufs, n, float(D), ps_item, rstd[:, 0:n], rstd_b, mode)
        for k in range(8):
            tp, tb = tmp_ring.get()
            self.stt("dve", tp[:, 0:n], ytmp[:, k, 0:n], vec[:, 2, k:k + 1], rstd[:, 0:n], ALU.mult, ALU.mult, [y_bufs[k], rstd_b, self.vec_b], [tb])
            self.tt("pool", hin[:, k, 0:n], hin[:, k, 0:n], tp[:, 0:n], ALU.add, [tb], [hb])

    def ffn_phase(self, i, s, src, dst, chunks):
        sch, ar = self.sch, self.ar
        ar.reset()
        svec = 0 if s == 0 else 2
        hslots = [(ar.f32(8 * 512).rearrange("p (k n) -> p k n", k=8), Buf()) for _ in range(2)]
        ld_sems = [sch.dsem(x) for x in range(2)]
        st_sems = [sch.dsem(2 + x) for x in range(2)]
        sqb = ar.bf16(8 * 512).rearrange("p (k n) -> p k n", k=8)
        sq_bufs = [Buf() for _ in range(8)]
        uTs = [(ar.bf16(8 * 512).rearrange("p (k n) -> p k n", k=8), [Buf() for _ in range(8)]) for _ in range(2)]
        aTs = [(ar.bf16(NJ * 512).rearrange("p (j n) -> p j n", j=NJ), [Buf() for _ in range(NJ)]) for _ in range(2)]
        tmp_ring = Ring([(ar.f32(512), Buf()) for _ in range(3)])
        sg_ring = Ring([(ar.f32(512), Buf()) for _ in range(3)])
        rstd_ring = Ring([(ar.f32(512), Buf()) for _ in range(2)])
        ytmp = ar.f32(8 * 512).rearrange("p (k n) -> p k n", k=8)
        y_bufs = [Buf() for _ in range(8)]
        ysq = ar.bf16(8 * 512).rearrange("p (k n) -> p k n", k=8)
        ysq_bufs = [Buf() for _ in range(8)]
        win_slots = [(ar.bf16(2 * 2048).rearrange("p (jj kc c) -> p jj kc c", jj=2, kc=8), Buf()) for _ in range(3)]
        win_sems = [sch.dsem(4 + x) for x in range(3)]
        wout_slots = [(ar.bf16(2 * NJ * 128).rearrange("p (mm j c) -> p mm j c", mm=2, j=NJ), Buf()) for _ in range(2)]
        wout_sems = [sch.dsem(7 + x) for x in range(2)]
        win_d = self.wb[f"win_{i}_{s}"].rearrange("p (j kc c) -> p j kc c", j=NJ, kc=8)
        wout_d = self.wb[f"wout_{i}_{s}"].rearrange("p (m j c) -> p m j c", m=8, j=NJ)
        cvb = self.need(f"F{i}{s}")
        wstream = Stream(sch, "sp", win_slots, win_sems, [cvb])
        ostream = Stream(sch, "sp", wout_slots, wout_sems, [cvb])
        for ci in chunks:
            for jg in range(NJ // 2):
                wstream.add(lambda ap, jg=jg: [(ap, win_d[:, 2 * jg:2 * jg + 2, :, :])])
            for mg in range(4):
                ostream.add(lambda ap, mg=mg: [(ap, wout_d[:, 2 * mg:2 * mg + 2, :, :])])
        ps_misc = self.ps_ring([0])
        ps_gu = self.ps_ring([1, 2, 3, 4])
        ps_y = self.ps_ring([5, 6])

        def load(idx):
            ci = chunks[idx]
            t0, n = chunk_range(ci)
            hp, hb = hslots[idx % 2]
            sch.dma("pool", hp[:, :, 0:n], src[:, :, t0:t0 + n].rearrange("k p n -> p k n"), ld_sems[idx % 2], writes=[hb])

        load(0)
        for idx, ci in enumerate(chunks):
            t0, n = chunk_range(ci)
            w = 1 if ci == 0 else 0
            vec = self.VEC[:, i, svec, w]
            hp, hb = hslots[idx % 2]
            if idx + 1 < len(chunks):
                load(idx + 1)
            uT, uT_bufs = uTs[idx % 2]
            aT, aT_bufs = aTs[idx % 2]
            ostream.prefetch()
            self.pump(2)
            rs, rsb = rstd_ring.get()
            self.norm_mod(hp, hb, n, vec, sqb, sq_bufs, ps_misc.get(), rs, rsb, tmp_ring, uT, uT_bufs, "sqrt")
            for jg in range(NJ // 2):
                wap, wbuf = wstream.get()
                for jj in range(2):
                    j = 2 * jg + jj
                    gp, gb = ps_gu.get()
                    up, ub = ps_gu.get()
                    self.mm_group(gp[:, 0:n], [(wap[:, jj, kc, 0:128], uT[:, kc, 0:n]) for kc in range(8)], reads=[wbuf] + uT_bufs, writes=[gb])
                    self.mm_group(up[:, 0:n], [(wap[:, jj, kc, 128:256], uT[:, kc, 0:n]) for kc in range(8)], reads=[wbuf] + uT_bufs, writes=[ub])
                    sgp, sgb = sg_ring.get()
                    self.act(sgp[:, 0:n], gp[:, 0:n], AF.Silu, [gb], [sgb])
                    self.tt("dve", aT[:, j, 0:n], up[:, 0:n], sgp[:, 0:n], ALU.mult, [ub, sgb], [aT_bufs[j]])
            for mg in range(4):
                oap, obuf = ostream.get()
                for mm in range(2):
                    m = 2 * mg + mm
                    yp, yb = ps_y.get()
                    self.mm_group(yp[:, 0:n], [(oap[:, mm, j, :], aT[:, j, 0:n]) for j in range(NJ)], reads=[obuf] + aT_bufs, writes=[yb])
                    self.act(ytmp[:, m, 0:n], yp[:, 0:n], AF.Copy, [yb], [y_bufs[m]])
                    self.tt("dve", ysq[:, m, 0:n], yp[:, 0:n], ytmp[:, m, 0:n], ALU.mult, [yb, y_bufs[m]], [ysq_bufs[m]])
            rs, rsb = rstd_ring.get()
            self.residual(hp, hb, n, vec, ytmp, ysq, y_bufs, ysq_bufs, ps_misc.get(), rs, rsb, tmp_ring, "sqrt")
            sch.dma("pool", dst[:, :, t0:t0 + n].rearrange("k p n -> p k n"), hp[:, :, 0:n], st_sems[idx % 2], reads=[hb])
        sch.barrier()

    def build(self):
        self.phase0()
        sub = 0
        for i in range(4):
            last = i == 3
            for s in range(3):
                if sub >= self.n_sub:
                    break
                src = self.xT if sub == 0 else self.outT
                if s == 0:
                    self.ffn_phase(i, 0, src, self.outT, list(range(NCH)))
                elif s == 2:
                    self.ffn_phase(i, 1, src, self.outT, list(range(1, NCH)) if last else list(range(NCH)))
                else:
                    self.mixer_phase(i, src, self.outT)
                sub += 1
        self.sch.emit()
        return self.nc


    def rope(self, pa, pab, pb_, pbb, M, n, tC, tS, tabb, dests, tmps, gains=None, rstd=None):
        (t1, t1b), (t2, t2b) = tmps
        if gains is None:
            self.tt("dve", t1[0:M, 0:n], pa[0:M, 0:n], tC[0:M, 0:n], ALU.mult, [pab, tabb], [t1b])
            self.tt("dve", t2[0:M, 0:n], pb_[0:M, 0:n], tS[0:M, 0:n], ALU.mult, [pbb, tabb], [t2b])
        else:
            ga, gb, gbuf = gains
            self.stt("dve", t1[0:M, 0:n], pa[0:M, 0:n], ga[0:M, :], tC[0:M, 0:n], ALU.mult, ALU.mult, [pab, tabb, gbuf], [t1b])
            self.stt("dve", t2[0:M, 0:n], pb_[0:M, 0:n], gb[0:M, :], tS[0:M, 0:n], ALU.mult, ALU.mult, [pbb, tabb, gbuf], [t2b])
        if rstd is None:
            for (r0, r1, dap, db) in dests:
                self.tt("pool", dap, t1[r0:r1, 0:n], t2[r0:r1, 0:n], ALU.add, [t1b, t2b], [db])
        else:
            rs, rsb = rstd
            self.tt("pool", t1[0:M, 0:n], t1[0:M, 0:n], t2[0:M, 0:n], ALU.add, [t2b], [t1b])
            for (r0, r1, dap, db) in dests:
                self.tt("dve", dap, t1[r0:r1, 0:n], rs[r0:r1, 0:n], ALU.mult, [t1b, rsb], [db])

    def load_const(self, q, dst, src, semkey, buf):
        self.sch.dma(q, dst, src, semkey, writes=[buf])

    def k_phase(self, L, src):
        sch, ar = self.sch, self.ar
        ar.reset()
        hp = ar.f32(8 * 512).rearrange("p (k n) -> p k n", k=8)
        hb = Buf()
        sqb = ar.bf16(8 * 512).rearrange("p (k n) -> p k n", k=8)
        sq_bufs = [Buf() for _ in range(8)]
        uT = ar.bf16(8 * 512).rearrange("p (k n) -> p k n", k=8)
        uT_bufs = [Buf() for _ in range(8)]
        rstd_ring = Ring([(ar.f32(512), Buf()) for _ in range(2)])
        tmp_ring = Ring([(ar.f32(512), Buf()) for _ in range(4)])
        tabC = ar.f32(512)
        tabS = ar.f32(512)
        tabb = Buf()
        kout_ring = Ring([(ar.bf16(512), Buf(), sch.dsem(3 + x)) for x in range(3)])
        VC = {0: 1024, 1: 256, 2: 1024, 3: 128}[L]
        dv = {0: 128, 1: 128, 2: 128, 3: 64}[L]
        nun = VC // dv
        vouts = [(ar.bf16(4 * VC).rearrange("p (t c) -> p t c", t=4), Buf(), sch.dsem(6 + x)) for x in range(2)]
        Vs = self.scr[f"V{L}"]
        wbuf = Buf()
        cvb = self.need(f"M{L}")
        sch.dsem(0); sch.dsem(1); sch.dsem(2)
        psr = self.ps_ring([0, 1, 2, 3, 4, 5, 6, 7])
        if L in (0, 1, 3):
            nkb = {0: 8, 1: 2, 3: 1}[L]
            nqb = 8
            wk = ar.bf16(2 * nkb * 1024).rearrange("p (b kc c) -> p b kc c", b=2 * nkb, kc=8)
            wsrc = self.wb[f"mx_qk_{L}"].rearrange("p (b kc c) -> p b kc c", kc=8, c=128)
            sch.dma("sp", wk, wsrc[:, 2 * nqb:2 * nqb + 2 * nkb, :, :], "d2", reads=[cvb], writes=[wbuf])
            wv = ar.bf16(8 * VC).rearrange("p (kc c) -> p kc c", kc=8)
            sch.dma("sp", wv, self.wb[f"mx_v_{L}"].rearrange("p (kc c) -> p kc c", kc=8), "d2", reads=[cvb], writes=[wbuf])
            KTs = self.scr[f"KT{L}"]
            if L == 1:
                g1 = ar.f32(4)
                sch.dma("sp", g1, self.din["sm_g_1"], "d2", reads=[cvb], writes=[wbuf])
                sqk_ring = Ring([(ar.bf16(512), Buf()) for _ in range(2)])
        else:
            win = ar.bf16(3 * 1024).rearrange("p (b kc c) -> p b kc c", b=3, kc=8)
            wsrc = self.wb["mx_in_2"].rearrange("p (b kc c) -> p b kc c", kc=8, c=128)
            sch.dma("sp", win, wsrc[:, 2:5, :, :], "d2", reads=[cvb], writes=[wbuf])
            wkn = ar.bf16(8 * 128).rearrange("p (b c) -> p b c", b=8)
            sch.dma("sp", wkn, self.wb["mx_kn_2"].rearrange("p (b c) -> p b c", b=8), "d2", reads=[cvb], writes=[wbuf])
            wuv = ar.bf16(1024)
            sch.dma("sp", wuv, self.wb["mx_uv_2"], "d2", reads=[cvb], writes=[wbuf])
            g2 = ar.f32(3)
            sch.dma("sp", g2, self.din["sm_g_2"], "d2", reads=[cvb], writes=[wbuf])
            sqk_ring = Ring([(ar.bf16(512), Buf()) for _ in range(2)])
            ckvn = ar.bf16(512)
            ckvn_b = Buf()
        rc = self.din[f"rp_c_{L}"]
        rs_ = self.din[f"rp_s_{L}"]
        for ci in range(NCH):
            t0, n = chunk_range(ci)
            w = 1 if ci == 0 else 0
            nt = n // 128
            tile0 = t0 // 128
            vec = self.VEC[:, L, 1, w]
            self.pump(2)
            sch.dma("pool", hp[:, :, 0:n], src[:, :, t0:t0 + n].rearrange("k p n -> p k n"), "d0", writes=[hb])
            sch.dma("pool", tabC[:, 0:n], rc[:, t0:t0 + n], "d1", writes=[tabb])
            sch.dma("pool", tabS[:, 0:n], rs_[:, t0:t0 + n], "d1", writes=[tabb])
            rsd, rsdb = rstd_ring.get()
            self.norm_mod(hp, hb, n, vec, sqb, sq_bufs, psr.get(), rsd, rsdb, tmp_ring, uT, uT_bufs, "lnexp")
            vout, vob, vsem = vouts[ci % 2]
            if L in (0, 1, 3):
                for b in range(nkb):
                    pa, pab = psr.get()
                    pb_, pbb = psr.get()
                    self.mm_group(pa[:, 0:n], [(wk[:, b, kc, :], uT[:, kc, 0:n]) for kc in range(8)], reads=[wbuf] + uT_bufs, writes=[pab])
                    self.mm_group(pb_[:, 0:n], [(wk[:, nkb + b, kc, :], uT[:, kc, 0:n]) for kc in range(8)], reads=[wbuf] + uT_bufs, writes=[pbb])
                    ko, kob, ksem = kout_ring.get()
                    gains = None
                    rstd = None
                    if L == 1:
                        sqk, sqkb = sqk_ring.get()
                        self.act(sqk[:, 0:n], pa[:, 0:n], AF.Square, [pab], [sqkb])
                        rk, rkb = rstd_ring.get()
                        self.rstd_from_sq([sqk[:, 0:n]], [sqkb], n, 128.0, psr.get(), rk[:, 0:n], rkb, "lnexp")
                        gains = (g1[:, 2:3], g1[:, 3:4], wbuf)
                        rstd = (rk, rkb)
                    self.rope(pa, pab, pb_, pbb, 128, n, tabC, tabS, tabb, [(0, 128, ko[:, 0:n], kob)], (tmp_ring.get(), tmp_ring.get()), gains, rstd)
                    sch.dma("pool", KTs[b * 128:(b + 1) * 128, t0:t0 + n], ko[:, 0:n], ksem, reads=[kob])
                for tt_ in range(nt):
                    for cg in range((VC + 511) // 512):
                        cw = min(512, VC - cg * 512)
                        pv, pvb = psr.get()
                        self.mm_group(pv[:, 0:cw], [(uT[:, kc, tt_ * 128:(tt_ + 1) * 128], wv[:, kc, cg * 512:cg * 512 + cw]) for kc in range(8)],
                                      reads=[wbuf] + uT_bufs, writes=[pvb])
                        self.act(vout[:, tt_, cg * 512:cg * 512 + cw], pv[:, 0:cw], AF.Copy, [pvb], [vob])
            else:
                pc, pcb = psr.get()
                self.mm_group(pc[:, 0:n], [(win[:, 0, kc, :], uT[:, kc, 0:n]) for kc in range(8)], reads=[wbuf] + uT_bufs, writes=[pcb])
                sqk, sqkb = sqk_ring.get()
                self.act(sqk[:, 0:n], pc[:, 0:n], AF.Square, [pcb], [sqkb])
                rk, rkb = rstd_ring.get()
                self.rstd_from_sq([sqk[:, 0:n]], [sqkb], n, 128.0, psr.get(), rk[:, 0:n], rkb, "lnexp")
                self.stt("dve", ckvn[:, 0:n], pc[:, 0:n], g2[:, 2:3], rk[:, 0:n], ALU.mult, ALU.mult, [pcb, rkb, wbuf], [ckvn_b])
                for jb in range(8):
                    pk, pkb = psr.get()
                    self.mm_group(pk[:, 0:n], [(wkn[:, jb, :], ckvn[:, 0:n])], reads=[wbuf, ckvn_b], writes=[pkb])
                    ko, kob, ksem = kout_ring.get()
                    self.act(ko[:, 0:n], pk[:, 0:n], AF.Copy, [pkb], [kob])
                    sch.dma("pool", self.scr["KN2"][jb * 128:(jb + 1) * 128, t0:t0 + n], ko[:, 0:n], ksem, reads=[kob])
                pa, pab = psr.get()
                pb_, pbb = psr.get()
                self.mm_group(pa[0:96, 0:n], [(win[:, 1, kc, 0:96], uT[:, kc, 0:n]) for kc in range(8)], reads=[wbuf] + uT_bufs, writes=[pab])
                self.mm_group(pb_[0:96, 0:n], [(win[:, 2, kc, 0:96], uT[:, kc, 0:n]) for kc in range(8)], reads=[wbuf] + uT_bufs, writes=[pbb])
                ko, kob, ksem = kout_ring.get()
                self.rope(pa, pab, pb_, pbb, 96, n, tabC, tabS, tabb, [(64, 96, ko[64:96, 0:n], kob)], (tmp_ring.get(), tmp_ring.get()))
                sch.dma("pool", self.scr["KR2"][:, t0:t0 + n], ko[64:96, 0:n], ksem, reads=[kob])
                for tt_ in range(nt):
                    for cg in range(2):
                        pv, pvb = psr.get()
                        self.mm_group(pv[:, 0:512], [(ckvn[:, tt_ * 128:(tt_ + 1) * 128], wuv[:, cg * 512:(cg + 1) * 512])], reads=[wbuf, ckvn_b], writes=[pvb])
                        self.act(vout[:, tt_, cg * 512:(cg + 1) * 512], pv[:, 0:512], AF.Copy, [pvb], [vob])
            for u in range(nun):
                vd = Vs[u].rearrange("p (t c) -> p t c", t=34)
                sch.dma("pool", vd[:, tile0:tile0 + nt, :], vout[:, 0:nt, u * dv:(u + 1) * dv], vsem, reads=[vob])
        sch.barrier()

    def run_jobs(self, jobs, scale, pt_ring, st_ring, kvs, mask, D=3):
        sch = self.sch
        nj = len(jobs)
        for idx in range(nj + D):
            if idx < nj:
                j = jobs[idx]
                if j.get("unit_first"):
                    slot = kvs.get(ahead=1)
                    assert slot is j["slot"]
                kt, lo, hi, moff = j["tile"]
                stp, stb = st_ring.get()
                sch.op("pe", lambda e, o=stp[:, lo:hi], l=j["K"][:, kt * 128:(kt + 1) * 128], r=j["Q"][:, lo:hi]: e.matmul(o, l, r, start=True, stop=True),
                       reads=[j["kb"], j["qb"]], writes=[stb])
                ptp, ptb = pt_ring.get()
                self.act(ptp[:, lo:hi], stp[:, lo:hi], AF.Exp, [stb], [ptb], scale=scale)
                if moff is not None:
                    mk, mkb = mask
                    self.tt("pool", ptp[:, lo:hi], ptp[:, lo:hi], mk[:, moff:moff + (hi - lo)], ALU.mult, [mkb], [ptb])
                j["pt"] = (ptp, ptb)
            k = idx - D
            if k >= 0:
                j = jobs[k]
                kt, lo, hi, moff = j["tile"]
                ptp, ptb = j["pt"]
                (otp, otb), (dnp, dnb) = j["acc"]
                first, last = j["first"], j["last"]
                fold = j.get("fold", False)
                sch.op("pe", lambda e, o=otp[:, lo:hi], l=j["V"][:, kt, :], r=ptp[:, lo:hi], first=first, last=last: e.matmul(o, l, r, start=first, stop=last),
                       reads=[ptb, j["kb"]], writes=[otb], signal=fold)
                if not fold:
                    sch.op("pe", lambda e, o=dnp[:, lo:hi], r=ptp[:, lo:hi], first=first, last=last: e.matmul(o, self.ones, r, start=first, stop=last),
                           reads=[ptb, self.ones_b], writes=[dnb])
                if j.get("post") is not None:
                    j["post"]()
                if j.get("unit_last"):
                    kvs.prefetch(ahead=1)

    def q_phase(self, L, src, dst):
        sch, ar = self.sch, self.ar
        ar.reset()
        chunks = list(range(1, NCH)) if L == 3 else list(range(NCH))
        hp = ar.f32(8 * 512).rearrange("p (k n) -> p k n", k=8)
        hb = Buf()
        sqb = ar.bf16(8 * 512).rearrange("p (k n) -> p k n", k=8)
        sq_bufs = [Buf() for _ in range(8)]
        uT = ar.bf16(8 * 512).rearrange("p (k n) -> p k n", k=8)
        uT_bufs = [Buf() for _ in range(8)]
        rstd_ring = Ring([(ar.f32(512), Buf()) for _ in range(2)])
        tmp_ring = Ring([(ar.f32(512), Buf()) for _ in range(5)])
        tabC = ar.f32(512)
        tabS = ar.f32(512)
        tabb = Buf()
        wbuf = Buf()
        cvb = self.need(f"M{L}")
        for x in range(8):
            sch.dsem(x)
        wo = ar.bf16(8 * 8 * 128).rearrange("p (m ko c) -> p m ko c", m=8, ko=8)
        sch.dma("sp", wo, self.wb[f"mx_o_{L}"].rearrange("p (m ko c) -> p m ko c", m=8, ko=8), "d2", reads=[cvb], writes=[wbuf])
        nQ = {0: 16, 1: 8, 2: 16, 3: 16}[L]
        QT = ar.bf16(nQ * 512).rearrange("p (q n) -> p q n", q=nQ)
        QT_bufs = [Buf() for _ in range(nQ)]
        if L in (0, 3):
            sch.op("pool", lambda e: e.memset(QT, 0.0), writes=QT_bufs)
        if L in (0, 1, 3):
            wq = ar.bf16(16 * 1024).rearrange("p (b kc c) -> p b kc c", b=16, kc=8)
            wsrc = self.wb[f"mx_qk_{L}"].rearrange("p (b kc c) -> p b kc c", kc=8, c=128)
            sch.dma("sp", wq, wsrc[:, 0:16, :, :], "d2", reads=[cvb], writes=[wbuf])
        else:
            win = ar.bf16(2 * 1024).rearrange("p (b kc c) -> p b kc c", b=2, kc=8)
            wsrc = self.wb["mx_in_2"].rearrange("p (b kc c) -> p b kc c", kc=8, c=128)
            sch.dma("sp", win, wsrc[:, 0:2, :, :], "d2", reads=[cvb], writes=[wbuf])
            wuq = ar.bf16(32 * 256).rearrange("p (b kc c) -> p b kc c", b=32, kc=2)
            sch.dma("sp", wuq, self.wb["mx_uq_2"].rearrange("p (b kc c) -> p b kc c", b=32, kc=2), "d2", reads=[cvb], writes=[wbuf])
            g2 = ar.f32(3)
            sch.dma("sp", g2, self.din["sm_g_2"], "d2", reads=[cvb], writes=[wbuf])
            cqn = ar.bf16(2 * 512).rearrange("p (c n) -> p c n", c=2)
            cqn_b = Buf()
            swm = ar.f32(128)
            sch.dma("sp", swm, self.din["sm_sw"], "d2", reads=[cvb], writes=[wbuf])
        sqk_ring = Ring([(ar.bf16(512), Buf()) for _ in range(2)])
        if L == 1:
            g1 = ar.f32(4)
            sch.dma("sp", g1, self.din["sm_g_1"], "d2", reads=[cvb], writes=[wbuf])
        mask = None
        if L == 3:
            mk = ar.bf16(384)
            mkb = Buf()
            sch.dma("pool", mk, self.din["sm_mask_3"], "d2", reads=[cvb], writes=[wbuf])
            mask = (mk, wbuf)
            esink = ar.f32(16)
            esb = Buf()
            sch.dma("sp", esink, self.din["sm_sink_3"], "d2", reads=[cvb], writes=[wbuf])
            self.act(esink, esink, AF.Exp, [wbuf], [esb])
        if L == 0:
            lam = ar.f32(256)
            lamb = Buf()
            sch.dma("sp", lam, self.din["sm_lam_0"], "d2", reads=[cvb], writes=[wbuf])
            subg = ar.f32(1)
            sch.dma("sp", subg, self.din["sm_subln_0"], "d2", reads=[cvb], writes=[wbuf])
            pp = ar.f32(128)
            sc2 = ar.f32(4)
            self.tt("dve", pp[:, 0:64], lam[:, 0:64], lam[:, 64:128], ALU.mult, [wbuf], [lamb])
            self.tt("dve", pp[:, 64:128], lam[:, 128:192], lam[:, 192:256], ALU.mult, [lamb], [lamb])
            sch.op("dve", lambda e: e.reduce_sum(out=sc2[:, 0:1], in_=pp[:, 0:64], axis=mybir.AxisListType.X), reads=[lamb], writes=[lamb])
            sch.op("dve", lambda e: e.reduce_sum(out=sc2[:, 1:2], in_=pp[:, 64:128], axis=mybir.AxisListType.X), reads=[lamb], writes=[lamb])
            self.act(sc2[:, 0:2], sc2[:, 0:2], AF.Exp, [lamb], [lamb])
            self.tt("dve", sc2[:, 2:3], sc2[:, 1:2], sc2[:, 0:1], ALU.subtract, [lamb], [lamb])
            self.ts("dve", sc2[:, 2:3], sc2[:, 2:3], -LAMBDA_INIT0, None, ALU.add, None, [lamb], [lamb])
            self.ts("dve", subg, subg, 1.0 - LAMBDA_INIT0, None, ALU.mult, None, [lamb], [lamb])
            neglam = sc2[:, 2:3]
        nslot = 2
        kv_slots = []
        for x in range(nslot):
            if L == 2:
                item = {"KA": ar.bf16(T), "KB": ar.bf16(T), "V": ar.bf16(34 * 192).rearrange("p (t c) -> p t c", t=34)}
            else:
                item = {"K": ar.bf16(T), "V": ar.bf16(34 * 128).rearrange("p (t c) -> p t c", t=34)}
            kv_slots.append((item, Buf()))
            if L == 2:
                sch.op("pool", lambda e, o=item["V"][:, :, 64:128]: e.memset(o, 1.0), writes=[kv_slots[-1][1]])
        kvs = Stream(sch, "sp", kv_slots, [sch.dsem(3), sch.dsem(4)])
        nunit = {0: 8, 1: 2, 2: 8, 3: 2}[L]

        def tiles_for(ci):
            t0, n = chunk_range(ci)
            if ci == 0:
                return [(0, 0, n, None), (1, 0, n, None)]
            if L != 3:
                return [(kt, 0, n, None) for kt in range(34)]
            cs = t0 - CTX
            out = [(0, 0, n, None), (1, 0, n, None)]
            for ktl in range(32):
                koff = ktl * 128 - cs
                lo = max(0, koff - 128)
                hi = min(512, koff + 256)
                if hi <= lo:
                    continue
                out.append((2 + ktl, lo, hi, lo - (koff - 128)))
            return out

        def add_fill(ci, u):
            tl = tiles_for(ci)
            kts = sorted(set(t[0] for t in tl))
            runs = []
            for kt in kts:
                if runs and runs[-1][1] == kt:
                    runs[-1][1] = kt + 1
                else:
                    runs.append([kt, kt + 1])

            def fn(item, runs=runs, u=u):
                out = []
                for a, b_ in runs:
                    ca, cb = a * 128, b_ * 128
                    if L in (0, 1):
                        out.append((item["K"][:, ca:cb], self.scr[f"KT{L}"][u * 128:(u + 1) * 128, ca:cb]))
                        vd = self.scr[f"V{L}"][u].rearrange("p (t c) -> p t c", t=34)
                        out.append((item["V"][:, a:b_, :], vd[:, a:b_, :]))
                    elif L == 2:
                        out.append((item["KA"][0:64, ca:cb], self.scr["KN2"][(2 * u) * 64:(2 * u) * 64 + 64, ca:cb]))
                        out.append((item["KA"][64:96, ca:cb], self.scr["KR2"][:, ca:cb]))
                        out.append((item["KB"][0:64, ca:cb], self.scr["KN2"][(2 * u + 1) * 64:(2 * u + 1) * 64 + 64, ca:cb]))
                        out.append((item["KB"][64:96, ca:cb], self.scr["KR2"][:, ca:cb]))
                        vd = self.scr["V2"][u].rearrange("p (t c) -> p t c", t=34)
                        out.append((item["V"][:, a:b_, 0:64], vd[:, a:b_, 0:64]))
                        out.append((item["V"][:, a:b_, 128:192], vd[:, a:b_, 64:128]))
                    else:
                        out.append((item["K"][0:64, ca:cb], self.scr["KT3"][u * 64:(u + 1) * 64, ca:cb]))
                        out.append((item["K"][64:128, ca:cb], self.scr["KT3"][u * 64:(u + 1) * 64, ca:cb]))
                        vd = self.scr["V3"][u].rearrange("p (t c) -> p t c", t=34)
                        out.append((item["V"][:, a:b_, 0:64], vd[:, a:b_, :]))
                        out.append((item["V"][:, a:b_, 64:128], vd[:, a:b_, :]))
                return out
            kvs.add(fn)

        for ci in chunks:
            for u in range(nunit):
                add_fill(ci, u)
        pt_ring = Ring([(ar.bf16(512), Buf()) for _ in range(6)])
        OTall = ar.bf16(8 * 512).rearrange("p (k n) -> p k n", k=8)
        OT_bufs = [Buf() for _ in range(8)]
        ytmp = ar.f32(8 * 512).rearrange("p (k n) -> p k n", k=8)
        y_bufs = [Buf() for _ in range(8)]
        ysq, ysq_bufs = sqb, sq_bufs
        st_ring = self.ps_ring([0, 1, 6, 7])
        aux_ring = self.ps_ring([0, 1])
        acc_ring = Ring([(self.psum[2], self.psum[3]), (self.psum[4], self.psum[5])])
        misc = self.ps_ring([6, 7, 2, 3, 4, 5])
        scale = {0: 64 ** -0.5, 1: 128 ** -0.5, 2: 96 ** -0.5, 3: 64 ** -0.5}[L]
        rc = self.din[f"rp_c_{L}"]
        rs_ = self.din[f"rp_s_{L}"]

        for ci in chunks:
            t0, n = chunk_range(ci)
            w = 1 if ci == 0 else 0
            vec = self.VEC[:, L, 1, w]
            tiles = tiles_for(ci)
            self.pump(2)
            sch.dma("pool", hp[:, :, 0:n], src[:, :, t0:t0 + n].rearrange("k p n -> p k n"), "d0", writes=[hb])
            sch.dma("pool", tabC[:, 0:n], rc[:, t0:t0 + n], "d1", writes=[tabb])
            sch.dma("pool", tabS[:, 0:n], rs_[:, t0:t0 + n], "d1", writes=[tabb])
            kvs.prefetch(ahead=2)
            rsd, rsdb = rstd_ring.get()
            self.norm_mod(hp, hb, n, vec, sqb, sq_bufs, aux_ring.get(), rsd, rsdb, tmp_ring, uT, uT_bufs, "lnexp")
            if L in (0, 1, 3):
                for b in range(8):
                    pa, pab = misc.get()
                    pb_, pbb = misc.get()
                    self.mm_group(pa[:, 0:n], [(wq[:, b, kc, :], uT[:, kc, 0:n]) for kc in range(8)], reads=[wbuf] + uT_bufs, writes=[pab])
                    self.mm_group(pb_[:, 0:n], [(wq[:, 8 + b, kc, :], uT[:, kc, 0:n]) for kc in range(8)], reads=[wbuf] + uT_bufs, writes=[pbb])
                    if L == 1:
                        sqk, sqkb = sqk_ring.get()
                        self.act(sqk[:, 0:n], pa[:, 0:n], AF.Square, [pab], [sqkb])
                        rk, rkb = rstd_ring.get()
                        self.rstd_from_sq([sqk[:, 0:n]], [sqkb], n, 128.0, aux_ring.get(), rk[:, 0:n], rkb, "lnexp")
                        self.rope(pa, pab, pb_, pbb, 128, n, tabC, tabS, tabb, [(0, 128, QT[:, b, 0:n], QT_bufs[b])],
                                  (tmp_ring.get(), tmp_ring.get()), (g1[:, 0:1], g1[:, 1:2], wbuf), (rk, rkb))
                    else:
                        self.rope(pa, pab, pb_, pbb, 128, n, tabC, tabS, tabb,
                                  [(0, 64, QT[0:64, 2 * b, 0:n], QT_bufs[2 * b]), (64, 128, QT[64:128, 2 * b + 1, 0:n], QT_bufs[2 * b + 1])],
                                  (tmp_ring.get(), tmp_ring.get()))
            else:
                pcs = []
                sqs = []
                for c in range(2):
                    pc, pcb = misc.get()
                    self.mm_group(pc[:, 0:n], [(win[:, c, kc, :], uT[:, kc, 0:n]) for kc in range(8)], reads=[wbuf] + uT_bufs, writes=[pcb])
                    sqk, sqkb = sqk_ring.get()
                    self.act(sqk[:, 0:n], pc[:, 0:n], AF.Square, [pcb], [sqkb])
                    pcs.append((pc, pcb))
                    sqs.append((sqk, sqkb))
                rk, rkb = rstd_ring.get()
                self.rstd_from_sq([q[0][:, 0:n] for q in sqs], [q[1] for q in sqs], n, 256.0, aux_ring.get(), rk[:, 0:n], rkb, "lnexp")
                for c in range(2):
                    self.stt("dve", cqn[:, c, 0:n], pcs[c][0][:, 0:n], g2[:, c:c + 1], rk[:, 0:n], ALU.mult, ALU.mult, [pcs[c][1], rkb, wbuf], [cqn_b])
                for h in range(16):
                    pa, pab = misc.get()
                    pb_, pbb = misc.get()
                    self.mm_group(pa[0:96, 0:n], [(wuq[:, h, c, 0:96], cqn[:, c, 0:n]) for c in range(2)], reads=[wbuf, cqn_b], writes=[pab])
                    self.mm_group(pb_[0:96, 0:n], [(wuq[:, 16 + h, c, 0:96], cqn[:, c, 0:n]) for c in range(2)], reads=[wbuf, cqn_b], writes=[pbb])
                    self.rope(pa, pab, pb_, pbb, 96, n, tabC, tabS, tabb, [(0, 96, QT[0:96, h, 0:n], QT_bufs[h])], (tmp_ring.get(), tmp_ring.get()))
            jobs = []
            n_ = n

            def add_map(slot, Kap, Qap, qb, acc, post, ufirst, ulast, Vap=None, fold=False):
                item, kvb = slot
                for ti, tl in enumerate(tiles):
                    jobs.append({"slot": slot, "K": Kap, "kb": kvb, "Q": Qap, "qb": qb, "V": item["V"] if Vap is None else Vap, "fold": fold, "tile": tl, "acc": acc,
                                 "first": ti == 0, "last": ti == len(tiles) - 1,
                                 "unit_first": ufirst and ti == 0, "unit_last": ulast and ti == len(tiles) - 1,
                                 "post": post if ti == len(tiles) - 1 else None})

            def simple_post(acc, r0, r1, ko, sink_h):
                def f():
                    (ot, otb), (dn, dnb) = acc
                    r_, rb_ = tmp_ring.get()
                    if sink_h is not None:
                        self.ts("dve", r_[r0:r1, 0:n_], dn[r0:r1, 0:n_], esink[r0:r1, sink_h:sink_h + 1], None, ALU.add, None, [dnb, esb], [rb_])
                        sch.op("dve", lambda e, o=r_[r0:r1, 0:n_]: e.reciprocal(out=o, in_=o), reads=[rb_], writes=[rb_])
                    else:
                        sch.op("dve", lambda e, o=r_[r0:r1, 0:n_], i_=dn[r0:r1, 0:n_]: e.reciprocal(out=o, in_=i_), reads=[dnb], writes=[rb_])
                    self.tt("dve", OTall[r0:r1, ko, 0:n_], ot[r0:r1, 0:n_], r_[r0:r1, 0:n_], ALU.mult, [otb, rb_], [OT_bufs[ko]])
                return f

            def fold_post(acc, r0, r1, ko):
                def f():
                    (ot, otb), (dn, dnb) = acc
                    cp, cpb = tmp_ring.get()
                    sch.op("dve", lambda e, o=cp[:, 0:n_], i_=ot[:, 0:n_]: e.tensor_copy(out=o, in_=i_), reads=[otb], writes=[cpb])
                    sch.op("pe", lambda e, o=dn[:, 0:n_], r=cp[:, 0:n_]: e.matmul(o, swm, r, start=True, stop=True), reads=[cpb, wbuf], writes=[dnb])
                    r_, rb_ = tmp_ring.get()
                    sch.op("dve", lambda e, o=r_[r0:r1, 0:n_], i_=dn[r0:r1, 0:n_]: e.reciprocal(out=o, in_=i_), reads=[dnb], writes=[rb_])
                    self.tt("dve", OTall[r0:r1, ko, 0:n_], cp[r0:r1, 0:n_], r_[r0:r1, 0:n_], ALU.mult, [cpb, rb_], [OT_bufs[ko]])
                return f

            def da_post(acc, h, which, store):
                def f():
                    (ot, otb), (dn, dnb) = acc
                    r_, rb_ = tmp_ring.get()
                    sch.op("dve", lambda e, o=r_[:, 0:n_], i_=dn[:, 0:n_]: e.reciprocal(out=o, in_=i_), reads=[dnb], writes=[rb_])
                    self.tt("dve", r_[:, 0:n_], ot[:, 0:n_], r_[:, 0:n_], ALU.mult, [otb], [rb_])
                    store[which] = (r_, rb_)
                    if which == 1:
                        (t0_, t0b), (t1_, t1b) = store[0], store[1]
                        self.stt("dve", t0_[:, 0:n_], t1_[:, 0:n_], neglam, t0_[:, 0:n_], ALU.mult, ALU.add, [t1b, lamb], [t0b])
                        sqk, sqkb = sqk_ring.get()
                        self.act(sqk[:, 0:n_], t0_[:, 0:n_], AF.Square, [t0b], [sqkb])
                        rk, rkb = rstd_ring.get()
                        self.rstd_from_sq([sqk[:, 0:n_]], [sqkb], n_, 128.0, st_ring.get(), rk[:, 0:n_], rkb, "lnexp")
                        self.stt("dve", OTall[:, h, 0:n_], t0_[:, 0:n_], subg[:, 0:1], rk[:, 0:n_], ALU.mult, ALU.mult, [t0b, rkb, lamb], [OT_bufs[h]])
                return f

            for u in range(nunit):
                slot = kvs.peek(u)
                item = slot[0]
                if L == 0:
                    h = u
                    store = {}
                    acc0 = acc_ring.get()
                    acc1 = acc_ring.get()
                    add_map(slot, item["K"], QT[:, 2 * h, :], QT_bufs[2 * h], acc0, da_post(acc0, h, 0, store), True, False)
                    add_map(slot, item["K"], QT[:, 2 * h + 1, :], QT_bufs[2 * h + 1], acc1, da_post(acc1, h, 1, store), False, True)
                else:
                    if L == 1:
                        maps = [(item["K"], QT[:, 4 * u + x, :], QT_bufs[4 * u + x], 0, 128, 4 * u + x, None) for x in range(4)]
                    elif L == 2:
                        maps = [(item["KA"][0:96, :], QT[0:96, 2 * u, :], QT_bufs[2 * u], 0, 64, u, None),
                                (item["KB"][0:96, :], QT[0:96, 2 * u + 1, :], QT_bufs[2 * u + 1], 64, 128, u, None)]
                    else:
                        maps = []
                        for x in range(8):
                            hh = 8 * u + x
                            maps.append((item["K"], QT[:, hh, :], QT_bufs[hh], (hh % 2) * 64, (hh % 2) * 64 + 64, hh // 2, hh))
                    for mi, (Kap, Qap, qb, r0, r1, ko, sink_h) in enumerate(maps):
                        acc = acc_ring.get()
                        if L == 2:
                            Vap = item["V"][:, :, 0:128] if mi == 0 else item["V"][:, :, 64:192]
                            add_map(slot, Kap, Qap, qb, acc, fold_post(acc, r0, r1, ko), mi == 0, mi == len(maps) - 1, Vap, True)
                        else:
                            add_map(slot, Kap, Qap, qb, acc, simple_post(acc, r0, r1, ko, sink_h), mi == 0, mi == len(maps) - 1)
            self.run_jobs(jobs, scale, pt_ring, st_ring, kvs, mask)
            for m in range(8):
                yp, yb = misc.get()
                self.mm_group(yp[:, 0:n], [(wo[:, m, ko, :], OTall[:, ko, 0:n]) for ko in range(8)], reads=[wbuf] + OT_bufs, writes=[yb])
                self.act(ytmp[:, m, 0:n], yp[:, 0:n], AF.Copy, [yb], [y_bufs[m]])
                self.tt("dve", ysq[:, m, 0:n], yp[:, 0:n], ytmp[:, m, 0:n], ALU.mult, [yb, y_bufs[m]], [ysq_bufs[m]])
            rsd, rsdb = rstd_ring.get()
            self.residual(hp, hb, n, vec, ytmp, ysq, y_bufs, ysq_bufs, aux_ring.get(), rsd, rsdb, tmp_ring, "lnexp")
            sch.dma("pool", dst[:, :, t0:t0 + n].rearrange("k p n -> p k n"), hp[:, :, 0:n], "d5", reads=[hb])
        sch.barrier()

    def mixer_phase(self, i, src, dst):
        self.k_phase(i, src)
        self.q_phase(i, src, dst)


def run(inp, n_sub=12, cores=NCORES):
    shared = prep_shared(inp)
    prog = Prog(n_sub, {k: v.shape for k, v in shared.items()})
    nc = prog.build()
    in_maps = []
    for b in range(cores):
        m = dict(shared)
        m.update(prep_core(inp, b))
        in_maps.append(m)
    res = run_bass_kernel_spmd(nc, in_maps, core_ids=list(range(cores)))
    return [r["outT"] for r in res.results]


def kernel(**inputs):
    outs = run(inputs, 12, NCORES)
    full = np.stack([np.ascontiguousarray(o.reshape(D, T)[:, CTX:].T) for o in outs], axis=0)
    return full.astype(np.float32)
```

```python
import math
import numpy as np
from contextlib import ExitStack
import concourse.bass as bass
import concourse.mybir as mybir
from concourse.bass_utils import run_bass_kernel_spmd

F32 = mybir.dt.float32
BF16 = mybir.dt.bfloat16
AF = mybir.ActivationFunctionType
ALU = mybir.AluOpType

D = 1024
KC = 8
CTX = 256
S = 4096
T = CTX + S
DFF = 2816
NJ = 22
EPS = 1e-6
NCH = 9
NCORES = 8


def chunk_range(ci):
    if ci == 0:
        return 0, CTX
    return CTX + 512 * (ci - 1), 512


class Buf:
    __slots__ = ("w", "r", "excl")

    def __init__(self, excl=False):
        self.w = None
        self.r = {}
        self.excl = excl


class Sched:
    ENGS = ("pe", "act", "dve", "pool", "sp")

    def __init__(self, nc, st):
        self.nc = nc
        self.st = st
        self.prog = {e: [] for e in self.ENGS}
        self.sems = {}
        self.cnt = {}
        self.seen = {e: {} for e in self.ENGS}
        for e in ("pe", "act", "dve", "pool"):
            self.new_sem(e)

    def new_sem(self, key):
        if key in self.sems:
            return key
        self.sems[key] = self.st.enter_context(self.nc.semaphore(key))
        self.cnt[key] = 0
        return key

    def dsem(self, idx):
        return self.new_sem(f"d{idx}")

    def _waits(self, e, reads, writes):
        need = {}
        for b in reads:
            if b.w is not None:
                k, v = b.w
                if v > need.get(k, 0):
                    need[k] = v
            if b.excl:
                for k, v in b.r.items():
                    if k != e and v > need.get(k, 0):
                        need[k] = v
        for b in writes:
            if b.w is not None:
                k, v = b.w
                if v > need.get(k, 0):
                    need[k] = v
            for k, v in b.r.items():
                if v > need.get(k, 0):
                    need[k] = v
        out = []
        seen = self.seen[e]
        for k, v in need.items():
            if k == e and e == "pe":
                continue
            if seen.get(k, 0) >= v:
                continue
            seen[k] = v
            out.append((self.sems[k], v))
        return out

    def op(self, e, fn, reads=(), writes=(), signal=True):
        waits = self._waits(e, reads, writes)
        sem = self.sems[e]
        if signal:
            self.cnt[e] += 1
            tv = self.cnt[e]
        else:
            tv = self.cnt[e] + 1
        self.prog[e].append((waits, fn, sem if signal else None, 1))
        for b in reads:
            if tv > b.r.get(e, 0):
                b.r[e] = tv
        for b in writes:
            b.w = (e, tv)
            b.r = {}

    def dma(self, q, out, in_, semkey, reads=(), writes=()):
        if not semkey.startswith("cv"):
            semkey = self.new_sem(("g" if q == "pool" else "h") + semkey)
        waits = self._waits(q, reads, writes)
        self.cnt[semkey] += 16
        tv = self.cnt[semkey]
        self.prog[q].append((waits, lambda e, out=out, in_=in_: e.dma_start(out=out, in_=in_), self.sems[semkey], 16))
        for b in reads:
            if tv > b.r.get(semkey, 0):
                b.r[semkey] = tv
        for b in writes:
            b.w = (semkey, tv)
            b.r = {}

    def barrier(self):
        for e in self.ENGS:
            waits = []
            for k, h in self.sems.items():
                v = self.cnt[k]
                if v == 0 or self.seen[e].get(k, 0) >= v or k.startswith("cv"):
                    continue
                if k == e and e == "pe":
                    continue
                self.seen[e][k] = v
                waits.append((h, v))
            if waits:
                self.prog[e].append((waits, None, None, 0))

    def emit(self):
        nc = self.nc
        with nc.Block() as block:
            def run(eng, lst):
                for waits, fn, sem, inc in lst:
                    for h, v in waits:
                        eng.wait_ge(h, v)
                    if fn is not None:
                        ins = fn(eng)
                        if sem is not None:
                            ins.then_inc(sem, inc)

            @block.tensor
            def _(e):
                run(e, self.prog["pe"])

            @block.scalar
            def _(e):
                run(e, self.prog["act"])

            @block.vector
            def _(e):
                run(e, self.prog["dve"])

            @block.gpsimd
            def _(e):
                run(e, self.prog["pool"])

            @block.sync
            def _(e):
                run(e, self.prog["sp"])


class Arena:
    def __init__(self, ap_f32, nwords):
        self.ap = ap_f32
        self.n = nwords
        self.off = 0
        self.base = 0

    def set_base(self):
        self.base = self.off

    def reset(self):
        self.off = self.base

    def f32(self, n):
        a = self.off
        self.off += n
        assert self.off <= self.n, ("arena overflow", self.off, self.n)
        return self.ap[:, a:a + n]

    def bf16(self, n):
        w = (n + 1) // 2
        return self.f32(w).bitcast(BF16)[:, 0:n]


class Ring:
    def __init__(self, items):
        self.items = items
        self.i = 0

    def get(self):
        it = self.items[self.i % len(self.items)]
        self.i += 1
        return it


class Stream:
    def __init__(self, sch, q, slots, semkeys, extra_reads=()):
        self.sch = sch
        self.extra_reads = list(extra_reads)
        self.q = q
        self.slots = slots
        self.sems = semkeys
        self.fills = []
        self.issued = 0
        self.taken = 0

    def add(self, fn):
        self.fills.append(fn)

    def _issue(self):
        i = self.issued
        ap, buf = self.slots[i % len(self.slots)]
        for (o, in_) in self.fills[i](ap):
            self.sch.dma(self.q, o, in_, self.sems[i % len(self.slots)], reads=self.extra_reads, writes=[buf])
        self.issued += 1

    def get(self, ahead=None):
        i = self.taken
        if ahead is None:
            ahead = len(self.slots)
        while self.issued < min(len(self.fills), i + ahead):
            self._issue()
        self.taken += 1
        return self.slots[i % len(self.slots)]

    def peek(self, off=0):
        return self.slots[(self.taken + off) % len(self.slots)]

    def prefetch(self, ahead=None):
        if ahead is None:
            ahead = len(self.slots)
        while self.issued < min(len(self.fills), self.taken + ahead):
            self._issue()


def fmv(v):
    v = np.asarray(v, np.float32)
    return np.ascontiguousarray(v.reshape(-1, 128).T)


def lhs_blocks(W, col_lists, kc):
    W = np.asarray(W, np.float32)
    out = np.zeros((128, len(col_lists), kc, 128), np.float32)
    Wr = W.reshape(kc, 128, W.shape[1])
    for b, cols in enumerate(col_lists):
        out[:, b, :, :len(cols)] = Wr[:, :, cols].transpose(1, 0, 2)
    return out.reshape(128, -1)


def rhs_fmt(W, cols, kc):
    W = np.asarray(W, np.float32)
    Wr = W.reshape(kc, 128, W.shape[1])[:, :, cols]
    return np.ascontiguousarray(Wr.transpose(1, 0, 2)).reshape(128, -1)


def swap_pairs(cols):
    c = np.asarray(cols).reshape(-1, 2)[:, ::-1].reshape(-1)
    return c


def rope_tables(rot_dim, row_dims):
    n_axis = rot_dim // 4
    pos = np.arange(S)
    row = (pos // 64).astype(np.float32)
    col = (pos % 64).astype(np.float32)
    freqs = (np.float32(10000.0) ** (-np.arange(n_axis, dtype=np.float32) / np.float32(n_axis))).astype(np.float32)
    ang = np.concatenate([row[:, None] * freqs, col[:, None] * freqs], axis=-1).astype(np.float32)
    cos = np.cos(ang).astype(np.float32)
    sin = np.sin(ang).astype(np.float32)
    C = np.ones((128, T), np.float32)
    Sg = np.zeros((128, T), np.float32)
    for p in range(128):
        d = row_dims[p]
        if d < 0:
            continue
        j = d // 2
        C[p, CTX:] = cos[:, j]
        Sg[p, CTX:] = -sin[:, j] if d % 2 == 0 else sin[:, j]
    return C, Sg


LAMBDA_INIT0 = 0.8 - 0.6 * math.exp(-0.3 * 0)


def prep_shared(inp):
    sh = {}
    sh["ada_w"] = np.ascontiguousarray(inp["ada_w"], np.float32)
    sh["ada_b"] = np.ascontiguousarray(inp["ada_b"], np.float32)
    sh["sm_i2"] = np.eye(2, dtype=np.float32)
    sw = np.zeros((128, 128), np.float32)
    sw[(np.arange(128) + 64) % 128, np.arange(128)] = 1.0
    sh["sm_sw"] = sw
    g = np.asarray(inp["norm_g"], np.float32).reshape(4, 6, 8, 128)
    sh["gT"] = np.ascontiguousarray(g.transpose(3, 0, 1, 2)).reshape(128, 4 * 6 * 8)
    wi = np.asarray(inp["ffn_w_in"], np.float32)
    wo = np.asarray(inp["ffn_w_out"], np.float32)
    for i in range(4):
        for s in range(2):
            W = wi[i, s]
            Wg = W[:, :DFF].reshape(8, 128, NJ, 128)
            Wu = W[:, DFF:].reshape(8, 128, NJ, 128)
            Wc = np.concatenate([Wg, Wu], axis=-1)
            sh[f"win_{i}_{s}"] = np.ascontiguousarray(Wc.transpose(1, 2, 0, 3)).reshape(128, NJ * 8 * 256)
            Wo = wo[i, s].reshape(NJ, 128, 8, 128)
            sh[f"wout_{i}_{s}"] = np.ascontiguousarray(Wo.transpose(1, 2, 0, 3)).reshape(128, 8 * NJ * 128)
    prep_mixers(inp, sh)
    return sh


def wo_fmt(W):
    W = np.asarray(W, np.float32).reshape(8, 128, 8, 128)
    return np.ascontiguousarray(W.transpose(1, 2, 0, 3)).reshape(128, -1)


def prep_mixers(inp, sh):
    ar = np.arange
    Wi = inp["da_w_in"][0]
    qc = [h * 128 + ar(128) for h in range(8)]
    kc_ = [1024 + h * 128 + ar(128) for h in range(8)]
    sh["mx_qk_0"] = lhs_blocks(Wi, qc + [swap_pairs(c) for c in qc] + kc_ + [swap_pairs(c) for c in kc_], 8)
    sh["mx_v_0"] = rhs_fmt(Wi, 2048 + ar(1024), 8)
    sh["mx_o_0"] = wo_fmt(inp["da_w_out"][0])
    sh["sm_lam_0"] = np.ascontiguousarray(np.broadcast_to(np.asarray(inp["da_lambda"][0], np.float32).reshape(1, 256), (128, 256)))
    sh["sm_subln_0"] = np.asarray(inp["da_subln"][0], np.float32).reshape(128, 1).copy()
    Wi = inp["ga_w_in"][0]
    qc = [h * 128 + ar(128) for h in range(8)]
    kc_ = [1024 + g * 128 + ar(128) for g in range(2)]
    sh["mx_qk_1"] = lhs_blocks(Wi, qc + [swap_pairs(c) for c in qc] + kc_ + [swap_pairs(c) for c in kc_], 8)
    sh["mx_v_1"] = rhs_fmt(Wi, 1280 + ar(256), 8)
    sh["mx_o_1"] = wo_fmt(inp["ga_w_out"][0])
    gq = np.asarray(inp["ga_q_norm"][0], np.float32)
    gk = np.asarray(inp["ga_k_norm"][0], np.float32)
    sp = swap_pairs(ar(128))
    sh["sm_g_1"] = np.ascontiguousarray(np.stack([gq, gq[sp], gk, gk[sp]], axis=1))
    Wi = inp["mla_w_in"][0]
    pad = np.full(64, 384)
    sh["mx_in_2"] = lhs_blocks(Wi, [ar(128), 128 + ar(128), 256 + ar(128), np.concatenate([pad, 384 + ar(32)]), np.concatenate([pad, 384 + swap_pairs(ar(32))])], 8)
    Wq = inp["mla_w_uq"][0]
    qc = [h * 96 + ar(96) for h in range(16)]
    qs = [h * 96 + np.concatenate([ar(64), 64 + swap_pairs(ar(32))]) for h in range(16)]
    sh["mx_uq_2"] = lhs_blocks(Wq, qc + qs, 2)
    Wkv = inp["mla_w_ukv"][0]
    sh["mx_kn_2"] = lhs_blocks(Wkv, [np.concatenate([(2 * jb) * 128 + ar(64), (2 * jb + 1) * 128 + ar(64)]) for jb in range(8)], 1)
    sh["mx_uv_2"] = rhs_fmt(Wkv, np.concatenate([h * 128 + 64 + ar(64) for h in range(16)]), 1)
    sh["mx_o_2"] = wo_fmt(inp["mla_w_out"][0])
    gq = np.asarray(inp["mla_q_norm"][0], np.float32)
    gkv = np.asarray(inp["mla_kv_norm"][0], np.float32)
    sh["sm_g_2"] = np.ascontiguousarray(np.stack([gq[0:128], gq[128:256], gkv], axis=1))
    Wi = inp["swa_w_in"][0]
    qc = [jb * 128 + ar(128) for jb in range(8)]
    kc_ = [1024 + ar(128)]
    sh["mx_qk_3"] = lhs_blocks(Wi, qc + [swap_pairs(c) for c in qc] + kc_ + [swap_pairs(c) for c in kc_], 8)
    sh["mx_v_3"] = rhs_fmt(Wi, 1152 + ar(128), 8)
    sh["mx_o_3"] = wo_fmt(inp["swa_w_out"][0])
    sh["sm_sink_3"] = np.ascontiguousarray(np.broadcast_to(np.asarray(inp["swa_sink"][0], np.float32).reshape(1, 16), (128, 16)))
    kk = ar(128)[:, None]
    qq = ar(384)[None, :]
    sh["sm_mask_3"] = ((qq >= kk) & (qq <= kk + 256)).astype(np.float32)
    p = ar(128)
    rd2 = np.full(128, -1)
    rd2[64:96] = ar(32)
    for L, (rot, rd) in enumerate([(64, p % 64), (128, p), (32, rd2), (64, p % 64)]):
        C, Sg = rope_tables(rot, rd)
        sh[f"rp_c_{L}"] = C
        sh[f"rp_s_{L}"] = Sg


def prep_core(inp, b):
    pc = {}
    h0 = np.concatenate([np.asarray(inp["ctx"][b], np.float32), np.asarray(inp["x"][b], np.float32)], axis=0)
    pc["xT"] = np.ascontiguousarray(h0.T).reshape(8, 128, T)
    pc["cT"] = np.ascontiguousarray(np.stack([fmv(inp["c"][b]), fmv(inp["c_ctx"])], axis=-1)).reshape(128, 16)
    return pc


class Prog:
    def __init__(self, n_sub, shared_shapes):
        self.n_sub = n_sub
        nc = self.nc = bass.Bass("TRN2", target_bir_lowering=False)
        self.st = ExitStack()
        st = self.st
        self.din = {}
        for name, shp in shared_shapes.items():
            self.din[name] = nc.dram_tensor(name, list(shp), F32, kind="ExternalInput").ap()
        self.xT = nc.dram_tensor("xT", [8, 128, T], F32, kind="ExternalInput").ap()
        self.cT = nc.dram_tensor("cT", [128, 16], F32, kind="ExternalInput").ap()
        self.outT = nc.dram_tensor("outT", [8, 128, T], F32, kind="ExternalOutput").ap()
        self.wb = {}
        for name, shp in shared_shapes.items():
            if name.startswith("win_") or name.startswith("wout_") or name.startswith("mx_"):
                self.wb[name] = nc.dram_tensor(name + "_b", list(shp), BF16, kind="Internal").ap()
        self.scr = {}
        for name, shp in (("KT0", [1024, T]), ("V0", [8, 128, 34 * 128]), ("KT1", [256, T]), ("V1", [2, 128, 34 * 128]),
                          ("KN2", [1024, T]), ("KR2", [32, T]), ("V2", [8, 128, 34 * 128]),
                          ("KT3", [128, T]), ("V3", [2, 128, 34 * 64])):
            self.scr[name] = nc.dram_tensor("scr_" + name, shp, BF16, kind="Internal").ap()
        self.sch = Sched(nc, st)
        NW = 50 * 1024
        self.arena_t = st.enter_context(nc.sbuf_tensor("arena", [128, NW], F32))
        self.ar = Arena(self.arena_t[:, :], NW)
        self.psum = []
        for i in range(8):
            t = st.enter_context(nc.psum_tensor(f"ps{i}", [128, 512], F32))
            self.psum.append((t, Buf(excl=True)))

    def ps_ring(self, idxs):
        return Ring([self.psum[i] for i in idxs])

    def mm_group(self, out_ap, pairs, reads, writes):
        n = len(pairs)
        for i, (l, r) in enumerate(pairs):
            last = i == n - 1
            self.sch.op("pe", lambda e, l=l, r=r, i=i, last=last: e.matmul(out_ap, l, r, start=(i == 0), stop=last),
                        reads=reads if i == 0 else (), writes=writes if i == 0 else (), signal=last)

    def act(self, out, in_, func, reads, writes, bias=None, scale=None):
        kw = {}
        if bias is not None:
            kw["bias"] = bias
        if scale is not None:
            kw["scale"] = scale
        self.sch.op("act", lambda e: e.activation(out=out, in_=in_, func=func, **kw), reads=reads, writes=writes)

    def tt(self, eng, out, in0, in1, op, reads, writes):
        self.sch.op(eng, lambda e: e.tensor_tensor(out=out, in0=in0, in1=in1, op=op), reads=reads, writes=writes)

    def ts(self, eng, out, in0, s1, s2, op0, op1, reads, writes):
        if op1 is None:
            self.sch.op(eng, lambda e: e.tensor_scalar(out=out, in0=in0, scalar1=s1, scalar2=None, op0=op0), reads=reads, writes=writes)
        else:
            self.sch.op(eng, lambda e: e.tensor_scalar(out=out, in0=in0, scalar1=s1, scalar2=s2, op0=op0, op1=op1), reads=reads, writes=writes)

    def stt(self, eng, out, in0, scalar, in1, op0, op1, reads, writes):
        self.sch.op(eng, lambda e: e.scalar_tensor_tensor(out=out, in0=in0, scalar=scalar, in1=in1, op0=op0, op1=op1), reads=reads, writes=writes)

    def cv_setup(self):
        order = []
        for i in range(4):
            order.append((f"F{i}0", [f"win_{i}_0", f"wout_{i}_0"]))
            order.append((f"M{i}", [n for n in self.wb if n.startswith("mx_") and n.endswith(f"_{i}")]))
            order.append((f"F{i}1", [f"win_{i}_1", f"wout_{i}_1"]))
        self.cv_groups = {}
        self.cv_list = []
        for gi, (g, names) in enumerate(order):
            sem = self.sch.new_sem(f"cv{gi}")
            grp = {"sem": sem, "buf": Buf(), "n": 0, "issued": 0}
            self.cv_groups[g] = grp
            for name in names:
                src, dst = self.din[name], self.wb[name]
                n = src.shape[1]
                step = 8192
                for a in range(0, n, step):
                    b_ = min(n, a + step)
                    self.cv_list.append((grp, dst[:, a:b_], src[:, a:b_]))
                    grp["n"] += 1
        self.cv_pos = 0

    def pump(self, k):
        while k > 0 and self.cv_pos < len(self.cv_list):
            grp, dst, src = self.cv_list[self.cv_pos]
            self.sch.dma("pool", dst, src, grp["sem"])
            grp["issued"] += 1
            if grp["issued"] == grp["n"]:
                grp["buf"].w = (grp["sem"], self.sch.cnt[grp["sem"]])
            self.cv_pos += 1
            k -= 1

    def need(self, g):
        grp = self.cv_groups[g]
        while grp["issued"] < grp["n"]:
            self.pump(1)
        return grp["buf"]

    def phase0(self):
        sch, ar, nc = self.sch, self.ar, self.nc
        self.ones = ar.bf16(128)
        self.ones_b = Buf()
        self.epsc = ar.f32(1)
        self.consts_b = Buf()
        self.VEC = ar.f32(4 * 3 * 2 * 3 * 8).rearrange("p (i s w v k) -> p i s w v k", i=4, s=3, w=2, v=3)
        self.vec_b = Buf()
        ar.set_base()
        sch.op("pool", lambda e: e.memset(self.ones, 1.0), writes=[self.ones_b])
        sch.op("pool", lambda e: e.memset(self.epsc, EPS), writes=[self.consts_b])
        self.cv_setup()
        self.need("F00")
        self.need("M0")
        cT = ar.f32(16)
        sc = ar.f32(16)
        scb = Buf()
        gT = ar.f32(4 * 6 * 8).rearrange("p (i r k) -> p i r k", i=4, r=6)
        mT = ar.f32(4 * 2 * 72).rearrange("p (i w n) -> p i w n", i=4, w=2)
        mTb = Buf()
        ldb = Buf()
        brow = ar.f32(9216)
        browb = Buf()
        mrow = ar.f32(9216)
        mrowb = Buf()
        ident = ar.f32(2)
        identb = Buf()
        sch.dsem(15)
        sch.dma("sp", cT, self.cT, "d15", writes=[ldb])
        sch.dma("sp", gT, self.din["gT"].rearrange("p (i r k) -> p i r k", i=4, r=6), "d15", writes=[ldb])
        self.act(sc, cT, AF.Silu, [ldb], [scb])
        sc3 = sc.rearrange("p (k w) -> p k w", w=2)
        sch.dsem(13)
        sch.dma("sp", ident[0:2, 0:2], self.din["sm_i2"], "d13", writes=[identb])
        nslot = 4
        slots = []
        semk = []
        for i in range(nslot):
            slots.append((ar.f32(8 * 512).rearrange("p (k n) -> p k n", k=8), Buf()))
            semk.append(sch.dsem(i))
        strm = Stream(sch, "sp", slots, semk)
        aw = self.din["ada_w"]
        for i in range(4):
            v = aw[i].rearrange("(kc p) n -> p kc n", p=128)
            for pc in range(18):
                strm.add(lambda ap, v=v, pc=pc: [(ap, v[:, :, pc * 512:(pc + 1) * 512])])
        pr = self.ps_ring([0, 1, 2, 3])
        pt_ring = self.ps_ring([4, 5])
        sch.dsem(14)
        for i in range(4):
            for w in range(2):
                sch.dma("sp", brow[w:w + 1, :], self.din["ada_b"][i:i + 1, :], "d14", writes=[browb])
            for pc in range(18):
                wap, wbuf = strm.get()
                pt, pb = pr.get()
                self.mm_group(pt[0:2, 0:512], [(sc3[:, kc, :], wap[:, kc, :]) for kc in range(8)], reads=[wbuf, scb], writes=[pb])
                self.tt("dve", mrow[0:2, pc * 512:(pc + 1) * 512], pt[0:2, 0:512], brow[0:2, pc * 512:(pc + 1) * 512], ALU.add, [pb, browb], [mrowb])
            tp, tb = pt_ring.get()
            ps3 = tp[:, 0:144].rearrange("p (n w) -> p n w", w=2)
            for n in range(72):
                self.mm_group(ps3[:, n, :], [(mrow[0:2, n * 128:(n + 1) * 128], ident[0:2, 0:2])], reads=[mrowb, identb], writes=[tb])
            for w in range(2):
                sch.op("dve", lambda e, o=mT[:, i, w, :], a=ps3[:, :, w]: e.tensor_copy(out=o, in_=a), reads=[tb], writes=[mTb])
        for i in range(4):
            for s in range(3):
                k0 = 3 * s
                wgt = 0.5 if s != 1 else 1.0
                for w in range(2):
                    m_sh = mT[:, i, w, (k0) * 8:(k0 + 1) * 8]
                    m_sc = mT[:, i, w, (k0 + 1) * 8:(k0 + 2) * 8]
                    m_gt = mT[:, i, w, (k0 + 2) * 8:(k0 + 3) * 8]
                    self.stt("dve", self.VEC[:, i, s, w, 0, :], m_sc, 1.0, gT[:, i, 2 * s, :], ALU.add, ALU.mult, [mTb, ldb], [self.vec_b])
                    self.sch.op("dve", lambda e, o=self.VEC[:, i, s, w, 1, :], a=m_sh: e.tensor_copy(out=o, in_=a), reads=[mTb], writes=[self.vec_b])
                    self.stt("dve", self.VEC[:, i, s, w, 2, :], m_gt, wgt, gT[:, i, 2 * s + 1, :], ALU.mult, ALU.mult, [mTb, ldb], [self.vec_b])
        sch.barrier()
        ar.reset()

    def rstd_from_sq(self, sq_aps, sq_bufs, n, dim, ps_item, out_ap, out_buf, mode, tmp_ap=None, tmp_buf=None):
        pt, pb = ps_item
        self.mm_group(pt[:, 0:n], [(self.ones, a) for a in sq_aps], reads=list(sq_bufs) + [self.ones_b], writes=[pb])
        if mode == "sqrt":
            self.act(out_ap, pt[:, 0:n], AF.Sqrt, [pb, self.consts_b], [out_buf], bias=self.epsc[:, 0:1], scale=1.0 / dim)
            self.sch.op("dve", lambda e: e.reciprocal(out=out_ap, in_=out_ap), reads=[out_buf], writes=[out_buf])
        else:
            self.act(out_ap, pt[:, 0:n], AF.Ln, [pb, self.consts_b], [out_buf], bias=self.epsc[:, 0:1], scale=1.0 / dim)
            self.act(out_ap, out_ap, AF.Exp, [out_buf], [out_buf], scale=-0.5)

    def norm_mod(self, hin, hb, n, vec, sqb, sq_bufs, ps_item, rstd, rstd_b, tmp_ring, uT, uT_bufs, mode):
        for k in range(8):
            self.act(sqb[:, k, 0:n], hin[:, k, 0:n], AF.Square, [hb], [sq_bufs[k]])
        self.rstd_from_sq([sqb[:, k, 0:n] for k in range(8)], sq_bufs, n, float(D), ps_item, rstd[:, 0:n], rstd_b, mode)
        for k in range(8):
            tp, tb = tmp_ring.get()
            self.tt("dve", tp[:, 0:n], hin[:, k, 0:n], rstd[:, 0:n], ALU.mult, [hb, rstd_b], [tb])
            self.act(uT[:, k, 0:n], tp[:, 0:n], AF.Identity, [tb, self.vec_b], [uT_bufs[k]], bias=vec[:, 1, k:k + 1], scale=vec[:, 0, k:k + 1])

    def residual(self, hin, hb, n, vec, ytmp, ysq, y_bufs, ysq_bufs, ps_item, rstd, rstd_b, tmp_ring, mode):
        self.rstd_from_sq([ysq[:, m, 0:n] for m in range(8)], ysq_bufs, n, float(D), ps_item, rstd[:, 0:n], rstd_b, mode)
        for k in range(8):
            tp, tb = tmp_ring.get()
            self.stt("dve", tp[:, 0:n], ytmp[:, k, 0:n], vec[:, 2, k:k + 1], rstd[:, 0:n], ALU.mult, ALU.mult, [y_bufs[k], rstd_b, self.vec_b], [tb])
            self.tt("pool", hin[:, k, 0:n], hin[:, k, 0:n], tp[:, 0:n], ALU.add, [tb], [hb])

    def ffn_phase(self, i, s, src, dst, chunks):
        sch, ar = self.sch, self.ar
        ar.reset()
        svec = 0 if s == 0 else 2
        hslots = [(ar.f32(8 * 512).rearrange("p (k n) -> p k n", k=8), Buf()) for _ in range(2)]
        ld_sems = [sch.dsem(x) for x in range(2)]
        st_sems = [sch.dsem(2 + x) for x in range(2)]
        sqb = ar.bf16(8 * 512).rearrange("p (k n) -> p k n", k=8)
        sq_bufs = [Buf() for _ in range(8)]
        uTs = [(ar.bf16(8 * 512).rearrange("p (k n) -> p k n", k=8), [Buf() for _ in range(8)]) for _ in range(2)]
        aTs = [(ar.bf16(NJ * 512).rearrange("p (j n) -> p j n", j=NJ), [Buf() for _ in range(NJ)]) for _ in range(2)]
        tmp_ring = Ring([(ar.f32(512), Buf()) for _ in range(3)])
        sg_ring = Ring([(ar.f32(512), Buf()) for _ in range(3)])
        rstd_ring = Ring([(ar.f32(512), Buf()) for _ in range(2)])
        ytmp = ar.f32(8 * 512).rearrange("p (k n) -> p k n", k=8)
        y_bufs = [Buf() for _ in range(8)]
        ysq = ar.bf16(8 * 512).rearrange("p (k n) -> p k n", k=8)
        ysq_bufs = [Buf() for _ in range(8)]
        win_slots = [(ar.bf16(2 * 2048).rearrange("p (jj kc c) -> p jj kc c", jj=2, kc=8), Buf()) for _ in range(3)]
        win_sems = [sch.dsem(4 + x) for x in range(3)]
        wout_slots = [(ar.bf16(2 * NJ * 128).rearrange("p (mm j c) -> p mm j c", mm=2, j=NJ), Buf()) for _ in range(2)]
        wout_sems = [sch.dsem(7 + x) for x in range(2)]
        win_d = self.wb[f"win_{i}_{s}"].rearrange("p (j kc c) -> p j kc c", j=NJ, kc=8)
        wout_d = self.wb[f"wout_{i}_{s}"].rearrange("p (m j c) -> p m j c", m=8, j=NJ)
        cvb = self.need(f"F{i}{s}")
        wstream = Stream(sch, "sp", win_slots, win_sems, [cvb])
        ostream = Stream(sch, "sp", wout_slots, wout_sems, [cvb])
        for ci in chunks:
            for jg in range(NJ // 2):
                wstream.add(lambda ap, jg=jg: [(ap, win_d[:, 2 * jg:2 * jg + 2, :, :])])
            for mg in range(4):
                ostream.add(lambda ap, mg=mg: [(ap, wout_d[:, 2 * mg:2 * mg + 2, :, :])])
        ps_misc = self.ps_ring([0])
        ps_gu = self.ps_ring([1, 2, 3, 4])
        ps_y = self.ps_ring([5, 6])

        def load(idx):
            ci = chunks[idx]
            t0, n = chunk_range(ci)
            hp, hb = hslots[idx % 2]
            sch.dma("pool", hp[:, :, 0:n], src[:, :, t0:t0 + n].rearrange("k p n -> p k n"), ld_sems[idx % 2], writes=[hb])

        load(0)
        for idx, ci in enumerate(chunks):
            t0, n = chunk_range(ci)
            w = 1 if ci == 0 else 0
            vec = self.VEC[:, i, svec, w]
            hp, hb = hslots[idx % 2]
            if idx + 1 < len(chunks):
                load(idx + 1)
            uT, uT_bufs = uTs[idx % 2]
            aT, aT_bufs = aTs[idx % 2]
            ostream.prefetch()
            self.pump(2)
            rs, rsb = rstd_ring.get()
            self.norm_mod(hp, hb, n, vec, sqb, sq_bufs, ps_misc.get(), rs, rsb, tmp_ring, uT, uT_bufs, "sqrt")
            for jg in range(NJ // 2):
                wap, wbuf = wstream.get()
                for jj in range(2):
                    j = 2 * jg + jj
                    gp, gb = ps_gu.get()
                    up, ub = ps_gu.get()
                    self.mm_group(gp[:, 0:n], [(wap[:, jj, kc, 0:128], uT[:, kc, 0:n]) for kc in range(8)], reads=[wbuf] + uT_bufs, writes=[gb])
                    self.mm_group(up[:, 0:n], [(wap[:, jj, kc, 128:256], uT[:, kc, 0:n]) for kc in range(8)], reads=[wbuf] + uT_bufs, writes=[ub])
                    sgp, sgb = sg_ring.get()
                    self.act(sgp[:, 0:n], gp[:, 0:n], AF.Silu, [gb], [sgb])
                    self.tt("dve", aT[:, j, 0:n], up[:, 0:n], sgp[:, 0:n], ALU.mult, [ub, sgb], [aT_bufs[j]])
            for mg in range(4):
                oap, obuf = ostream.get()
                for mm in range(2):
                    m = 2 * mg + mm
                    yp, yb = ps_y.get()
                    self.mm_group(yp[:, 0:n], [(oap[:, mm, j, :], aT[:, j, 0:n]) for j in range(NJ)], reads=[obuf] + aT_bufs, writes=[yb])
                    self.act(ytmp[:, m, 0:n], yp[:, 0:n], AF.Copy, [yb], [y_bufs[m]])
                    self.tt("dve", ysq[:, m, 0:n], yp[:, 0:n], ytmp[:, m, 0:n], ALU.mult, [yb, y_bufs[m]], [ysq_bufs[m]])
            rs, rsb = rstd_ring.get()
            self.residual(hp, hb, n, vec, ytmp, ysq, y_bufs, ysq_bufs, ps_misc.get(), rs, rsb, tmp_ring, "sqrt")
            sch.dma("pool", dst[:, :, t0:t0 + n].rearrange("k p n -> p k n"), hp[:, :, 0:n], st_sems[idx % 2], reads=[hb])
        sch.barrier()

    def build(self):
        self.phase0()
        sub = 0
        for i in range(4):
            last = i == 3
            for s in range(3):
                if sub >= self.n_sub:
                    break
                src = self.xT if sub == 0 else self.outT
                if s == 0:
                    self.ffn_phase(i, 0, src, self.outT, list(range(NCH)))
                elif s == 2:
                    self.ffn_phase(i, 1, src, self.outT, list(range(1, NCH)) if last else list(range(NCH)))
                else:
                    self.mixer_phase(i, src, self.outT)
                sub += 1
        self.sch.emit()
        return self.nc


    def rope(self, pa, pab, pb_, pbb, M, n, tC, tS, tabb, dests, tmps, gains=None, rstd=None):
        (t1, t1b), (t2, t2b) = tmps
        if gains is None:
            self.tt("dve", t1[0:M, 0:n], pa[0:M, 0:n], tC[0:M, 0:n], ALU.mult, [pab, tabb], [t1b])
            self.tt("dve", t2[0:M, 0:n], pb_[0:M, 0:n], tS[0:M, 0:n], ALU.mult, [pbb, tabb], [t2b])
        else:
            ga, gb, gbuf = gains
            self.stt("dve", t1[0:M, 0:n], pa[0:M, 0:n], ga[0:M, :], tC[0:M, 0:n], ALU.mult, ALU.mult, [pab, tabb, gbuf], [t1b])
            self.stt("dve", t2[0:M, 0:n], pb_[0:M, 0:n], gb[0:M, :], tS[0:M, 0:n], ALU.mult, ALU.mult, [pbb, tabb, gbuf], [t2b])
        if rstd is None:
            for (r0, r1, dap, db) in dests:
                self.tt("pool", dap, t1[r0:r1, 0:n], t2[r0:r1, 0:n], ALU.add, [t1b, t2b], [db])
        else:
            rs, rsb = rstd
            self.tt("pool", t1[0:M, 0:n], t1[0:M, 0:n], t2[0:M, 0:n], ALU.add, [t2b], [t1b])
            for (r0, r1, dap, db) in dests:
                self.tt("dve", dap, t1[r0:r1, 0:n], rs[r0:r1, 0:n], ALU.mult, [t1b, rsb], [db])

    def load_const(self, q, dst, src, semkey, buf):
        self.sch.dma(q, dst, src, semkey, writes=[buf])

    def k_phase(self, L, src):
        sch, ar = self.sch, self.ar
        ar.reset()
        hp = ar.f32(8 * 512).rearrange("p (k n) -> p k n", k=8)
        hb = Buf()
        sqb = ar.bf16(8 * 512).rearrange("p (k n) -> p k n", k=8)
        sq_bufs = [Buf() for _ in range(8)]
        uT = ar.bf16(8 * 512).rearrange("p (k n) -> p k n", k=8)
        uT_bufs = [Buf() for _ in range(8)]
        rstd_ring = Ring([(ar.f32(512), Buf()) for _ in range(2)])
        tmp_ring = Ring([(ar.f32(512), Buf()) for _ in range(4)])
        tabC = ar.f32(512)
        tabS = ar.f32(512)
        tabb = Buf()
        kout_ring = Ring([(ar.bf16(512), Buf(), sch.dsem(3 + x)) for x in range(3)])
        VC = {0: 1024, 1: 256, 2: 1024, 3: 128}[L]
        dv = {0: 128, 1: 128, 2: 128, 3: 64}[L]
        nun = VC // dv
        vouts = [(ar.bf16(4 * VC).rearrange("p (t c) -> p t c", t=4), Buf(), sch.dsem(6 + x)) for x in range(2)]
        Vs = self.scr[f"V{L}"]
        wbuf = Buf()
        cvb = self.need(f"M{L}")
        sch.dsem(0); sch.dsem(1); sch.dsem(2)
        psr = self.ps_ring([0, 1, 2, 3, 4, 5, 6, 7])
        if L in (0, 1, 3):
            nkb = {0: 8, 1: 2, 3: 1}[L]
            nqb = 8
            wk = ar.bf16(2 * nkb * 1024).rearrange("p (b kc c) -> p b kc c", b=2 * nkb, kc=8)
            wsrc = self.wb[f"mx_qk_{L}"].rearrange("p (b kc c) -> p b kc c", kc=8, c=128)
            sch.dma("sp", wk, wsrc[:, 2 * nqb:2 * nqb + 2 * nkb, :, :], "d2", reads=[cvb], writes=[wbuf])
            wv = ar.bf16(8 * VC).rearrange("p (kc c) -> p kc c", kc=8)
            sch.dma("sp", wv, self.wb[f"mx_v_{L}"].rearrange("p (kc c) -> p kc c", kc=8), "d2", reads=[cvb], writes=[wbuf])
            KTs = self.scr[f"KT{L}"]
            if L == 1:
                g1 = ar.f32(4)
                sch.dma("sp", g1, self.din["sm_g_1"], "d2", reads=[cvb], writes=[wbuf])
                sqk_ring = Ring([(ar.bf16(512), Buf()) for _ in range(2)])
        else:
            win = ar.bf16(3 * 1024).rearrange("p (b kc c) -> p b kc c", b=3, kc=8)
            wsrc = self.wb["mx_in_2"].rearrange("p (b kc c) -> p b kc c", kc=8, c=128)
            sch.dma("sp", win, wsrc[:, 2:5, :, :], "d2", reads=[cvb], writes=[wbuf])
            wkn = ar.bf16(8 * 128).rearrange("p (b c) -> p b c", b=8)
            sch.dma("sp", wkn, self.wb["mx_kn_2"].rearrange("p (b c) -> p b c", b=8), "d2", reads=[cvb], writes=[wbuf])
            wuv = ar.bf16(1024)
            sch.dma("sp", wuv, self.wb["mx_uv_2"], "d2", reads=[cvb], writes=[wbuf])
            g2 = ar.f32(3)
            sch.dma("sp", g2, self.din["sm_g_2"], "d2", reads=[cvb], writes=[wbuf])
            sqk_ring = Ring([(ar.bf16(512), Buf()) for _ in range(2)])
            ckvn = ar.bf16(512)
            ckvn_b = Buf()
        rc = self.din[f"rp_c_{L}"]
        rs_ = self.din[f"rp_s_{L}"]
        for ci in range(NCH):
            t0, n = chunk_range(ci)
            w = 1 if ci == 0 else 0
            nt = n // 128
            tile0 = t0 // 128
            vec = self.VEC[:, L, 1, w]
            self.pump(2)
            sch.dma("pool", hp[:, :, 0:n], src[:, :, t0:t0 + n].rearrange("k p n -> p k n"), "d0", writes=[hb])
            sch.dma("pool", tabC[:, 0:n], rc[:, t0:t0 + n], "d1", writes=[tabb])
            sch.dma("pool", tabS[:, 0:n], rs_[:, t0:t0 + n], "d1", writes=[tabb])
            rsd, rsdb = rstd_ring.get()
            self.norm_mod(hp, hb, n, vec, sqb, sq_bufs, psr.get(), rsd, rsdb, tmp_ring, uT, uT_bufs, "lnexp")
            vout, vob, vsem = vouts[ci % 2]
            if L in (0, 1, 3):
                for b in range(nkb):
                    pa, pab = psr.get()
                    pb_, pbb = psr.get()
                    self.mm_group(pa[:, 0:n], [(wk[:, b, kc, :], uT[:, kc, 0:n]) for kc in range(8)], reads=[wbuf] + uT_bufs, writes=[pab])
                    self.mm_group(pb_[:, 0:n], [(wk[:, nkb + b, kc, :], uT[:, kc, 0:n]) for kc in range(8)], reads=[wbuf] + uT_bufs, writes=[pbb])
                    ko, kob, ksem = kout_ring.get()
                    gains = None
                    rstd = None
                    if L == 1:
                        sqk, sqkb = sqk_ring.get()
                        self.act(sqk[:, 0:n], pa[:, 0:n], AF.Square, [pab], [sqkb])
                        rk, rkb = rstd_ring.get()
                        self.rstd_from_sq([sqk[:, 0:n]], [sqkb], n, 128.0, psr.get(), rk[:, 0:n], rkb, "lnexp")
                        gains = (g1[:, 2:3], g1[:, 3:4], wbuf)
                        rstd = (rk, rkb)
                    self.rope(pa, pab, pb_, pbb, 128, n, tabC, tabS, tabb, [(0, 128, ko[:, 0:n], kob)], (tmp_ring.get(), tmp_ring.get()), gains, rstd)
                    sch.dma("pool", KTs[b * 128:(b + 1) * 128, t0:t0 + n], ko[:, 0:n], ksem, reads=[kob])
                for tt_ in range(nt):
                    for cg in range((VC + 511) // 512):
                        cw = min(512, VC - cg * 512)
                        pv, pvb = psr.get()
                        self.mm_group(pv[:, 0:cw], [(uT[:, kc, tt_ * 128:(tt_ + 1) * 128], wv[:, kc, cg * 512:cg * 512 + cw]) for kc in range(8)],
                                      reads=[wbuf] + uT_bufs, writes=[pvb])
                        self.act(vout[:, tt_, cg * 512:cg * 512 + cw], pv[:, 0:cw], AF.Copy, [pvb], [vob])
            else:
                pc, pcb = psr.get()
                self.mm_group(pc[:, 0:n], [(win[:, 0, kc, :], uT[:, kc, 0:n]) for kc in range(8)], reads=[wbuf] + uT_bufs, writes=[pcb])
                sqk, sqkb = sqk_ring.get()
                self.act(sqk[:, 0:n], pc[:, 0:n], AF.Square, [pcb], [sqkb])
                rk, rkb = rstd_ring.get()
                self.rstd_from_sq([sqk[:, 0:n]], [sqkb], n, 128.0, psr.get(), rk[:, 0:n], rkb, "lnexp")
                self.stt("dve", ckvn[:, 0:n], pc[:, 0:n], g2[:, 2:3], rk[:, 0:n], ALU.mult, ALU.mult, [pcb, rkb, wbuf], [ckvn_b])
                for jb in range(8):
                    pk, pkb = psr.get()
                    self.mm_group(pk[:, 0:n], [(wkn[:, jb, :], ckvn[:, 0:n])], reads=[wbuf, ckvn_b], writes=[pkb])
                    ko, kob, ksem = kout_ring.get()
                    self.act(ko[:, 0:n], pk[:, 0:n], AF.Copy, [pkb], [kob])
                    sch.dma("pool", self.scr["KN2"][jb * 128:(jb + 1) * 128, t0:t0 + n], ko[:, 0:n], ksem, reads=[kob])
                pa, pab = psr.get()
                pb_, pbb = psr.get()
                self.mm_group(pa[0:96, 0:n], [(win[:, 1, kc, 0:96], uT[:, kc, 0:n]) for kc in range(8)], reads=[wbuf] + uT_bufs, writes=[pab])
                self.mm_group(pb_[0:96, 0:n], [(win[:, 2, kc, 0:96], uT[:, kc, 0:n]) for kc in range(8)], reads=[wbuf] + uT_bufs, writes=[pbb])
                ko, kob, ksem = kout_ring.get()
                self.rope(pa, pab, pb_, pbb, 96, n, tabC, tabS, tabb, [(64, 96, ko[64:96, 0:n], kob)], (tmp_ring.get(), tmp_ring.get()))
                sch.dma("pool", self.scr["KR2"][:, t0:t0 + n], ko[64:96, 0:n], ksem, reads=[kob])
                for tt_ in range(nt):
                    for cg in range(2):
                        pv, pvb = psr.get()
                        self.mm_group(pv[:, 0:512], [(ckvn[:, tt_ * 128:(tt_ + 1) * 128], wuv[:, cg * 512:(cg + 1) * 512])], reads=[wbuf, ckvn_b], writes=[pvb])
                        self.act(vout[:, tt_, cg * 512:(cg + 1) * 512], pv[:, 0:512], AF.Copy, [pvb], [vob])
            for u in range(nun):
                vd = Vs[u].rearrange("p (t c) -> p t c", t=34)
                sch.dma("pool", vd[:, tile0:tile0 + nt, :], vout[:, 0:nt, u * dv:(u + 1) * dv], vsem, reads=[vob])
        sch.barrier()

    def run_jobs(self, jobs, scale, pt_ring, st_ring, kvs, mask, D=3):
        sch = self.sch
        nj = len(jobs)
        for idx in range(nj + D):
            if idx < nj:
                j = jobs[idx]
                if j.get("unit_first"):
                    slot = kvs.get(ahead=1)
                    assert slot is j["slot"]
                kt, lo, hi, moff = j["tile"]
                stp, stb = st_ring.get()
                sch.op("pe", lambda e, o=stp[:, lo:hi], l=j["K"][:, kt * 128:(kt + 1) * 128], r=j["Q"][:, lo:hi]: e.matmul(o, l, r, start=True, stop=True),
                       reads=[j["kb"], j["qb"]], writes=[stb])
                ptp, ptb = pt_ring.get()
                self.act(ptp[:, lo:hi], stp[:, lo:hi], AF.Exp, [stb], [ptb], scale=scale)
                if moff is not None:
                    mk, mkb = mask
                    self.tt("pool", ptp[:, lo:hi], ptp[:, lo:hi], mk[:, moff:moff + (hi - lo)], ALU.mult, [mkb], [ptb])
                j["pt"] = (ptp, ptb)
            k = idx - D
            if k >= 0:
                j = jobs[k]
                kt, lo, hi, moff = j["tile"]
                ptp, ptb = j["pt"]
                (otp, otb), (dnp, dnb) = j["acc"]
                first, last = j["first"], j["last"]
                fold = j.get("fold", False)
                sch.op("pe", lambda e, o=otp[:, lo:hi], l=j["V"][:, kt, :], r=ptp[:, lo:hi], first=first, last=last: e.matmul(o, l, r, start=first, stop=last),
                       reads=[ptb, j["kb"]], writes=[otb], signal=fold)
                if not fold:
                    sch.op("pe", lambda e, o=dnp[:, lo:hi], r=ptp[:, lo:hi], first=first, last=last: e.matmul(o, self.ones, r, start=first, stop=last),
                           reads=[ptb, self.ones_b], writes=[dnb])
                if j.get("post") is not None:
                    j["post"]()
                if j.get("unit_last"):
                    kvs.prefetch(ahead=1)

    def q_phase(self, L, src, dst):
        sch, ar = self.sch, self.ar
        ar.reset()
        chunks = list(range(1, NCH)) if L == 3 else list(range(NCH))
        hp = ar.f32(8 * 512).rearrange("p (k n) -> p k n", k=8)
        hb = Buf()
        sqb = ar.bf16(8 * 512).rearrange("p (k n) -> p k n", k=8)
        sq_bufs = [Buf() for _ in range(8)]
        uT = ar.bf16(8 * 512).rearrange("p (k n) -> p k n", k=8)
        uT_bufs = [Buf() for _ in range(8)]
        rstd_ring = Ring([(ar.f32(512), Buf()) for _ in range(2)])
        tmp_ring = Ring([(ar.f32(512), Buf()) for _ in range(5)])
        tabC = ar.f32(512)
        tabS = ar.f32(512)
        tabb = Buf()
        wbuf = Buf()
        cvb = self.need(f"M{L}")
        for x in range(8):
            sch.dsem(x)
        wo = ar.bf16(8 * 8 * 128).rearrange("p (m ko c) -> p m ko c", m=8, ko=8)
        sch.dma("sp", wo, self.wb[f"mx_o_{L}"].rearrange("p (m ko c) -> p m ko c", m=8, ko=8), "d2", reads=[cvb], writes=[wbuf])
        nQ = {0: 16, 1: 8, 2: 16, 3: 16}[L]
        QT = ar.bf16(nQ * 512).rearrange("p (q n) -> p q n", q=nQ)
        QT_bufs = [Buf() for _ in range(nQ)]
        if L in (0, 3):
            sch.op("pool", lambda e: e.memset(QT, 0.0), writes=QT_bufs)
        if L in (0, 1, 3):
            wq = ar.bf16(16 * 1024).rearrange("p (b kc c) -> p b kc c", b=16, kc=8)
            wsrc = self.wb[f"mx_qk_{L}"].rearrange("p (b kc c) -> p b kc c", kc=8, c=128)
            sch.dma("sp", wq, wsrc[:, 0:16, :, :], "d2", reads=[cvb], writes=[wbuf])
        else:
            win = ar.bf16(2 * 1024).rearrange("p (b kc c) -> p b kc c", b=2, kc=8)
            wsrc = self.wb["mx_in_2"].rearrange("p (b kc c) -> p b kc c", kc=8, c=128)
            sch.dma("sp", win, wsrc[:, 0:2, :, :], "d2", reads=[cvb], writes=[wbuf])
            wuq = ar.bf16(32 * 256).rearrange("p (b kc c) -> p b kc c", b=32, kc=2)
            sch.dma("sp", wuq, self.wb["mx_uq_2"].rearrange("p (b kc c) -> p b kc c", b=32, kc=2), "d2", reads=[cvb], writes=[wbuf])
            g2 = ar.f32(3)
            sch.dma("sp", g2, self.din["sm_g_2"], "d2", reads=[cvb], writes=[wbuf])
            cqn = ar.bf16(2 * 512).rearrange("p (c n) -> p c n", c=2)
            cqn_b = Buf()
            swm = ar.f32(128)
            sch.dma("sp", swm, self.din["sm_sw"], "d2", reads=[cvb], writes=[wbuf])
        sqk_ring = Ring([(ar.bf16(512), Buf()) for _ in range(2)])
        if L == 1:
            g1 = ar.f32(4)
            sch.dma("sp", g1, self.din["sm_g_1"], "d2", reads=[cvb], writes=[wbuf])
        mask = None
        if L == 3:
            mk = ar.bf16(384)
            mkb = Buf()
            sch.dma("pool", mk, self.din["sm_mask_3"], "d2", reads=[cvb], writes=[wbuf])
            mask = (mk, wbuf)
            esink = ar.f32(16)
            esb = Buf()
            sch.dma("sp", esink, self.din["sm_sink_3"], "d2", reads=[cvb], writes=[wbuf])
            self.act(esink, esink, AF.Exp, [wbuf], [esb])
        if L == 0:
            lam = ar.f32(256)
            lamb = Buf()
            sch.dma("sp", lam, self.din["sm_lam_0"], "d2", reads=[cvb], writes=[wbuf])
            subg = ar.f32(1)
            sch.dma("sp", subg, self.din["sm_subln_0"], "d2", reads=[cvb], writes=[wbuf])
            pp = ar.f32(128)
            sc2 = ar.f32(4)
            self.tt("dve", pp[:, 0:64], lam[:, 0:64], lam[:, 64:128], ALU.mult, [wbuf], [lamb])
            self.tt("dve", pp[:, 64:128], lam[:, 128:192], lam[:, 192:256], ALU.mult, [lamb], [lamb])
            sch.op("dve", lambda e: e.reduce_sum(out=sc2[:, 0:1], in_=pp[:, 0:64], axis=mybir.AxisListType.X), reads=[lamb], writes=[lamb])
            sch.op("dve", lambda e: e.reduce_sum(out=sc2[:, 1:2], in_=pp[:, 64:128], axis=mybir.AxisListType.X), reads=[lamb], writes=[lamb])
            self.act(sc2[:, 0:2], sc2[:, 0:2], AF.Exp, [lamb], [lamb])
            self.tt("dve", sc2[:, 2:3], sc2[:, 1:2], sc2[:, 0:1], ALU.subtract, [lamb], [lamb])
            self.ts("dve", sc2[:, 2:3], sc2[:, 2:3], -LAMBDA_INIT0, None, ALU.add, None, [lamb], [lamb])
            self.ts("dve", subg, subg, 1.0 - LAMBDA_INIT0, None, ALU.mult, None, [lamb], [lamb])
            neglam = sc2[:, 2:3]
        nslot = 2
        kv_slots = []
        for x in range(nslot):
            if L == 2:
                item = {"KA": ar.bf16(T), "KB": ar.bf16(T), "V": ar.bf16(34 * 192).rearrange("p (t c) -> p t c", t=34)}
            else:
                item = {"K": ar.bf16(T), "V": ar.bf16(34 * 128).rearrange("p (t c) -> p t c", t=34)}
            kv_slots.append((item, Buf()))
            if L == 2:
                sch.op("pool", lambda e, o=item["V"][:, :, 64:128]: e.memset(o, 1.0), writes=[kv_slots[-1][1]])
        kvs = Stream(sch, "sp", kv_slots, [sch.dsem(3), sch.dsem(4)])
        nunit = {0: 8, 1: 2, 2: 8, 3: 2}[L]

        def tiles_for(ci):
            t0, n = chunk_range(ci)
            if ci == 0:
                return [(0, 0, n, None), (1, 0, n, None)]
            if L != 3:
                return [(kt, 0, n, None) for kt in range(34)]
            cs = t0 - CTX
            out = [(0, 0, n, None), (1, 0, n, None)]
            for ktl in range(32):
                koff = ktl * 128 - cs
                lo = max(0, koff - 128)
                hi = min(512, koff + 256)
                if hi <= lo:
                    continue
                out.append((2 + ktl, lo, hi, lo - (koff - 128)))
            return out

        def add_fill(ci, u):
            tl = tiles_for(ci)
            kts = sorted(set(t[0] for t in tl))
            runs = []
            for kt in kts:
                if runs and runs[-1][1] == kt:
                    runs[-1][1] = kt + 1
                else:
                    runs.append([kt, kt + 1])

            def fn(item, runs=runs, u=u):
                out = []
                for a, b_ in runs:
                    ca, cb = a * 128, b_ * 128
                    if L in (0, 1):
                        out.append((item["K"][:, ca:cb], self.scr[f"KT{L}"][u * 128:(u + 1) * 128, ca:cb]))
                        vd = self.scr[f"V{L}"][u].rearrange("p (t c) -> p t c", t=34)
                        out.append((item["V"][:, a:b_, :], vd[:, a:b_, :]))
                    elif L == 2:
                        out.append((item["KA"][0:64, ca:cb], self.scr["KN2"][(2 * u) * 64:(2 * u) * 64 + 64, ca:cb]))
                        out.append((item["KA"][64:96, ca:cb], self.scr["KR2"][:, ca:cb]))
                        out.append((item["KB"][0:64, ca:cb], self.scr["KN2"][(2 * u + 1) * 64:(2 * u + 1) * 64 + 64, ca:cb]))
                        out.append((item["KB"][64:96, ca:cb], self.scr["KR2"][:, ca:cb]))
                        vd = self.scr["V2"][u].rearrange("p (t c) -> p t c", t=34)
                        out.append((item["V"][:, a:b_, 0:64], vd[:, a:b_, 0:64]))
                        out.append((item["V"][:, a:b_, 128:192], vd[:, a:b_, 64:128]))
                    else:
                        out.append((item["K"][0:64, ca:cb], self.scr["KT3"][u * 64:(u + 1) * 64, ca:cb]))
                        out.append((item["K"][64:128, ca:cb], self.scr["KT3"][u * 64:(u + 1) * 64, ca:cb]))
                        vd = self.scr["V3"][u].rearrange("p (t c) -> p t c", t=34)
                        out.append((item["V"][:, a:b_, 0:64], vd[:, a:b_, :]))
                        out.append((item["V"][:, a:b_, 64:128], vd[:, a:b_, :]))
                return out
            kvs.add(fn)

        for ci in chunks:
            for u in range(nunit):
                add_fill(ci, u)
        pt_ring = Ring([(ar.bf16(512), Buf()) for _ in range(6)])
        OTall = ar.bf16(8 * 512).rearrange("p (k n) -> p k n", k=8)
        OT_bufs = [Buf() for _ in range(8)]
        ytmp = ar.f32(8 * 512).rearrange("p (k n) -> p k n", k=8)
        y_bufs = [Buf() for _ in range(8)]
        ysq, ysq_bufs = sqb, sq_bufs
        st_ring = self.ps_ring([0, 1, 6, 7])
        aux_ring = self.ps_ring([0, 1])
        acc_ring = Ring([(self.psum[2], self.psum[3]), (self.psum[4], self.psum[5])])
        misc = self.ps_ring([6, 7, 2, 3, 4, 5])
        scale = {0: 64 ** -0.5, 1: 128 ** -0.5, 2: 96 ** -0.5, 3: 64 ** -0.5}[L]
        rc = self.din[f"rp_c_{L}"]
        rs_ = self.din[f"rp_s_{L}"]

        for ci in chunks:
            t0, n = chunk_range(ci)
            w = 1 if ci == 0 else 0
            vec = self.VEC[:, L, 1, w]
            tiles = tiles_for(ci)
            self.pump(2)
            sch.dma("pool", hp[:, :, 0:n], src[:, :, t0:t0 + n].rearrange("k p n -> p k n"), "d0", writes=[hb])
            sch.dma("pool", tabC[:, 0:n], rc[:, t0:t0 + n], "d1", writes=[tabb])
            sch.dma("pool", tabS[:, 0:n], rs_[:, t0:t0 + n], "d1", writes=[tabb])
            kvs.prefetch(ahead=2)
            rsd, rsdb = rstd_ring.get()
            self.norm_mod(hp, hb, n, vec, sqb, sq_bufs, aux_ring.get(), rsd, rsdb, tmp_ring, uT, uT_bufs, "lnexp")
            if L in (0, 1, 3):
                for b in range(8):
                    pa, pab = misc.get()
                    pb_, pbb = misc.get()
                    self.mm_group(pa[:, 0:n], [(wq[:, b, kc, :], uT[:, kc, 0:n]) for kc in range(8)], reads=[wbuf] + uT_bufs, writes=[pab])
                    self.mm_group(pb_[:, 0:n], [(wq[:, 8 + b, kc, :], uT[:, kc, 0:n]) for kc in range(8)], reads=[wbuf] + uT_bufs, writes=[pbb])
                    if L == 1:
                        sqk, sqkb = sqk_ring.get()
                        self.act(sqk[:, 0:n], pa[:, 0:n], AF.Square, [pab], [sqkb])
                        rk, rkb = rstd_ring.get()
                        self.rstd_from_sq([sqk[:, 0:n]], [sqkb], n, 128.0, aux_ring.get(), rk[:, 0:n], rkb, "lnexp")
                        self.rope(pa, pab, pb_, pbb, 128, n, tabC, tabS, tabb, [(0, 128, QT[:, b, 0:n], QT_bufs[b])],
                                  (tmp_ring.get(), tmp_ring.get()), (g1[:, 0:1], g1[:, 1:2], wbuf), (rk, rkb))
                    else:
                        self.rope(pa, pab, pb_, pbb, 128, n, tabC, tabS, tabb,
                                  [(0, 64, QT[0:64, 2 * b, 0:n], QT_bufs[2 * b]), (64, 128, QT[64:128, 2 * b + 1, 0:n], QT_bufs[2 * b + 1])],
                                  (tmp_ring.get(), tmp_ring.get()))
            else:
                pcs = []
                sqs = []
                for c in range(2):
                    pc, pcb = misc.get()
                    self.mm_group(pc[:, 0:n], [(win[:, c, kc, :], uT[:, kc, 0:n]) for kc in range(8)], reads=[wbuf] + uT_bufs, writes=[pcb])
                    sqk, sqkb = sqk_ring.get()
                    self.act(sqk[:, 0:n], pc[:, 0:n], AF.Square, [pcb], [sqkb])
                    pcs.append((pc, pcb))
                    sqs.append((sqk, sqkb))
                rk, rkb = rstd_ring.get()
                self.rstd_from_sq([q[0][:, 0:n] for q in sqs], [q[1] for q in sqs], n, 256.0, aux_ring.get(), rk[:, 0:n], rkb, "lnexp")
                for c in range(2):
                    self.stt("dve", cqn[:, c, 0:n], pcs[c][0][:, 0:n], g2[:, c:c + 1], rk[:, 0:n], ALU.mult, ALU.mult, [pcs[c][1], rkb, wbuf], [cqn_b])
                for h in range(16):
                    pa, pab = misc.get()
                    pb_, pbb = misc.get()
                    self.mm_group(pa[0:96, 0:n], [(wuq[:, h, c, 0:96], cqn[:, c, 0:n]) for c in range(2)], reads=[wbuf, cqn_b], writes=[pab])
                    self.mm_group(pb_[0:96, 0:n], [(wuq[:, 16 + h, c, 0:96], cqn[:, c, 0:n]) for c in range(2)], reads=[wbuf, cqn_b], writes=[pbb])
                    self.rope(pa, pab, pb_, pbb, 96, n, tabC, tabS, tabb, [(0, 96, QT[0:96, h, 0:n], QT_bufs[h])], (tmp_ring.get(), tmp_ring.get()))
            jobs = []
            n_ = n

            def add_map(slot, Kap, Qap, qb, acc, post, ufirst, ulast, Vap=None, fold=False):
                item, kvb = slot
                for ti, tl in enumerate(tiles):
                    jobs.append({"slot": slot, "K": Kap, "kb": kvb, "Q": Qap, "qb": qb, "V": item["V"] if Vap is None else Vap, "fold": fold, "tile": tl, "acc": acc,
                                 "first": ti == 0, "last": ti == len(tiles) - 1,
                                 "unit_first": ufirst and ti == 0, "unit_last": ulast and ti == len(tiles) - 1,
                                 "post": post if ti == len(tiles) - 1 else None})

            def simple_post(acc, r0, r1, ko, sink_h):
                def f():
                    (ot, otb), (dn, dnb) = acc
                    r_, rb_ = tmp_ring.get()
                    if sink_h is not None:
                        self.ts("dve", r_[r0:r1, 0:n_], dn[r0:r1, 0:n_], esink[r0:r1, sink_h:sink_h + 1], None, ALU.add, None, [dnb, esb], [rb_])
                        sch.op("dve", lambda e, o=r_[r0:r1, 0:n_]: e.reciprocal(out=o, in_=o), reads=[rb_], writes=[rb_])
                    else:
                        sch.op("dve", lambda e, o=r_[r0:r1, 0:n_], i_=dn[r0:r1, 0:n_]: e.reciprocal(out=o, in_=i_), reads=[dnb], writes=[rb_])
                    self.tt("dve", OTall[r0:r1, ko, 0:n_], ot[r0:r1, 0:n_], r_[r0:r1, 0:n_], ALU.mult, [otb, rb_], [OT_bufs[ko]])
                return f

            def fold_post(acc, r0, r1, ko):
                def f():
                    (ot, otb), (dn, dnb) = acc
                    cp, cpb = tmp_ring.get()
                    sch.op("dve", lambda e, o=cp[:, 0:n_], i_=ot[:, 0:n_]: e.tensor_copy(out=o, in_=i_), reads=[otb], writes=[cpb])
                    sch.op("pe", lambda e, o=dn[:, 0:n_], r=cp[:, 0:n_]: e.matmul(o, swm, r, start=True, stop=True), reads=[cpb, wbuf], writes=[dnb])
                    r_, rb_ = tmp_ring.get()
                    sch.op("dve", lambda e, o=r_[r0:r1, 0:n_], i_=dn[r0:r1, 0:n_]: e.reciprocal(out=o, in_=i_), reads=[dnb], writes=[rb_])
                    self.tt("dve", OTall[r0:r1, ko, 0:n_], cp[r0:r1, 0:n_], r_[r0:r1, 0:n_], ALU.mult, [cpb, rb_], [OT_bufs[ko]])
                return f

            def da_post(acc, h, which, store):
                def f():
                    (ot, otb), (dn, dnb) = acc
                    r_, rb_ = tmp_ring.get()
                    sch.op("dve", lambda e, o=r_[:, 0:n_], i_=dn[:, 0:n_]: e.reciprocal(out=o, in_=i_), reads=[dnb], writes=[rb_])
                    self.tt("dve", r_[:, 0:n_], ot[:, 0:n_], r_[:, 0:n_], ALU.mult, [otb], [rb_])
                    store[which] = (r_, rb_)
                    if which == 1:
                        (t0_, t0b), (t1_, t1b) = store[0], store[1]
                        self.stt("dve", t0_[:, 0:n_], t1_[:, 0:n_], neglam, t0_[:, 0:n_], ALU.mult, ALU.add, [t1b, lamb], [t0b])
                        sqk, sqkb = sqk_ring.get()
                        self.act(sqk[:, 0:n_], t0_[:, 0:n_], AF.Square, [t0b], [sqkb])
                        rk, rkb = rstd_ring.get()
                        self.rstd_from_sq([sqk[:, 0:n_]], [sqkb], n_, 128.0, st_ring.get(), rk[:, 0:n_], rkb, "lnexp")
                        self.stt("dve", OTall[:, h, 0:n_], t0_[:, 0:n_], subg[:, 0:1], rk[:, 0:n_], ALU.mult, ALU.mult, [t0b, rkb, lamb], [OT_bufs[h]])
                return f

            for u in range(nunit):
                slot = kvs.peek(u)
                item = slot[0]
                if L == 0:
                    h = u
                    store = {}
                    acc0 = acc_ring.get()
                    acc1 = acc_ring.get()
                    add_map(slot, item["K"], QT[:, 2 * h, :], QT_bufs[2 * h], acc0, da_post(acc0, h, 0, store), True, False)
                    add_map(slot, item["K"], QT[:, 2 * h + 1, :], QT_bufs[2 * h + 1], acc1, da_post(acc1, h, 1, store), False, True)
                else:
                    if L == 1:
                        maps = [(item["K"], QT[:, 4 * u + x, :], QT_bufs[4 * u + x], 0, 128, 4 * u + x, None) for x in range(4)]
                    elif L == 2:
                        maps = [(item["KA"][0:96, :], QT[0:96, 2 * u, :], QT_bufs[2 * u], 0, 64, u, None),
                                (item["KB"][0:96, :], QT[0:96, 2 * u + 1, :], QT_bufs[2 * u + 1], 64, 128, u, None)]
                    else:
                        maps = []
                        for x in range(8):
                            hh = 8 * u + x
                            maps.append((item["K"], QT[:, hh, :], QT_bufs[hh], (hh % 2) * 64, (hh % 2) * 64 + 64, hh // 2, hh))
                    for mi, (Kap, Qap, qb, r0, r1, ko, sink_h) in enumerate(maps):
                        acc = acc_ring.get()
                        if L == 2:
                            Vap = item["V"][:, :, 0:128] if mi == 0 else item["V"][:, :, 64:192]
                            add_map(slot, Kap, Qap, qb, acc, fold_post(acc, r0, r1, ko), mi == 0, mi == len(maps) - 1, Vap, True)
                        else:
                            add_map(slot, Kap, Qap, qb, acc, simple_post(acc, r0, r1, ko, sink_h), mi == 0, mi == len(maps) - 1)
            self.run_jobs(jobs, scale, pt_ring, st_ring, kvs, mask)
            for m in range(8):
                yp, yb = misc.get()
                self.mm_group(yp[:, 0:n], [(wo[:, m, ko, :], OTall[:, ko, 0:n]) for ko in range(8)], reads=[wbuf] + OT_bufs, writes=[yb])
                self.act(ytmp[:, m, 0:n], yp[:, 0:n], AF.Copy, [yb], [y_bufs[m]])
                self.tt("dve", ysq[:, m, 0:n], yp[:, 0:n], ytmp[:, m, 0:n], ALU.mult, [yb, y_bufs[m]], [ysq_bufs[m]])
            rsd, rsdb = rstd_ring.get()
            self.residual(hp, hb, n, vec, ytmp, ysq, y_bufs, ysq_bufs, aux_ring.get(), rsd, rsdb, tmp_ring, "lnexp")
            sch.dma("pool", dst[:, :, t0:t0 + n].rearrange("k p n -> p k n"), hp[:, :, 0:n], "d5", reads=[hb])
        sch.barrier()

    def mixer_phase(self, i, src, dst):
        self.k_phase(i, src)
        self.q_phase(i, src, dst)


def run(inp, n_sub=12, cores=NCORES):
    shared = prep_shared(inp)
    prog = Prog(n_sub, {k: v.shape for k, v in shared.items()})
    nc = prog.build()
    in_maps = []
    for b in range(cores):
        m = dict(shared)
        m.update(prep_core(inp, b))
        in_maps.append(m)
    res = run_bass_kernel_spmd(nc, in_maps, core_ids=list(range(cores)))
    return [r["outT"] for r in res.results]


def kernel(**inputs):
    outs = run(inputs, 12, NCORES)
    full = np.stack([np.ascontiguousarray(o.reshape(D, T)[:, CTX:].T) for o in outs], axis=0)
    return full.astype(np.float32)
```
